# Optimizing a Trainium2 kernel written in Bass

```python
import math
import jax, jax.numpy as jnp
from jax import lax
import numpy as np

D_MODEL = 1024
BATCH = 2
SEQ = 8192
DEPTH = 1

MIX_WIDTH = D_MODEL
HEAD_DIM = 64
ATTN_WIDTH = MIX_WIDTH // 2
CONV_DIM = MIX_WIDTH - ATTN_WIDTH
N_Q_HEADS = ATTN_WIDTH // HEAD_DIM
N_KV_HEADS = 2
GQA_GROUP = N_Q_HEADS // N_KV_HEADS
KV_WIDTH = N_KV_HEADS * HEAD_DIM
N_CONV_GROUPS = CONV_DIM // HEAD_DIM
WINDOW = 128
BLOCK = 128
CONV_WIDTH = 3
NUM_BUCKETS = 32
MAX_DISTANCE = 128
D_FF = 2816
EPS = 1e-6
IN_SPLITS = (ATTN_WIDTH, ATTN_WIDTH + KV_WIDTH, ATTN_WIDTH + 2 * KV_WIDTH,
             ATTN_WIDTH + 2 * KV_WIDTH + CONV_DIM,
             ATTN_WIDTH + 2 * KV_WIDTH + 2 * CONV_DIM)
IN_COLS = ATTN_WIDTH + 2 * KV_WIDTH + 3 * CONV_DIM

kernel_name = "hymba_swa_sink_shortconv_macaron_t5bias"


def rms_norm(x, g):
    x32 = x.astype(jnp.float32)
    inv = lax.rsqrt(jnp.mean(x32 * x32, axis=-1, keepdims=True) + EPS)
    return (x32 * inv * g.astype(jnp.float32)).astype(x.dtype)


def swiglu(h, w_gate, w_up, w_down):
    return (jax.nn.silu(h @ w_gate) * (h @ w_up)) @ w_down


def t5_causal_bucket(dist):
    n = jnp.maximum(dist, 0)
    max_exact = NUM_BUCKETS // 2
    large = max_exact + (jnp.log(jnp.maximum(n, 1).astype(jnp.float32) / max_exact)
                         / math.log(MAX_DISTANCE / max_exact)
                         * (NUM_BUCKETS - max_exact)).astype(jnp.int32)
    large = jnp.minimum(large, NUM_BUCKETS - 1)
    return jnp.where(n < max_exact, n, large)


def band(t):
    pad = [(0, 0)] * t.ndim
    pad[1] = (1, 0)
    prev = jnp.pad(t, pad)[:, :-1]
    return jnp.concatenate([prev, t], axis=2)


def sliding_window_sink_attention(q, k, v, sinks, rel_table):
    b, s, _ = q.shape
    nb = s // BLOCK
    qb = q.reshape(b, nb, BLOCK, N_KV_HEADS, GQA_GROUP, HEAD_DIM).astype(jnp.float32)
    kb = band(k.reshape(b, nb, BLOCK, N_KV_HEADS, HEAD_DIM)).astype(jnp.float32)
    vb = band(v.reshape(b, nb, BLOCK, N_KV_HEADS, HEAD_DIM)).astype(jnp.float32)
    scores = jnp.einsum('bnqhgd,bnjhd->bnhgqj', qb, kb) * (HEAD_DIM ** -0.5)
    qi = jnp.arange(BLOCK, dtype=jnp.int32)[:, None]
    kj = jnp.arange(2 * BLOCK, dtype=jnp.int32)[None, :]
    dist = qi + BLOCK - kj
    bias = rel_table.astype(jnp.float32)[t5_causal_bucket(dist)]
    bias = bias.transpose(2, 0, 1).reshape(N_KV_HEADS, GQA_GROUP, BLOCK, 2 * BLOCK)
    key_abs = (jnp.arange(nb, dtype=jnp.int32)[:, None] * BLOCK
               + jnp.arange(2 * BLOCK, dtype=jnp.int32)[None, :] - BLOCK)
    valid = ((dist >= 0) & (dist < WINDOW))[None, :, :] & (key_abs >= 0)[:, None, :]
    valid = valid[None, :, None, None, :, :]
    scores = jnp.where(valid, scores + bias, -jnp.inf)
    sink = sinks.astype(jnp.float32).reshape(N_KV_HEADS, GQA_GROUP)[:, :, None, None]
    m = jnp.maximum(jnp.max(scores, axis=-1, keepdims=True), sink)
    p = jnp.exp(scores - m)
    denom = jnp.sum(p, axis=-1, keepdims=True) + jnp.exp(sink - m)
    out = jnp.einsum('bnhgqj,bnjhd->bnqhgd', p / denom, vb)
    return out.reshape(b, s, ATTN_WIDTH).astype(q.dtype)


def causal_short_conv(u, w):
    return lax.conv_general_dilated(
        u, w[:, None, :].astype(u.dtype), window_strides=(1,),
        padding=[(CONV_WIDTH - 1, 0)], dimension_numbers=('NWC', 'WIO', 'NWC'),
        feature_group_count=u.shape[-1])


def setup_inputs(seed: int = 0) -> dict:
    key = jax.random.key(seed)
    ks = jax.random.split(key, 20)
    f32 = jnp.float32

    def nrm(k, shape, scale):
        return jax.random.normal(k, shape, f32) * scale

    def gain(k, shape):
        return 1.0 + 0.02 * jax.random.normal(k, shape, f32)

    L = DEPTH
    return {
        "x": nrm(ks[0], (BATCH, SEQ, D_MODEL), 1.0),
        "rel_bias_table": nrm(ks[1], (NUM_BUCKETS, N_Q_HEADS), 0.5),
        "ffn1_norm": gain(ks[2], (L, D_MODEL)),
        "ffn1_w_gate": nrm(ks[3], (L, D_MODEL, D_FF), D_MODEL ** -0.5),
        "ffn1_w_up": nrm(ks[4], (L, D_MODEL, D_FF), D_MODEL ** -0.5),
        "ffn1_w_down": nrm(ks[5], (L, D_FF, D_MODEL), D_FF ** -0.5),
        "mix_norm": gain(ks[6], (L, D_MODEL)),
        "w_in": nrm(ks[7], (L, D_MODEL, IN_COLS), D_MODEL ** -0.5),
        "conv_w": nrm(ks[8], (L, CONV_WIDTH, CONV_DIM), CONV_WIDTH ** -0.5),
        "attn_sinks": nrm(ks[9], (L, N_Q_HEADS), 0.5),
        "attn_out_norm": gain(ks[10], (L, ATTN_WIDTH)),
        "conv_out_norm": gain(ks[11], (L, CONV_DIM)),
        "w_out": nrm(ks[12], (L, MIX_WIDTH, D_MODEL), MIX_WIDTH ** -0.5),
        "ffn2_norm": gain(ks[13], (L, D_MODEL)),
        "ffn2_w_gate": nrm(ks[14], (L, D_MODEL, D_FF), D_MODEL ** -0.5),
        "ffn2_w_up": nrm(ks[15], (L, D_MODEL, D_FF), D_MODEL ** -0.5),
        "ffn2_w_down": nrm(ks[16], (L, D_FF, D_MODEL), D_FF ** -0.5),
        "final_norm": gain(ks[17], (D_MODEL,)),
    }


def reference(x, rel_bias_table, ffn1_norm, ffn1_w_gate, ffn1_w_up, ffn1_w_down,
              mix_norm, w_in, conv_w, attn_sinks, attn_out_norm, conv_out_norm,
              w_out, ffn2_norm, ffn2_w_gate, ffn2_w_up, ffn2_w_down, final_norm):
    for l in range(DEPTH):
        x = x + 0.5 * swiglu(rms_norm(x, ffn1_norm[l]), ffn1_w_gate[l], ffn1_w_up[l], ffn1_w_down[l])
        h = rms_norm(x, mix_norm[l])
        z = h @ w_in[l]
        q, k, v, u, gate_b, gate_c = jnp.split(z, IN_SPLITS, axis=-1)
        attn = sliding_window_sink_attention(q, k, v, attn_sinks[l], rel_bias_table)
        conv = gate_b * causal_short_conv(gate_c * u, conv_w[l])
        mixed = jnp.concatenate([rms_norm(attn, attn_out_norm[l]),
                                 rms_norm(conv, conv_out_norm[l])], axis=-1)
        x = x + mixed @ w_out[l]
        x = x + 0.5 * swiglu(rms_norm(x, ffn2_norm[l]), ffn2_w_gate[l], ffn2_w_up[l], ffn2_w_down[l])
    return rms_norm(x, final_norm)
```

```python
import math
import numpy as np
import concourse.bass as bass
import concourse.mybir as mybir
from concourse.bass_utils import run_bass_kernel_spmd
from contextlib import ExitStack

F32 = mybir.dt.float32
BF16 = mybir.dt.bfloat16
AF = mybir.ActivationFunctionType
ALU = mybir.AluOpType

D = 1024
DFF = 2816
NF = DFF // 128
SEQ = 8192
NCORE = 8
SEG = 2048
HALO = 128
NTOK = SEG + HALO
ST_LEN = (1152, 1024)
ST_G0 = (0, 1152)
GROUPS = [list(range(0, 7)), list(range(7, 14)), list(range(14, 22))]
NS = 12
EPS = 1e-6
G_FFN1, G_MIX, G_FFN2, G_FINAL, G_CONVN, G_CONVW = 0, 8, 16, 24, 32, 36
NGCOL = 48


def stream_order():
    order = []

    def ffn(tag):
        for grp in GROUPS:
            for f in grp:
                order.append((tag + "g", f))
                order.append((tag + "u", f))
            for f in grp:
                order.append((tag + "d", f))
    ffn("1")
    order.append(("k", 0))
    order.append(("v", 0))
    for c in range(4):
        order.append(("q", c))
    for i in range(4):
        order.append(("cu", i))
        order.append(("cc", i))
        order.append(("cb", i))
    for o in range(8):
        order.append(("o", o))
    ffn("2")
    return order


NCHUNK = len(stream_order())


class Res:
    __slots__ = ("w", "r")

    def __init__(self):
        self.w = None
        self.r = []


class Sched:
    ENG = ("pe", "act", "dve", "pool", "sp")

    def __init__(self, nc, es):
        self.nc = nc
        self.es = es
        self.engs = {}
        for name in self.ENG:
            sem = es.enter_context(nc.semaphore("s_" + name))
            self.engs[name] = dict(sem=sem, count=0, ops=[], waited={})
        self.res = {}

    def dma_sem(self, name):
        return dict(sem=self.es.enter_context(self.nc.semaphore(name)), count=0)

    def R(self, key):
        r = self.res.get(key)
        if r is None:
            r = self.res[key] = Res()
        return r

    def op(self, eng, fn, reads=(), writes=(), dma=None, extra=()):
        E = self.engs[eng]
        need = []
        for k in reads:
            r = self.R(k)
            if r.w is not None:
                need.append(r.w)
        for k in writes:
            r = self.R(k)
            if r.w is not None:
                need.append(r.w)
            need.extend(r.r)
        need.extend(extra)
        if dma is not None:
            dma["count"] += 16
            tok = (dma["sem"], dma["count"], None)
        else:
            E["count"] += 1
            tok = (E["sem"], E["count"], eng)
        mx = {}
        for (sem, val, src) in need:
            if src == "pe" and eng == "pe" and dma is None:
                continue
            k = id(sem)
            if k not in mx or mx[k][1] < val:
                mx[k] = (sem, val)
        waits = []
        for k, (sem, val) in mx.items():
            if E["waited"].get(k, 0) >= val:
                continue
            E["waited"][k] = val
            waits.append((sem, val))
        E["ops"].append((waits, fn, tok))
        for k in reads:
            self.R(k).r.append(tok)
        for k in writes:
            r = self.R(k)
            r.w = tok
            r.r = []
        return tok

    def emit(self, block, final_waits=()):
        def runner(name):
            def body(e):
                for (waits, fn, tok) in self.engs[name]["ops"]:
                    for (sem, val) in waits:
                        e.wait_ge(sem, val)
                    inst = fn(e)
                    inst.then_inc(tok[0], 16 if tok[2] is None else 1)
                if name == "sp":
                    for (sem, val, _) in final_waits:
                        e.wait_ge(sem, val)
            return body

        block.tensor(runner("pe"))
        block.scalar(runner("act"))
        block.vector(runner("dve"))
        block.gpsimd(runner("pool"))
        block.sync(runner("sp"))


def hkeys(t0, n, kcs=range(8)):
    return [("h", kc, b) for kc in kcs for b in range(t0 // 128, (t0 + n + 127) // 128)]


def blk_keys(kind, t0, n, *extra):
    return [(kind,) + tuple(extra) + (b,) for b in range(t0 // 128, (t0 + n + 127) // 128)]


def build_program():
    nc = bass.Bass("TRN2", target_bir_lowering=False)
    xT_d = nc.dram_tensor("xT", [128, 8, NTOK], F32, kind="ExternalInput").ap()
    ws_d = nc.dram_tensor("ws", [NCHUNK, 128, 1024], F32, kind="ExternalInput").ap()
    gains_d = nc.dram_tensor("gains", [128, NGCOL], F32, kind="ExternalInput").ap()
    gattn_d = nc.dram_tensor("gattn", [128, 512], F32, kind="ExternalInput").ap()
    sinks_d = nc.dram_tensor("sinks", [128, 8], F32, kind="ExternalInput").ap()
    halo_d = nc.dram_tensor("halo_ok", [128, 1], F32, kind="ExternalInput").ap()
    mask_d = nc.dram_tensor("maskT", [128, 2, 1024], F32, kind="ExternalInput").ap()
    bias_d = nc.dram_tensor("biasT", [128, 2, 1024], F32, kind="ExternalInput").ap()
    ident_d = nc.dram_tensor("ident", [128, 128], F32, kind="ExternalInput").ap()
    out_d = nc.dram_tensor("outT", [128, 8, SEG], F32, kind="ExternalOutput").ap()

    with ExitStack() as es:
        def sb(name, shape, dt):
            return es.enter_context(nc.sbuf_tensor(name, shape, dt))

        xT = sb("xT_sb", [128, 8, 1152], F32)
        hT = sb("hT_sb", [128, 8, 1152], BF16)
        aT = sb("aT_sb", [128, 8, 1152], BF16)
        ring = sb("ring_sb", [128, NS, 1024], BF16)
        qT = sb("qT_sb", [128, 4, 1152], BF16)
        kT = sb("kT_sb", [128, 18 * 128], BF16)
        vaug = sb("vaug_sb", [128, 18, 2, 66], BF16)
        cuT = sb("cuT_sb", [128, 4, 2 + 18 * 128], BF16)
        uS = sb("uS_sb", [128, 2, 512], F32)
        bS = sb("bS_sb", [128, 2, 512], F32)
        convT = sb("convT_sb", [128, 4, 1152], F32)
        pT = sb("pT_sb", [128, 2, 4, 512], BF16)
        atmp = sb("atmp_sb", [128, 2, 512], F32)
        anf = sb("anf_sb", [128, 2, 512], F32)
        anb = sb("anb_sb", [128, 2, 512], BF16)
        EB = sb("EB_sb", [128, 2, 1024], F32)
        sS = sb("sS_sb", [128, 2, 512], F32)
        identf = sb("identf_sb", [128, 128], F32)
        ones = sb("ones_sb", [128, 128], BF16)
        diag = sb("diag_sb", [128, 12, 128], BF16)
        gains = sb("gains_sb", [128, NGCOL], F32)
        gattn = sb("gattn_sb", [128, 512], F32)
        esink = sb("esink_sb", [128, 8, 1], F32)
        rden = sb("rden_sb", [128, 2, 8, 1], F32)
        halo = sb("halo_sb", [128, 1], F32)
        epst = sb("eps_sb", [128, 1], F32)
        ssq = sb("ssq_sb", [128, 2], F32)
        rstq = sb("rstq_sb", [128, 2], F32)
        psall = es.enter_context(nc.psum_tensor("psall", [128, 8, 512], F32))
        ps = [psall[:, i, :] for i in range(8)]
        BH = sb("BH_sb", [128, 2, 1024], BF16)
        BL = sb("BL_sb", [128, 2, 1024], BF16)
        identb = sb("identb_sb", [128, 128], BF16)

        S = Sched(nc, es)
        ringsem = [S.dma_sem("rg%d" % i) for i in range(NS)]
        xsem = [S.dma_sem("xl%d" % i) for i in range(4)]
        osem = [S.dma_sem("os%d" % i) for i in range(2)]
        last_out = {}
        setup_sems = {n: S.dma_sem("su_" + n) for n in ("gains", "gattn", "sinks", "halo", "mask", "mask2", "bias", "ident")}

        TILES_ALL = ([(0, 384), (384, 384), (768, 384)], [(0, 512), (512, 512)])
        TILES_OWN = ([(128, 512), (640, 512)], [(0, 512), (512, 512)])
        state = dict(xtok=[], bS=0, bM=0, bN=0, b6=0, issued=0, released=0, consumed=0, sS=0, uS=0, bS_=0, att=0)
        order = stream_order()
        total_chunks = 2 * NCHUNK

        def bankS():
            b = state["bS"]
            state["bS"] = (b + 1) % 4
            return b

        def bankM(wide=False):
            b = state["bM"] % 2
            state["bM"] = (b + 1) % 2
            return 4 + b

        def bank6():
            b = state["b6"]
            state["b6"] = (b + 1) % 6
            return b

        def bankN():
            b = state["bN"]
            state["bN"] ^= 1
            return 6 + b

        pend_dve = []
        pend_low = []

        def drain_dve(k=1):
            for _ in range(k):
                if pend_dve:
                    pend_dve.pop(0)[1]()
                elif pend_low:
                    pend_low.pop(0)()

        def flush_for(t0, n):
            lo, hi = t0, t0 + n
            while any((a < hi and lo < a + m_) for ((a, m_), _) in pend_dve):
                pend_dve.pop(0)[1]()

        def pump():
            while state["issued"] < total_chunks and state["issued"] < state["released"] + NS:
                n = state["issued"]
                slot = n % NS
                ci = n % NCHUNK
                S.op("pool", lambda e, slot=slot, ci=ci: e.dma_start(out=ring[:, slot, :], in_=ws_d[ci]),
                     writes=[("ring", slot)], dma=ringsem[slot], extra=state["xtok"])
                state["issued"] += 1

        def consume(expect):
            n = state["consumed"]
            assert order[n % NCHUNK] == expect, (order[n % NCHUNK], expect)
            state["consumed"] += 1
            return n % NS

        def release(k):
            state["released"] += k
            pump()

        def setup():
            S.op("act", lambda e: e.dma_start(out=gains[:], in_=gains_d), writes=["gains"], dma=setup_sems["gains"])
            S.op("act", lambda e: e.dma_start(out=identf[:], in_=ident_d), writes=["identf"], dma=setup_sems["ident"])
            S.op("act", lambda e: e.dma_start(out=gattn[:], in_=gattn_d), writes=["gattn"], dma=setup_sems["gattn"])
            S.op("act", lambda e: e.dma_start(out=esink[:, :, 0], in_=sinks_d), writes=["esink"], dma=setup_sems["sinks"])
            S.op("act", lambda e: e.dma_start(out=halo[:], in_=halo_d), writes=["halo"], dma=setup_sems["halo"])
            S.op("act", lambda e: e.dma_start(out=EB[:], in_=bias_d), writes=["EB"], dma=setup_sems["bias"])
            S.op("act", lambda e: e.dma_start(out=uS[:].rearrange("p a b -> p (a b)"), in_=mask_d[:, 0, :]),
                 writes=[("uS", 0), ("uS", 1)], dma=setup_sems["mask"])
            S.op("act", lambda e: e.dma_start(out=bS[:].rearrange("p a b -> p (a b)"), in_=mask_d[:, 1, :]),
                 writes=[("bS", 0), ("bS", 1)], dma=setup_sems["mask2"])
            S.op("pool", lambda e: e.memset(ones[:], 1.0), writes=["ones"])
            S.op("pool", lambda e: e.memset(epst[:], EPS), writes=["eps"])
            S.op("pool", lambda e: e.memset(vaug[:], 1.0), writes=["vaug_init"])
            S.op("pool", lambda e: e.memset(cuT[:, :, 0:2], 0.0), writes=["cu_pad"])

        def setup2():
            ops = []
            _Sop = S.op

            def defer(*a, **k):
                ops.append(lambda: _Sop(*a, **k))
            defer("act", lambda e: e.activation(out=esink[:], in_=esink[:], func=AF.Exp), reads=["esink"], writes=["esink"])
            defer("dve", lambda e: e.scalar_tensor_tensor(out=EB[:, 0, :], in0=EB[:, 0, :], scalar=8.0,
                                                         in1=uS[:].rearrange("p a b -> p (a b)"), op0=ALU.mult, op1=ALU.add),
                 reads=["EB", ("uS", 0), ("uS", 1)], writes=["EB"])
            defer("dve", lambda e: e.scalar_tensor_tensor(out=EB[:, 1, :], in0=EB[:, 1, :], scalar=8.0,
                                                         in1=bS[:].rearrange("p a b -> p (a b)"), op0=ALU.mult, op1=ALU.add),
                 reads=["EB", ("bS", 0), ("bS", 1)], writes=["EB"])
            defer("dve", lambda e: e.tensor_copy(out=BH[:], in_=EB[:]), reads=["EB"], writes=["BH"])
            defer("dve", lambda e: e.tensor_tensor(out=EB[:], in0=EB[:], in1=BH[:], op=ALU.subtract), reads=["EB", "BH"], writes=["EB"])
            defer("dve", lambda e: e.tensor_copy(out=BL[:], in_=EB[:]), reads=["EB"], writes=["BL"])
            defer("dve", lambda e: e.tensor_copy(out=identb[:], in_=identf[:]), reads=["identf"], writes=["identb"])

            def halo_cols(e):
                e.tensor_copy(out=vaug[:, 0, 0, 64:65], in_=halo[:, 0:1])
                return e.tensor_copy(out=vaug[:, 0, 1, 64:65], in_=halo[:, 0:1])
            defer("dve", halo_cols, reads=["halo", "vaug_init"], writes=["vaug_halo"])

            def mkdiag(e):
                inst = None
                for i in range(4):
                    for r in range(3):
                        col = G_CONVW + r * 4 + i
                        inst = e.tensor_scalar(out=diag[:, i * 3 + r, :], in0=identf[:], scalar1=gains[:, col:col + 1],
                                               scalar2=None, op0=ALU.mult)
                return inst
            defer("dve", mkdiag, reads=["identf", "gains"], writes=["diag"])
            pend_low.extend(ops)

        def load_x_first(tile):
            t0, n = tile
            tok = S.op("sp", lambda e: e.dma_start(out=xT[:, 0:4, t0:t0 + n], in_=xT_d[:, 0:4, t0:t0 + n]),
                       writes=blk_keys("x0a", t0, n), dma=xsem[0])
            S.op("act", lambda e: e.dma_start(out=xT[:, 4:8, t0:t0 + n], in_=xT_d[:, 4:8, t0:t0 + n]),
                 writes=blk_keys("x0b", t0, n), dma=xsem[3])
            state["xtok"] = [tok]

        def load_x_tile(st, i, tile, gate=False):
            g0 = ST_G0[st]
            t0, n = tile
            tok = S.op("sp", lambda e: e.dma_start(out=xT[:, :, t0:t0 + n], in_=xT_d[:, :, g0 + t0:g0 + t0 + n]),
                       writes=blk_keys("x", t0, n), dma=xsem[i])
            state["xtok"] = [tok] if gate else []

        def norm_A(tile):
            t0, n = tile
            S.op("act", lambda e: e.activation(out=hT[:, :, t0:t0 + n], in_=xT[:, :, t0:t0 + n], func=AF.Square),
                 reads=blk_keys("x", t0, n) + blk_keys("x0a", t0, n) + blk_keys("x0b", t0, n), writes=hkeys(t0, n))

        def norm_sq_row(tile, o):
            t0, n = tile
            S.op("act", lambda e: e.activation(out=hT[:, o, t0:t0 + n], in_=xT[:, o, t0:t0 + n], func=AF.Square),
                 reads=blk_keys("xr", t0, n, o), writes=hkeys(t0, n, [o]))

        def norm_B0(tile):
            t0, n = tile
            hk = hkeys(t0, n)
            b = bankN()

            def mm(e):
                for kc in range(8):
                    inst = e.matmul(ps[b][:, 0:n], lhsT=ones[:], rhs=hT[:, kc, t0:t0 + n], start=(kc == 0), stop=(kc == 7))
                return inst
            S.op("pe", mm, reads=hk + ["ones"], writes=[("ps", b)])
            S.op("act", lambda e: e.activation(out=ps[b][:, 0:n], in_=ps[b][:, 0:n], func=AF.Ln, scale=1.0 / D, bias=epst[:]),
                 reads=[("ps", b), "eps"], writes=[("ps", b)])
            S.op("act", lambda e: e.activation(out=ps[b][:, 0:n], in_=ps[b][:, 0:n], func=AF.Exp, scale=-0.5),
                 reads=[("ps", b)], writes=[("ps", b)])
            return b

        def norm_B(tile, gcol):
            t0, n = tile
            b = norm_B0(tile)

            def hmul(kc):
                S.op("dve", lambda e: e.scalar_tensor_tensor(out=hT[:, kc, t0:t0 + n], in0=xT[:, kc, t0:t0 + n],
                                                             scalar=gains[:, gcol + kc:gcol + kc + 1], in1=ps[b][:, 0:n],
                                                             op0=ALU.mult, op1=ALU.mult),
                     reads=blk_keys("x", t0, n) + [("ps", b), "gains"], writes=hkeys(t0, n, [kc]))
            for kc in range(8):
                pend_dve.append(((t0, n), lambda kc=kc: hmul(kc)))

        def norm_h_tile(tile, gcol):
            norm_A(tile)
            norm_B(tile, gcol)
            flush_for(*tile)

        finals = []
        tmp0 = convT[:].rearrange("p a b -> p (a b)")[:, 0:4096].rearrange("p (k n) -> p k n", k=8)
        tmp1a = EB[:].rearrange("p a b -> p (a b)").rearrange("p (k n) -> p k n", k=4)
        tmp1b = atmp[:]
        tmp1c = anf[:]
        sqs = qT[:].rearrange("p a b -> p (a b)")[:, 0:4096].rearrange("p (k n) -> p k n", k=8)

        def tmp_kc(ti, kc):
            if ti == 0:
                return tmp0[:, kc, :]
            if kc < 4:
                return tmp1a[:, kc, :]
            return tmp1b[:, kc - 4, :] if kc < 6 else tmp1c[:, kc - 6, :]

        def final_sq_row(tile, o):
            t0, n = tile
            S.op("act", lambda e: e.activation(out=sqs[:, o, 0:n], in_=xT[:, o, t0:t0 + n], func=AF.Square),
                 reads=blk_keys("xr", t0, n, o), writes=[("sqs", o)])

        def final_norm_B(st, tile, after=None, c0=0, bank=None):
            t0, n = tile
            if bank is None:
                while pend_dve:
                    pend_dve.pop(0)[1]()
                b = bankN()
            else:
                b = bank

            def mm(e):
                for kc in range(8):
                    inst = e.matmul(ps[b][:, 0:n], lhsT=ones[:], rhs=sqs[:, kc, c0:c0 + n], start=(kc == 0), stop=(kc == 7))
                return inst
            S.op("pe", mm, reads=[("sqs", k) for k in range(8)] + ["ones"], writes=[("ps", b)])
            S.op("act", lambda e: e.activation(out=ps[b][:, 0:n], in_=ps[b][:, 0:n], func=AF.Ln, scale=1.0 / D, bias=epst[:]),
                 reads=[("ps", b), "eps"], writes=[("ps", b)])
            S.op("act", lambda e: e.activation(out=ps[b][:, 0:n], in_=ps[b][:, 0:n], func=AF.Exp, scale=-0.5),
                 reads=[("ps", b)], writes=[("ps", b)])
            o0 = (t0 - HALO) if st == 0 else (1024 + t0)

            def fmul(kc):
                S.op("dve", lambda e: e.scalar_tensor_tensor(out=xT[:, kc, t0:t0 + n], in0=xT[:, kc, t0:t0 + n],
                                                             scalar=gains[:, G_FINAL + kc:G_FINAL + kc + 1], in1=ps[b][:, 0:n],
                                                             op0=ALU.mult, op1=ALU.mult),
                     reads=blk_keys("x", t0, n) + [("ps", b), "gains"], writes=blk_keys("xf", t0, n, kc))
                if kc == 3 or kc == 7:
                    lo = kc - 3
                    qi = 0 if kc == 3 else 1
                    rk = blk_keys("x", t0, n)
                    for k in range(lo, lo + 4):
                        rk += blk_keys("xf", t0, n, k)
                    t = S.op("sp", lambda e: e.dma_start(out=out_d[:, lo:lo + 4, o0:o0 + n], in_=xT[:, lo:lo + 4, t0:t0 + n]),
                             reads=rk, dma=osem[qi])
                    last_out[qi] = t
                    if kc == 7 and after is not None:
                        after()
            for kc in range(8):
                pend_low.append(lambda kc=kc: fmul(kc))

        mixer_end = []

        def pre_load(ti):
            g0 = ST_G0[1]
            t0, n = TILES_ALL[1][ti]
            if ti == 0:
                S.op("sp", lambda e: e.dma_start(out=tmp0, in_=xT_d[:, :, g0 + t0:g0 + t0 + n]),
                     writes=[("tmp", 0, 0), ("tmp", 0, 1), ("tmp", 0, 2)], dma=xsem[0], extra=mixer_end)
            else:
                S.op("sp", lambda e: e.dma_start(out=tmp1a, in_=xT_d[:, 0:4, g0 + t0:g0 + t0 + n]),
                     writes=[("tmp", 1, 0)], dma=xsem[1], extra=mixer_end)
                S.op("sp", lambda e: e.dma_start(out=tmp1b, in_=xT_d[:, 4:6, g0 + t0:g0 + t0 + n]),
                     writes=[("tmp", 1, 1)], dma=xsem[2], extra=mixer_end)
                S.op("sp", lambda e: e.dma_start(out=tmp1c, in_=xT_d[:, 6:8, g0 + t0:g0 + t0 + n]),
                     writes=[("tmp", 1, 2)], dma=xsem[3], extra=mixer_end)

        def pre_norm_A(ti):
            t0, n = TILES_ALL[1][ti]
            if ti == 0:
                S.op("act", lambda e: e.activation(out=hT[:, :, t0:t0 + n], in_=tmp0, func=AF.Square),
                     reads=[("tmp", 0, 0), ("tmp", 0, 1), ("tmp", 0, 2)], writes=hkeys(t0, n))
            else:
                S.op("act", lambda e: e.activation(out=hT[:, 0:4, t0:t0 + n], in_=tmp1a, func=AF.Square),
                     reads=[("tmp", 1, 0)], writes=hkeys(t0, n, range(4)))
                S.op("act", lambda e: e.activation(out=hT[:, 4:6, t0:t0 + n], in_=tmp1b, func=AF.Square),
                     reads=[("tmp", 1, 1)], writes=hkeys(t0, n, range(4, 6)))
                S.op("act", lambda e: e.activation(out=hT[:, 6:8, t0:t0 + n], in_=tmp1c, func=AF.Square),
                     reads=[("tmp", 1, 2)], writes=hkeys(t0, n, range(6, 8)))

        def pre_norm_B(ti):
            t0, n = TILES_ALL[1][ti]
            b = norm_B0((t0, n))

            def hmul(kc):
                S.op("dve", lambda e: e.scalar_tensor_tensor(out=hT[:, kc, t0:t0 + n], in0=tmp_kc(ti, kc),
                                                             scalar=gains[:, G_FFN1 + kc:G_FFN1 + kc + 1], in1=ps[b][:, 0:n],
                                                             op0=ALU.mult, op1=ALU.mult),
                     reads=[("tmp", ti, (0 if kc < 4 else (1 if kc < 6 else 2))), ("ps", b), "gains"], writes=hkeys(t0, n, [kc]))
            for kc in range(8):
                pend_dve.append(((t0, n), lambda kc=kc: hmul(kc)))

        LEAD = 3

        def ffn(tag, tiles, pendA, pendB, sqrow, doneB, done_last_inside, hook=None):
            def phase1(fl, sg, su, t0, n, mid=None, fine=False):
                flush_for(t0, n)
                hk = hkeys(t0, n)
                bg = bankS()
                bu = bankS()

                def mmw(e, s, b):
                    for kc in range(8):
                        inst = e.matmul(ps[b][:, 0:n], lhsT=ring[:, s, kc * 128:(kc + 1) * 128],
                                        rhs=hT[:, kc, t0:t0 + n], start=(kc == 0), stop=(kc == 7))
                    return inst
                if fine:
                    for kc in range(8):
                        S.op("pe", lambda e, kc=kc: e.matmul(ps[bg][:, 0:n], lhsT=ring[:, sg, kc * 128:(kc + 1) * 128],
                                                              rhs=hT[:, kc, t0:t0 + n], start=(kc == 0), stop=(kc == 7)),
                             reads=hkeys(t0, n, [kc]) + [("ring", sg)], writes=[("ps", bg)])
                else:
                    S.op("pe", lambda e: mmw(e, sg, bg), reads=hk + [("ring", sg)], writes=[("ps", bg)])
                if mid is not None:
                    mid()
                S.op("pe", lambda e: mmw(e, su, bu), reads=hk + [("ring", su)], writes=[("ps", bu)])
                sb_i = state["sS"]
                state["sS"] ^= 1
                S.op("act", lambda e: e.activation(out=sS[:, sb_i, 0:n], in_=ps[bg][:, 0:n], func=AF.Silu),
                     reads=[("ps", bg)], writes=[("sS", sb_i)])
                S.op("dve", lambda e: e.tensor_tensor(out=aT[:, fl, t0:t0 + n], in0=ps[bu][:, 0:n], in1=sS[:, sb_i, 0:n], op=ALU.mult),
                     reads=[("ps", bu), ("sS", sb_i)], writes=blk_keys("a", t0, n, fl))
                drain_dve(4)

            nt = len(tiles)
            for gi, grp in enumerate(GROUPS):
                fls = list(enumerate(grp))
                if hook is not None and gi == len(GROUPS) - 1:
                    hook("last_group_start", 0, 0)
                if gi == 0:
                    lead = fls[:LEAD]
                    slots = [(consume((tag + "g", f)), consume((tag + "u", f))) for (_, f) in lead]
                    for ti, (t0, n) in enumerate(tiles):
                        early = ti + 1 < nt and pendA[ti] is None
                        if ti + 1 < nt and pendA[ti] is not None:
                            pendA[ti]()
                        for li, ((fl, f), (sg, su)) in enumerate(zip(lead, slots)):
                            if li == 0 and early:
                                phase1(fl, sg, su, t0, n, lambda ti=ti: (pendB[ti](), drain_dve(4)), fine=True)
                            else:
                                phase1(fl, sg, su, t0, n, fine=(li == 0))
                            if li == 0 and ti + 1 < nt and not early:
                                pendB[ti]()
                                drain_dve(4)
                    release(2 * len(lead))
                    fls = fls[LEAD:]
                for fl, f in fls:
                    sg = consume((tag + "g", f))
                    su = consume((tag + "u", f))
                    for (t0, n) in tiles:
                        phase1(fl, sg, su, t0, n)
                    release(2)
                dslots = [consume((tag + "d", f)) for f in grp]
                last = gi == len(GROUPS) - 1
                if last:
                    S.op("act", lambda e: e.activation(out=rstq[:, 0:1], in_=epst[:], func=AF.Ln), reads=["eps"], writes=[("rstq", 0)])
                if last and hook is not None:
                    hook("last_p2_start", 0, 0)
                for ti, (t0, n) in enumerate(tiles):
                    for o in range(8):
                        b = (bankS() if hook is not None else bank6()) if last else bankM(True)

                        def mmd(e, b=b, o=o, t0=t0, n=n, dslots=dslots):
                            for fl, s in enumerate(dslots):
                                inst = e.matmul(ps[b][:, 0:n], lhsT=ring[:, s, o * 128:(o + 1) * 128],
                                                rhs=aT[:, fl, t0:t0 + n], start=(fl == 0), stop=(fl == len(dslots) - 1))
                            return inst
                        rk = [("ring", s) for s in dslots]
                        for fl in range(len(dslots)):
                            rk += blk_keys("a", t0, n, fl)
                        S.op("pe", mmd, reads=rk, writes=[("ps", b)])
                        S.op("dve", lambda e, b=b, o=o, t0=t0, n=n: e.scalar_tensor_tensor(
                            out=xT[:, o, t0:t0 + n], in0=ps[b][:, 0:n], scalar=0.5, in1=xT[:, o, t0:t0 + n],
                            op0=ALU.mult, op1=ALU.add),
                            reads=[("ps", b)] + blk_keys("x", t0, n), writes=blk_keys("x", t0, n) + blk_keys("xr", t0, n, o))
                        drain_dve(2 if (last and hook is None and ti == nt - 1 and 1 <= o <= 4) else 1)
                        if last and hook is not None:
                            hook("last_p2_group", ti, o)
                        if last and ti >= 1 and o == 0:
                            doneB(ti - 1)
                        if last:
                            sqrow(ti, o)
                if last and done_last_inside:
                    doneB(nt - 1)
                    drain_dve(1000)
                release(len(dslots))

        def proj(slot, t0, n, bank):
            flush_for(t0, n)

            def mm(e):
                for kc in range(8):
                    inst = e.matmul(ps[bank][:, 0:n], lhsT=ring[:, slot, kc * 128:(kc + 1) * 128],
                                    rhs=hT[:, kc, t0:t0 + n], start=(kc == 0), stop=(kc == 7))
                return inst
            S.op("pe", mm, reads=hkeys(t0, n) + [("ring", slot)], writes=[("ps", bank)])

        def mixer(st, tiles_all, tiles_own, pendB_last):
            g0 = ST_G0[st]
            nblk = ST_LEN[st] // 128
            gb0 = g0 // 128
            def own_part(tile):
                a, m = tile
                if st == 0 and a < HALO:
                    return (HALO, a + m - HALO)
                return tile

            while pend_low and not pend_dve:
                pend_low.pop(0)()
            s_k = consume(("k", 0))
            s_v = consume(("v", 0))
            s_q = [consume(("q", c)) for c in range(4)]
            for ti, (t0, n) in enumerate(tiles_all):
                b = bankS()
                proj(s_k, t0, n, b)
                if ti == 0 and len(tiles_all) == 2 and pendB_last is not None:
                    pendB_last()
                    pendB_last = None
                S.op("act", lambda e, b=b, t0=t0, n=n: e.activation(out=kT[:, g0 + t0:g0 + t0 + n], in_=ps[b][:, 0:n], func=AF.Copy),
                     reads=[("ps", b)], writes=blk_keys("k", g0 + t0, n))
                lb0 = t0 // 128
                nb = n // 128
                b = bankS()

                def mmv(e, b=b, lb0=lb0, nb=nb):
                    for j in range(nb):
                        c0 = (lb0 + j) * 128
                        for kc in range(8):
                            inst = e.matmul(ps[b][:, j * 128:(j + 1) * 128], lhsT=hT[:, kc, c0:c0 + 128],
                                            rhs=ring[:, s_v, kc * 128:(kc + 1) * 128], start=(kc == 0), stop=(kc == 7))
                    return inst
                S.op("pe", mmv, reads=hkeys(t0, n) + [("ring", s_v)], writes=[("ps", b)])
                S.op("dve", lambda e, b=b, lb0=lb0, nb=nb: e.tensor_copy(
                    out=vaug[:, gb0 + lb0:gb0 + lb0 + nb, :, 0:64],
                    in_=ps[b][:, 0:nb * 128].rearrange("p (b k d) -> p b k d", b=nb, k=2, d=64)),
                    reads=[("ps", b), "vaug_init", "vaug_halo"], writes=[("v", gb0 + lb0 + j) for j in range(nb)])
                tq, nq = own_part((t0, n))
                for c in range(4):
                    b = bankS()
                    proj(s_q[c], tq, nq, b)
                    S.op("act", lambda e, b=b, tq=tq, nq=nq, c=c: e.activation(out=qT[:, c, tq:tq + nq], in_=ps[b][:, 0:nq], func=AF.Copy),
                         reads=[("ps", b)], writes=blk_keys("q", tq, nq, c))
                    drain_dve(2)
                if ti == 0 and pendB_last is not None:
                    pendB_last()
            release(6)

            cwide = [False]

            def cbank(pool):
                if cwide[0]:
                    return bank6()
                return bankM() if pool == "M" else bankS()

            def conv_uc(i, t0, n, slots_i):
                s_u, s_c, s_b = slots_i
                ub = state["uS"]
                state["uS"] ^= 1
                b1 = cbank("M")
                proj(s_u, t0, n, b1)
                S.op("act", lambda e: e.activation(out=uS[:, ub, 0:n], in_=ps[b1][:, 0:n], func=AF.Copy),
                     reads=[("ps", b1)], writes=[("uS", ub)])
                b2 = cbank("S")
                proj(s_c, t0, n, b2)
                S.op("dve", lambda e: e.tensor_tensor(out=cuT[:, i, 2 + g0 + t0:2 + g0 + t0 + n], in0=ps[b2][:, 0:n],
                                                      in1=uS[:, ub, 0:n], op=ALU.mult),
                     reads=[("ps", b2), ("uS", ub), "cu_pad"], writes=blk_keys("cu", g0 + t0, n, i))

            def conv_by(i, t0, n, slots_i):
                s_u, s_c, s_b = slots_i
                bb = state["bS_"]
                state["bS_"] ^= 1
                b1 = cbank("M")
                proj(s_b, t0, n, b1)
                S.op("act", lambda e: e.activation(out=bS[:, bb, 0:n], in_=ps[b1][:, 0:n], func=AF.Copy),
                     reads=[("ps", b1)], writes=[("bS", bb)])
                by = cbank("S")

                def mmy(e):
                    for r in range(3):
                        c0 = g0 + t0 + r
                        inst = e.matmul(ps[by][:, 0:n], lhsT=diag[:, i * 3 + r, :], rhs=cuT[:, i, c0:c0 + n],
                                        start=(r == 0), stop=(r == 2))
                    return inst
                rk = ["diag", "cu_pad"] + blk_keys("cu", max(g0 + t0 - 2, 0), n + 2, i)
                S.op("pe", mmy, reads=rk, writes=[("ps", by)])
                S.op("dve", lambda e: e.tensor_tensor(out=convT[:, i, t0:t0 + n], in0=ps[by][:, 0:n], in1=bS[:, bb, 0:n], op=ALU.mult),
                     reads=[("ps", by), ("bS", bb)], writes=blk_keys("conv", t0, n, i))

            cn_bank = {}

            def conv_norm_A(t0, n):
                ck = []
                for i in range(4):
                    ck += blk_keys("conv", t0, n, i)
                flush_for(t0, n)
                hk = hkeys(t0, n, range(4))
                S.op("act", lambda e: e.activation(out=hT[:, 0:4, t0:t0 + n], in_=convT[:, :, t0:t0 + n], func=AF.Square),
                     reads=ck, writes=hk)

            def conv_norm_B(t0, n):
                ck = []
                for i in range(4):
                    ck += blk_keys("conv", t0, n, i)
                hk = hkeys(t0, n, range(4))
                b = bankN()

                def mmn(e):
                    for i in range(4):
                        inst = e.matmul(ps[b][:, 0:n], lhsT=ones[:], rhs=hT[:, i, t0:t0 + n], start=(i == 0), stop=(i == 3))
                    return inst
                S.op("pe", mmn, reads=hk + ["ones"], writes=[("ps", b)])
                S.op("act", lambda e: e.activation(out=ps[b][:, 0:n], in_=ps[b][:, 0:n], func=AF.Ln, scale=1.0 / 512, bias=epst[:]),
                     reads=[("ps", b), "eps"], writes=[("ps", b)])
                S.op("act", lambda e: e.activation(out=ps[b][:, 0:n], in_=ps[b][:, 0:n], func=AF.Exp, scale=-0.5),
                     reads=[("ps", b)], writes=[("ps", b)])

                def cmul(e):
                    for i in range(4):
                        col = G_CONVN + i
                        inst = e.scalar_tensor_tensor(out=aT[:, 4 + i, t0:t0 + n], in0=convT[:, i, t0:t0 + n],
                                                      scalar=gains[:, col:col + 1], in1=ps[b][:, 0:n],
                                                      op0=ALU.mult, op1=ALU.mult)
                    return inst
                mk = []
                for i in range(4):
                    mk += blk_keys("mix", t0, n, 4 + i)
                S.op("dve", cmul, reads=ck + [("ps", b), "gains"], writes=mk)

            own_lb0 = tiles_own[0][0] // 128
            blocks = list(range(own_lb0, nblk))
            pbs = {}

            def att_A(lbq):
                gb = gb0 + lbq
                pb = state["att"] % 2
                state["att"] += 1
                pbs[lbq] = pb
                q0 = lbq * 128
                for jc in range(2):
                    ba = state["bS"] & 2
                    state["bS"] = (ba + 2) % 4
                    kcol = (gb - 1 + jc) * 128

                    def mms(e, jc=jc, ba=ba, kcol=kcol):
                        for kv in range(2):
                            b = ba + kv
                            e.matmul(ps[b].rearrange("p (c q) -> p c q", c=4), lhsT=kT[kv * 64:(kv + 1) * 64, kcol:kcol + 128],
                                     rhs=qT[kv * 64:(kv + 1) * 64, 0:4, q0:q0 + 128], start=True, stop=False)
                        for kv in range(2):
                            b = ba + kv
                            e.matmul(ps[b], lhsT=identb[:], rhs=BH[:, jc, kv * 512:(kv + 1) * 512], start=False, stop=False)
                            inst = e.matmul(ps[b], lhsT=identb[:], rhs=BL[:, jc, kv * 512:(kv + 1) * 512], start=False, stop=True)
                        return inst
                    S.op("pe", mms, reads=blk_keys("k", kcol, 128) + [("q", c, lbq) for c in range(4)] + ["BH", "BL", "identb"],
                         writes=[("ps", ba), ("ps", ba + 1)])
                    S.op("act", lambda e, jc=jc, ba=ba: e.activation(out=pT[:, pb, 2 * jc:2 * jc + 2, :], in_=psall[:, ba:ba + 2, :],
                                                                     func=AF.Exp, scale=0.125),
                         reads=[("ps", ba), ("ps", ba + 1)], writes=[("pTs", pb, jc)])

            def att_B(lbq):
                gb = gb0 + lbq
                pb = pbs[lbq]
                for kv in range(2):
                    def mmpv(e, kv=kv):
                        for c in range(4):
                            for jc in range(2):
                                inst = e.matmul(ps[6 + kv][:, c * 65:(c + 1) * 65],
                                                lhsT=pT[:, pb, jc * 2 + kv, c * 128:(c + 1) * 128],
                                                rhs=vaug[:, gb - 1 + jc, kv, 0:65], start=(jc == 0), stop=(jc == 1))
                        return inst
                    S.op("pe", mmpv, reads=[("pTs", pb, 0), ("pTs", pb, 1), ("v", gb - 1), ("v", gb), "vaug_halo"],
                         writes=[("ps", 6 + kv)])

                def dens(e):
                    for kv in range(2):
                        den = ps[6 + kv][:, 0:260].rearrange("p (c e) -> p c e", e=65)[:, :, 64:65]
                        inst = e.tensor_tensor(out=rden[:, pb, kv * 4:(kv + 1) * 4, :], in0=den, in1=esink[:, kv * 4:(kv + 1) * 4, :], op=ALU.add)
                    return inst
                S.op("dve", dens, reads=[("ps", 6), ("ps", 7), "esink"], writes=[("rden", pb)])
                S.op("dve", lambda e: e.reciprocal(out=rden[:, pb], in_=rden[:, pb]), reads=[("rden", pb)], writes=[("rden", pb)])

                def normz(e):
                    for kv in range(2):
                        pvv = ps[6 + kv][:, 0:260].rearrange("p (c e) -> p c e", e=65)[:, :, 0:64]
                        inst = e.tensor_tensor(
                            out=atmp[:, pb, kv * 256:(kv + 1) * 256].rearrange("p (c d) -> p c d", d=64),
                            in0=pvv, in1=rden[:, pb, kv * 4:(kv + 1) * 4, :].broadcast_to([128, 4, 64]), op=ALU.mult)
                    return inst
                S.op("dve", normz, reads=[("ps", 6), ("ps", 7), ("rden", pb)], writes=[("atmp", pb)])
                S.op("act", lambda e: e.activation(out=anf[:, pb, :], in_=atmp[:, pb, :], func=AF.Square, accum_out=ssq[:, pb:pb + 1]),
                     reads=[("atmp", pb)], writes=[("anf", pb), ("ssq", pb)])
                S.op("act", lambda e: e.activation(out=rstq[:, pb:pb + 1], in_=ssq[:, pb:pb + 1], func=AF.Ln, scale=1.0 / 512, bias=epst[:]),
                     reads=[("ssq", pb), "eps"], writes=[("rstq", pb)])
                S.op("act", lambda e: e.activation(out=rstq[:, pb:pb + 1], in_=rstq[:, pb:pb + 1], func=AF.Exp, scale=-0.5),
                     reads=[("rstq", pb)], writes=[("rstq", pb)])
                S.op("dve", lambda e: e.scalar_tensor_tensor(
                    out=anb[:, pb, :], in0=atmp[:, pb, :], scalar=rstq[:, pb:pb + 1], in1=gattn[:], op0=ALU.mult, op1=ALU.mult),
                    reads=[("atmp", pb), ("rstq", pb), "gattn"], writes=[("anb", pb)])

            def att_C(lbq):
                pb = pbs[lbq]
                q0 = lbq * 128
                b = bankM()

                def mmt(e):
                    for c in range(4):
                        inst = e.matmul(ps[b][:, c * 128:(c + 1) * 128], lhsT=anb[:, pb, c * 128:(c + 1) * 128], rhs=identb[:],
                                        start=True, stop=True)
                    return inst
                S.op("pe", mmt, reads=[("anb", pb), "identb"], writes=[("ps", b)])
                S.op("dve", lambda e: e.tensor_copy(
                    out=aT[:, 0:4, q0:q0 + 128], in_=ps[b].rearrange("p (c q) -> p c q", c=4)),
                    reads=[("ps", b)], writes=[("mix", c, lbq) for c in range(4)])

            conv_units = []
            cslots = {}

            def cunit_uc(i, t0, n, first):
                if first:
                    cslots[i] = (consume(("cu", i)), consume(("cc", i)), consume(("cb", i)))
                conv_uc(i, t0, n, cslots[i])

            def cunit_by(i, t0, n, last):
                conv_by(i, t0, n, cslots[i])
                if last:
                    release(3)
            for i in range(4):
                for ti, (t0, n) in enumerate(tiles_all):
                    if st == 0 and t0 < HALO:
                        t0, n = HALO - 2, t0 + n - (HALO - 2)
                    conv_units.append(lambda i=i, t0=t0, n=n, f=(ti == 0): cunit_uc(i, t0, n, f))
                for ti, tile in enumerate(tiles_all):
                    t0, n = own_part(tile)
                    conv_units.append(lambda i=i, t0=t0, n=n, l=(ti == len(tiles_all) - 1): cunit_by(i, t0, n, l))

            oslots = []

            wbank = [0]
            wstate = {}
            wsq = []

            def wout_half(ti, o, half):
                if not oslots:
                    oslots.extend(consume(("o", oo)) for oo in range(8))
                t0, n = tiles_own[ti]
                if half == 0:
                    b = wbank[0]
                    wbank[0] = (b + 1) % 6
                    wstate[(ti, o)] = b
                b = wstate[(ti, o)]
                kcs = range(4) if half == 0 else range(4, 8)
                mk = []
                for c in kcs:
                    mk += blk_keys("mix", t0, n, c)

                def mmo(e):
                    for kc in kcs:
                        inst = e.matmul(ps[b][:, 0:n], lhsT=ring[:, oslots[o], kc * 128:(kc + 1) * 128],
                                        rhs=aT[:, kc, t0:t0 + n], start=(kc == 0), stop=(kc == 7))
                    return inst
                S.op("pe", mmo, reads=mk + [("ring", oslots[o])], writes=[("ps", b)])
                if half == 1:
                    S.op("dve", lambda e: e.tensor_tensor(out=xT[:, o, t0:t0 + n], in0=ps[b][:, 0:n], in1=xT[:, o, t0:t0 + n], op=ALU.add),
                         reads=[("ps", b)] + blk_keys("x", t0, n), writes=blk_keys("x", t0, n) + blk_keys("xr", t0, n, o))
                    drain_dve(1)
                    wsq.append((ti, o))

            def wout_group(ti, o):
                wout_half(ti, o, 0)
                wout_half(ti, o, 1)

            nb_ = len(blocks)
            nsteps = nb_ + 2
            ncu = len(conv_units)
            for step in range(nsteps):
                if step < nb_:
                    att_A(blocks[step])
                else:
                    cwide[0] = True
                for _ in range(((step + 1) * ncu) // nsteps - (step * ncu) // nsteps):
                    if conv_units:
                        conv_units.pop(0)()
                if 1 <= step < nb_ + 1:
                    att_B(blocks[step - 1])
                if 2 <= step:
                    att_C(blocks[step - 2])
            while conv_units:
                conv_units.pop(0)()
            conv_norm_A(*tiles_own[0])
            for (t0, n) in tiles_own[1:]:
                conv_norm_A(t0, n)
            for o in range(0, 3):
                wout_half(0, o, 0)
            conv_norm_B(*tiles_own[0])
            for o in range(3, 6):
                wout_half(0, o, 0)
            for (t0, n) in tiles_own[1:]:
                conv_norm_B(t0, n)
            def wsq_flush():
                while wsq:
                    ti_, o_ = wsq.pop(0)
                    norm_sq_row(tiles_own[ti_], o_)
            for o in range(0, 6):
                wout_half(0, o, 1)
            for o in range(6, 8):
                wout_group(0, o)
            wsq_flush()
            for ti in range(1, len(tiles_own)):
                for o in range(8):
                    wout_group(ti, o)
                    if o == 0:
                        norm_B(tiles_own[ti - 1], G_FFN2)
                    wsq_flush()
                    if o >= 1:
                        drain_dve(1)
            release(8)
            del mixer_end[:]
            mixer_end.extend((S.engs[nm]["sem"], S.engs[nm]["count"], nm) for nm in ("pe", "act", "dve") if S.engs[nm]["count"] > 0)

        load_x_first(TILES_ALL[0][0])
        setup()
        S.op("act", lambda e: e.activation(out=rstq[:, 0:1], in_=epst[:], func=AF.Ln), reads=["eps"], writes=[("rstq", 0)])
        pump()
        for i, tile in enumerate(TILES_ALL[0]):
            if i >= 1:
                load_x_tile(0, i, tile)
        for st in range(2):
            tiles_all = TILES_ALL[st]
            tiles_own = TILES_OWN[st]
            if st == 0:
                norm_h_tile(tiles_all[0], G_FFN1)
                setup2()
                pendA = [(lambda tile=tile: norm_A(tile)) for tile in tiles_all[1:]]
                pendB = [(lambda tile=tile: norm_B(tile, G_FFN1)) for tile in tiles_all[1:]]
            else:
                while pend_dve:
                    pend_dve.pop(0)[1]()
                pendA = [None]
                pendB = [deferred_final[0]]
            ffn("1", tiles_all, pendA, pendB,
                lambda ti, o: norm_sq_row(tiles_all[ti], o), lambda ti: norm_B(tiles_all[ti], G_MIX), False)
            mixer(st, tiles_all, tiles_own, lambda: norm_B(tiles_all[-1], G_MIX))

            def finB(ti, st=st, tiles_own=tiles_own):
                if st == 0:
                    final_norm_B(st, tiles_own[ti], (lambda: load_x_tile(1, ti, TILES_ALL[1][ti])), 0, 4 + ti)
                else:
                    final_norm_B(st, tiles_own[ti], None)

            def hook(ev, ti, o):
                if ev == "last_group_start":
                    pre_load(0)
                    pre_load(1)
                elif ev == "last_p2_start":
                    pre_norm_A(0)
                    pre_norm_A(1)
                elif ev == "last_p2_group" and ti == 0 and o == 3:
                    pre_norm_B(0)
                elif ev == "last_p2_group" and ti == 0 and o == 7:
                    pre_norm_B(1)
            if st == 0:
                ffn("2", tiles_own, [None], [lambda: norm_B(tiles_own[-1], G_FFN2)],
                    lambda ti, o, tiles_own=tiles_own: final_sq_row(tiles_own[ti], o), finB, False, hook)
                deferred_final = [lambda finB=finB, k=len(tiles_own) - 1: finB(k)]
            else:
                t0l, nl = tiles_own[-1]
                halves = [(t0l, nl // 2), (t0l + nl // 2, nl // 2)]

                def doneB2(ti, tiles_own=tiles_own):
                    if ti < len(tiles_own) - 1:
                        final_norm_B(1, tiles_own[ti])
                    else:
                        for qi in range(4):
                            drain_dve(1000)
                            final_norm_B(1, (t0l + qi * (nl // 4), nl // 4), None, qi * (nl // 4))
                ffn("2", tiles_own, [None], [lambda: norm_B(tiles_own[-1], G_FFN2)],
                    lambda ti, o, tiles_own=tiles_own: final_sq_row(tiles_own[ti], o), doneB2, True, None)
                drain_dve(1000)
        assert state["consumed"] == total_chunks and state["issued"] == total_chunks

        with nc.Block() as block:
            S.emit(block, list(last_out.values()))
    return nc


def _chunk_cols(W, cols):
    sub = np.ascontiguousarray(W[:, cols])
    return sub.reshape(8, 128, 128).transpose(1, 0, 2).reshape(128, 1024)


def _t5_bucket(n):
    n = np.maximum(n, 0)
    max_exact = 16
    large = max_exact + (np.log(np.maximum(n, 1).astype(np.float32) / np.float32(max_exact))
                         / np.float32(math.log(128 / max_exact)) * np.float32(32 - max_exact)).astype(np.int32)
    large = np.minimum(large, 31)
    return np.where(n < max_exact, n, large)


def kernel(x, rel_bias_table, ffn1_norm, ffn1_w_gate, ffn1_w_up, ffn1_w_down, mix_norm, w_in, conv_w, attn_sinks,
           attn_out_norm, conv_out_norm, w_out, ffn2_norm, ffn2_w_gate, ffn2_w_up, ffn2_w_down, final_norm):
    f32 = np.float32
    x = np.asarray(x, f32)
    order = stream_order()
    ws = np.empty((NCHUNK, 128, 1024), f32)
    W = {"1g": np.asarray(ffn1_w_gate, f32)[0], "1u": np.asarray(ffn1_w_up, f32)[0], "1d": np.asarray(ffn1_w_down, f32)[0],
         "2g": np.asarray(ffn2_w_gate, f32)[0], "2u": np.asarray(ffn2_w_up, f32)[0], "2d": np.asarray(ffn2_w_down, f32)[0]}
    win = np.asarray(w_in, f32)[0]
    wo = np.asarray(w_out, f32)[0]
    ar = np.arange
    for n, (kind, idx) in enumerate(order):
        if kind in ("1g", "1u", "2g", "2u"):
            ws[n] = _chunk_cols(W[kind], ar(idx * 128, (idx + 1) * 128))
        elif kind in ("1d", "2d"):
            ws[n] = W[kind][idx * 128:(idx + 1) * 128, :]
        elif kind == "q":
            cols = np.concatenate([ar(idx * 64, (idx + 1) * 64), ar((4 + idx) * 64, (5 + idx) * 64)])
            ws[n] = _chunk_cols(win, cols)
        elif kind == "k":
            ws[n] = _chunk_cols(win, ar(512, 640))
        elif kind == "v":
            ws[n] = _chunk_cols(win, ar(640, 768))
        elif kind == "cu":
            ws[n] = _chunk_cols(win, ar(768 + idx * 128, 768 + (idx + 1) * 128))
        elif kind == "cb":
            ws[n] = _chunk_cols(win, ar(1280 + idx * 128, 1280 + (idx + 1) * 128))
        elif kind == "cc":
            ws[n] = _chunk_cols(win, ar(1792 + idx * 128, 1792 + (idx + 1) * 128))
        elif kind == "o":
            ws[n] = _chunk_cols(wo, ar(idx * 128, (idx + 1) * 128))
        else:
            raise AssertionError(kind)
    gains = np.zeros((128, NGCOL), f32)

    def cols(v, n):
        return np.asarray(v, f32).reshape(n, 128).T
    gains[:, G_FFN1:G_FFN1 + 8] = cols(np.asarray(ffn1_norm)[0], 8)
    gains[:, G_MIX:G_MIX + 8] = cols(np.asarray(mix_norm)[0], 8)
    gains[:, G_FFN2:G_FFN2 + 8] = cols(np.asarray(ffn2_norm)[0], 8)
    gains[:, G_FINAL:G_FINAL + 8] = cols(np.asarray(final_norm), 8)
    gains[:, G_CONVN:G_CONVN + 4] = cols(np.asarray(conv_out_norm)[0], 4)
    cw = np.asarray(conv_w, f32)[0]
    for r in range(3):
        gains[:, G_CONVW + r * 4:G_CONVW + r * 4 + 4] = cols(cw[r], 4)
    gattn = np.ascontiguousarray(np.broadcast_to(np.asarray(attn_out_norm, f32)[0][None, :], (128, 512)))
    sinks = np.ascontiguousarray(np.broadcast_to(np.asarray(attn_sinks, f32)[0][None, :], (128, 8)))
    j = np.arange(128)[:, None]
    q = np.arange(128)[None, :]
    tbl = np.asarray(rel_bias_table, f32)
    biasT = np.empty((128, 2, 8, 128), f32)
    maskT = np.empty((128, 2, 8, 128), f32)
    for jc in range(2):
        dist = q + 128 - (jc * 128 + j)
        valid = (dist >= 0) & (dist < 128)
        bk = _t5_bucket(dist)
        g = tbl[bk]
        biasT[:, jc] = g.transpose(0, 2, 1)
        maskT[:, jc] = np.where(valid[:, None, :], f32(0.0), f32(-240000.0))
    biasT = biasT.reshape(128, 2, 1024)
    maskT = maskT.reshape(128, 2, 1024)
    ident = np.eye(128, dtype=f32)
    in_maps = []
    for c in range(NCORE):
        b, s = divmod(c, 4)
        own = x[b, s * SEG:(s + 1) * SEG]
        if s == 0:
            hal = np.zeros((HALO, D), f32)
        else:
            hal = x[b, s * SEG - HALO:s * SEG]
        xc = np.concatenate([hal, own], axis=0)
        xTc = np.ascontiguousarray(xc.T.reshape(8, 128, NTOK).transpose(1, 0, 2))
        in_maps.append({
            "xT": xTc, "ws": ws, "gains": gains, "gattn": gattn, "sinks": sinks,
            "halo_ok": np.full((128, 1), 0.0 if s == 0 else 1.0, f32),
            "maskT": maskT, "biasT": biasT, "ident": ident,
        })
    nc = build_program()
    res = run_bass_kernel_spmd(nc, in_maps, core_ids=list(range(NCORE)))
    out = np.empty((2, SEQ, D), f32)
    for c in range(NCORE):
        b, s = divmod(c, 4)
        oT = res.results[c]["outT"]
        out[b, s * SEG:(s + 1) * SEG] = oT.transpose(1, 0, 2).reshape(D, SEG).T
    return out
```

```python
import math
import numpy as np
import concourse.bass as bass
import concourse.mybir as mybir
from concourse.bass_utils import run_bass_kernel_spmd
from contextlib import ExitStack

F32 = mybir.dt.float32
BF16 = mybir.dt.bfloat16
AF = mybir.ActivationFunctionType
ALU = mybir.AluOpType

D = 1024
DFF = 2816
NF = DFF // 128
SEQ = 8192
NCORE = 8
SEG = 2048
HALO = 128
NTOK = SEG + HALO
ST_LEN = (1152, 1024)
ST_G0 = (0, 1152)
GROUPS = [list(range(0, 7)), list(range(7, 14)), list(range(14, 22))]
NS = 12
EPS = 1e-6
G_FFN1, G_MIX, G_FFN2, G_FINAL, G_CONVN, G_CONVW = 0, 8, 16, 24, 32, 36
NGCOL = 48


def stream_order():
    order = []

    def ffn(tag):
        for grp in GROUPS:
            for f in grp:
                order.append((tag + "g", f))
                order.append((tag + "u", f))
            for f in grp:
                order.append((tag + "d", f))
    ffn("1")
    order.append(("k", 0))
    order.append(("v", 0))
    for c in range(4):
        order.append(("q", c))
    for i in range(4):
        order.append(("cu", i))
        order.append(("cc", i))
        order.append(("cb", i))
    for o in range(8):
        order.append(("o", o))
    ffn("2")
    return order


NCHUNK = len(stream_order())


class Res:
    __slots__ = ("w", "r")

    def __init__(self):
        self.w = None
        self.r = []


class Sched:
    ENG = ("pe", "act", "dve", "pool", "sp")

    def __init__(self, nc, es):
        self.nc = nc
        self.es = es
        self.engs = {}
        for name in self.ENG:
            sem = es.enter_context(nc.semaphore("s_" + name))
            self.engs[name] = dict(sem=sem, count=0, ops=[], waited={})
        self.res = {}

    def dma_sem(self, name):
        return dict(sem=self.es.enter_context(self.nc.semaphore(name)), count=0)

    def R(self, key):
        r = self.res.get(key)
        if r is None:
            r = self.res[key] = Res()
        return r

    def op(self, eng, fn, reads=(), writes=(), dma=None, extra=()):
        E = self.engs[eng]
        need = []
        for k in reads:
            r = self.R(k)
            if r.w is not None:
                need.append(r.w)
        for k in writes:
            r = self.R(k)
            if r.w is not None:
                need.append(r.w)
            need.extend(r.r)
        need.extend(extra)
        if dma is not None:
            dma["count"] += 16
            tok = (dma["sem"], dma["count"], None)
        else:
            E["count"] += 1
            tok = (E["sem"], E["count"], eng)
        mx = {}
        for (sem, val, src) in need:
            if src == "pe" and eng == "pe" and dma is None:
                continue
            k = id(sem)
            if k not in mx or mx[k][1] < val:
                mx[k] = (sem, val)
        waits = []
        for k, (sem, val) in mx.items():
            if E["waited"].get(k, 0) >= val:
                continue
            E["waited"][k] = val
            waits.append((sem, val))
        E["ops"].append((waits, fn, tok))
        for k in reads:
            self.R(k).r.append(tok)
        for k in writes:
            r = self.R(k)
            r.w = tok
            r.r = []
        return tok

    def emit(self, block, final_waits=()):
        def runner(name):
            def body(e):
                for (waits, fn, tok) in self.engs[name]["ops"]:
                    for (sem, val) in waits:
                        e.wait_ge(sem, val)
                    inst = fn(e)
                    inst.then_inc(tok[0], 16 if tok[2] is None else 1)
                if name == "sp":
                    for (sem, val, _) in final_waits:
                        e.wait_ge(sem, val)
            return body

        block.tensor(runner("pe"))
        block.scalar(runner("act"))
        block.vector(runner("dve"))
        block.gpsimd(runner("pool"))
        block.sync(runner("sp"))


def hkeys(t0, n, kcs=range(8)):
    return [("h", kc, b) for kc in kcs for b in range(t0 // 128, (t0 + n + 127) // 128)]


def blk_keys(kind, t0, n, *extra):
    return [(kind,) + tuple(extra) + (b,) for b in range(t0 // 128, (t0 + n + 127) // 128)]


def build_program():
    nc = bass.Bass("TRN2", target_bir_lowering=False)
    xT_d = nc.dram_tensor("xT", [128, 8, NTOK], F32, kind="ExternalInput").ap()
    ws_d = nc.dram_tensor("ws", [NCHUNK, 128, 1024], F32, kind="ExternalInput").ap()
    gains_d = nc.dram_tensor("gains", [128, NGCOL], F32, kind="ExternalInput").ap()
    gattn_d = nc.dram_tensor("gattn", [128, 512], F32, kind="ExternalInput").ap()
    sinks_d = nc.dram_tensor("sinks", [128, 8], F32, kind="ExternalInput").ap()
    halo_d = nc.dram_tensor("halo_ok", [128, 1], F32, kind="ExternalInput").ap()
    mask_d = nc.dram_tensor("maskT", [128, 2, 1024], F32, kind="ExternalInput").ap()
    bias_d = nc.dram_tensor("biasT", [128, 2, 1024], F32, kind="ExternalInput").ap()
    ident_d = nc.dram_tensor("ident", [128, 128], F32, kind="ExternalInput").ap()
    out_d = nc.dram_tensor("outT", [128, 8, SEG], F32, kind="ExternalOutput").ap()

    with ExitStack() as es:
        def sb(name, shape, dt):
            return es.enter_context(nc.sbuf_tensor(name, shape, dt))

        xT = sb("xT_sb", [128, 8, 1152], F32)
        hT = sb("hT_sb", [128, 8, 1152], BF16)
        aT = sb("aT_sb", [128, 8, 1152], BF16)
        ring = sb("ring_sb", [128, NS, 1024], BF16)
        qT = sb("qT_sb", [128, 4, 1152], BF16)
        kT = sb("kT_sb", [128, 18 * 128], BF16)
        vaug = sb("vaug_sb", [128, 18, 2, 66], BF16)
        cuT = sb("cuT_sb", [128, 4, 2 + 18 * 128], BF16)
        uS = sb("uS_sb", [128, 2, 512], F32)
        bS = sb("bS_sb", [128, 2, 512], F32)
        convT = sb("convT_sb", [128, 4, 1152], F32)
        pT = sb("pT_sb", [128, 2, 4, 512], BF16)
        atmp = sb("atmp_sb", [128, 2, 512], F32)
        anf = sb("anf_sb", [128, 2, 512], F32)
        anb = sb("anb_sb", [128, 2, 512], BF16)
        EB = sb("EB_sb", [128, 2, 1024], F32)
        sS = sb("sS_sb", [128, 2, 512], F32)
        identf = sb("identf_sb", [128, 128], F32)
        ones = sb("ones_sb", [128, 128], BF16)
        diag = sb("diag_sb", [128, 12, 128], BF16)
        gains = sb("gains_sb", [128, NGCOL], F32)
        gattn = sb("gattn_sb", [128, 512], F32)
        esink = sb("esink_sb", [128, 8, 1], F32)
        rden = sb("rden_sb", [128, 2, 8, 1], F32)
        halo = sb("halo_sb", [128, 1], F32)
        epst = sb("eps_sb", [128, 1], F32)
        ssq = sb("ssq_sb", [128, 2], F32)
        rstq = sb("rstq_sb", [128, 2], F32)
        psall = es.enter_context(nc.psum_tensor("psall", [128, 8, 512], F32))
        ps = [psall[:, i, :] for i in range(8)]
        BH = sb("BH_sb", [128, 2, 1024], BF16)
        BL = sb("BL_sb", [128, 2, 1024], BF16)
        identb = sb("identb_sb", [128, 128], BF16)

        S = Sched(nc, es)
        ringsem = [S.dma_sem("rg%d" % i) for i in range(NS)]
        xsem = [S.dma_sem("xl%d" % i) for i in range(4)]
        osem = [S.dma_sem("os%d" % i) for i in range(2)]
        last_out = {}
        setup_sems = {n: S.dma_sem("su_" + n) for n in ("gains", "gattn", "sinks", "halo", "mask", "mask2", "bias", "ident")}

        TILES_ALL = ([(0, 384), (384, 384), (768, 384)], [(0, 512), (512, 512)])
        TILES_OWN = ([(128, 512), (640, 512)], [(0, 512), (512, 512)])
        state = dict(xtok=[], bS=0, bM=0, bN=0, b6=0, issued=0, released=0, consumed=0, sS=0, uS=0, bS_=0, att=0)
        order = stream_order()
        total_chunks = 2 * NCHUNK

        def bankS():
            b = state["bS"]
            state["bS"] = (b + 1) % 4
            return b

        def bankM(wide=False):
            b = state["bM"] % 2
            state["bM"] = (b + 1) % 2
            return 4 + b

        def bank6():
            b = state["b6"]
            state["b6"] = (b + 1) % 6
            return b

        def bankN():
            b = state["bN"]
            state["bN"] ^= 1
            return 6 + b

        pend_dve = []
        pend_low = []

        def drain_dve(k=1):
            for _ in range(k):
                if pend_dve:
                    pend_dve.pop(0)[1]()
                elif pend_low:
                    pend_low.pop(0)()

        def flush_for(t0, n):
            lo, hi = t0, t0 + n
            while any((a < hi and lo < a + m_) for ((a, m_), _) in pend_dve):
                pend_dve.pop(0)[1]()

        def pump():
            while state["issued"] < total_chunks and state["issued"] < state["released"] + NS:
                n = state["issued"]
                slot = n % NS
                ci = n % NCHUNK
                S.op("pool", lambda e, slot=slot, ci=ci: e.dma_start(out=ring[:, slot, :], in_=ws_d[ci]),
                     writes=[("ring", slot)], dma=ringsem[slot], extra=state["xtok"])
                state["issued"] += 1

        def consume(expect):
            n = state["consumed"]
            assert order[n % NCHUNK] == expect, (order[n % NCHUNK], expect)
            state["consumed"] += 1
            return n % NS

        def release(k):
            state["released"] += k
            pump()

        def setup():
            S.op("act", lambda e: e.dma_start(out=gains[:], in_=gains_d), writes=["gains"], dma=setup_sems["gains"])
            S.op("act", lambda e: e.dma_start(out=identf[:], in_=ident_d), writes=["identf"], dma=setup_sems["ident"])
            S.op("act", lambda e: e.dma_start(out=gattn[:], in_=gattn_d), writes=["gattn"], dma=setup_sems["gattn"])
            S.op("act", lambda e: e.dma_start(out=esink[:, :, 0], in_=sinks_d), writes=["esink"], dma=setup_sems["sinks"])
            S.op("act", lambda e: e.dma_start(out=halo[:], in_=halo_d), writes=["halo"], dma=setup_sems["halo"])
            S.op("act", lambda e: e.dma_start(out=EB[:], in_=bias_d), writes=["EB"], dma=setup_sems["bias"])
            S.op("act", lambda e: e.dma_start(out=uS[:].rearrange("p a b -> p (a b)"), in_=mask_d[:, 0, :]),
                 writes=[("uS", 0), ("uS", 1)], dma=setup_sems["mask"])
            S.op("act", lambda e: e.dma_start(out=bS[:].rearrange("p a b -> p (a b)"), in_=mask_d[:, 1, :]),
                 writes=[("bS", 0), ("bS", 1)], dma=setup_sems["mask2"])
            S.op("pool", lambda e: e.memset(ones[:], 1.0), writes=["ones"])
            S.op("pool", lambda e: e.memset(epst[:], EPS), writes=["eps"])
            S.op("pool", lambda e: e.memset(vaug[:], 1.0), writes=["vaug_init"])
            S.op("pool", lambda e: e.memset(cuT[:, :, 0:2], 0.0), writes=["cu_pad"])

        def setup2():
            ops = []
            _Sop = S.op

            def defer(*a, **k):
                ops.append(lambda: _Sop(*a, **k))
            defer("act", lambda e: e.activation(out=esink[:], in_=esink[:], func=AF.Exp), reads=["esink"], writes=["esink"])
            defer("dve", lambda e: e.scalar_tensor_tensor(out=EB[:, 0, :], in0=EB[:, 0, :], scalar=8.0,
                                                         in1=uS[:].rearrange("p a b -> p (a b)"), op0=ALU.mult, op1=ALU.add),
                 reads=["EB", ("uS", 0), ("uS", 1)], writes=["EB"])
            defer("dve", lambda e: e.scalar_tensor_tensor(out=EB[:, 1, :], in0=EB[:, 1, :], scalar=8.0,
                                                         in1=bS[:].rearrange("p a b -> p (a b)"), op0=ALU.mult, op1=ALU.add),
                 reads=["EB", ("bS", 0), ("bS", 1)], writes=["EB"])
            defer("dve", lambda e: e.tensor_copy(out=BH[:], in_=EB[:]), reads=["EB"], writes=["BH"])
            defer("dve", lambda e: e.tensor_tensor(out=EB[:], in0=EB[:], in1=BH[:], op=ALU.subtract), reads=["EB", "BH"], writes=["EB"])
            defer("dve", lambda e: e.tensor_copy(out=BL[:], in_=EB[:]), reads=["EB"], writes=["BL"])
            defer("dve", lambda e: e.tensor_copy(out=identb[:], in_=identf[:]), reads=["identf"], writes=["identb"])

            def halo_cols(e):
                e.tensor_copy(out=vaug[:, 0, 0, 64:65], in_=halo[:, 0:1])
                return e.tensor_copy(out=vaug[:, 0, 1, 64:65], in_=halo[:, 0:1])
            defer("dve", halo_cols, reads=["halo", "vaug_init"], writes=["vaug_halo"])

            def mkdiag(e):
                inst = None
                for i in range(4):
                    for r in range(3):
                        col = G_CONVW + r * 4 + i
                        inst = e.tensor_scalar(out=diag[:, i * 3 + r, :], in0=identf[:], scalar1=gains[:, col:col + 1],
                                               scalar2=None, op0=ALU.mult)
                return inst
            defer("dve", mkdiag, reads=["identf", "gains"], writes=["diag"])
            pend_low.extend(ops)

        def load_x_first(tile):
            t0, n = tile
            tok = S.op("sp", lambda e: e.dma_start(out=xT[:, 0:4, t0:t0 + n], in_=xT_d[:, 0:4, t0:t0 + n]),
                       writes=blk_keys("x0a", t0, n), dma=xsem[0])
            S.op("act", lambda e: e.dma_start(out=xT[:, 4:8, t0:t0 + n], in_=xT_d[:, 4:8, t0:t0 + n]),
                 writes=blk_keys("x0b", t0, n), dma=xsem[3])
            state["xtok"] = [tok]

        def load_x_tile(st, i, tile, gate=False):
            g0 = ST_G0[st]
            t0, n = tile
            tok = S.op("sp", lambda e: e.dma_start(out=xT[:, :, t0:t0 + n], in_=xT_d[:, :, g0 + t0:g0 + t0 + n]),
                       writes=blk_keys("x", t0, n), dma=xsem[i])
            state["xtok"] = [tok] if gate else []

        def norm_A(tile):
            t0, n = tile
            S.op("act", lambda e: e.activation(out=hT[:, :, t0:t0 + n], in_=xT[:, :, t0:t0 + n], func=AF.Square),
                 reads=blk_keys("x", t0, n) + blk_keys("x0a", t0, n) + blk_keys("x0b", t0, n), writes=hkeys(t0, n))

        def norm_sq_row(tile, o):
            t0, n = tile
            S.op("act", lambda e: e.activation(out=hT[:, o, t0:t0 + n], in_=xT[:, o, t0:t0 + n], func=AF.Square),
                 reads=blk_keys("xr", t0, n, o), writes=hkeys(t0, n, [o]))

        def norm_B0(tile):
            t0, n = tile
            hk = hkeys(t0, n)
            b = bankN()

            def mm(e):
                for kc in range(8):
                    inst = e.matmul(ps[b][:, 0:n], lhsT=ones[:], rhs=hT[:, kc, t0:t0 + n], start=(kc == 0), stop=(kc == 7))
                return inst
            S.op("pe", mm, reads=hk + ["ones"], writes=[("ps", b)])
            S.op("act", lambda e: e.activation(out=ps[b][:, 0:n], in_=ps[b][:, 0:n], func=AF.Ln, scale=1.0 / D, bias=epst[:]),
                 reads=[("ps", b), "eps"], writes=[("ps", b)])
            S.op("act", lambda e: e.activation(out=ps[b][:, 0:n], in_=ps[b][:, 0:n], func=AF.Exp, scale=-0.5),
                 reads=[("ps", b)], writes=[("ps", b)])
            return b

        def norm_B(tile, gcol):
            t0, n = tile
            b = norm_B0(tile)

            def hmul(kc):
                S.op("dve", lambda e: e.scalar_tensor_tensor(out=hT[:, kc, t0:t0 + n], in0=xT[:, kc, t0:t0 + n],
                                                             scalar=gains[:, gcol + kc:gcol + kc + 1], in1=ps[b][:, 0:n],
                                                             op0=ALU.mult, op1=ALU.mult),
                     reads=blk_keys("x", t0, n) + [("ps", b), "gains"], writes=hkeys(t0, n, [kc]))
            for kc in range(8):
                pend_dve.append(((t0, n), lambda kc=kc: hmul(kc)))

        def norm_h_tile(tile, gcol):
            norm_A(tile)
            norm_B(tile, gcol)
            flush_for(*tile)

        finals = []
        tmp0 = convT[:].rearrange("p a b -> p (a b)")[:, 0:4096].rearrange("p (k n) -> p k n", k=8)
        tmp1a = EB[:].rearrange("p a b -> p (a b)").rearrange("p (k n) -> p k n", k=4)
        tmp1b = atmp[:]
        tmp1c = anf[:]
        sqs = qT[:].rearrange("p a b -> p (a b)")[:, 0:4096].rearrange("p (k n) -> p k n", k=8)

        def tmp_kc(ti, kc):
            if ti == 0:
                return tmp0[:, kc, :]
            if kc < 4:
                return tmp1a[:, kc, :]
            return tmp1b[:, kc - 4, :] if kc < 6 else tmp1c[:, kc - 6, :]

        def final_sq_row(tile, o):
            t0, n = tile
            S.op("act", lambda e: e.activation(out=sqs[:, o, 0:n], in_=xT[:, o, t0:t0 + n], func=AF.Square),
                 reads=blk_keys("xr", t0, n, o), writes=[("sqs", o)])

        def final_norm_B(st, tile, after=None, c0=0, bank=None):
            t0, n = tile
            if bank is None:
                while pend_dve:
                    pend_dve.pop(0)[1]()
                b = bankN()
            else:
                b = bank

            def mm(e):
                for kc in range(8):
                    inst = e.matmul(ps[b][:, 0:n], lhsT=ones[:], rhs=sqs[:, kc, c0:c0 + n], start=(kc == 0), stop=(kc == 7))
                return inst
            S.op("pe", mm, reads=[("sqs", k) for k in range(8)] + ["ones"], writes=[("ps", b)])
            S.op("act", lambda e: e.activation(out=ps[b][:, 0:n], in_=ps[b][:, 0:n], func=AF.Ln, scale=1.0 / D, bias=epst[:]),
                 reads=[("ps", b), "eps"], writes=[("ps", b)])
            S.op("act", lambda e: e.activation(out=ps[b][:, 0:n], in_=ps[b][:, 0:n], func=AF.Exp, scale=-0.5),
                 reads=[("ps", b)], writes=[("ps", b)])
            o0 = (t0 - HALO) if st == 0 else (1024 + t0)

            def fmul(kc):
                S.op("dve", lambda e: e.scalar_tensor_tensor(out=xT[:, kc, t0:t0 + n], in0=xT[:, kc, t0:t0 + n],
                                                             scalar=gains[:, G_FINAL + kc:G_FINAL + kc + 1], in1=ps[b][:, 0:n],
                                                             op0=ALU.mult, op1=ALU.mult),
                     reads=blk_keys("x", t0, n) + [("ps", b), "gains"], writes=blk_keys("xf", t0, n, kc))
                if kc == 3 or kc == 7:
                    lo = kc - 3
                    qi = 0 if kc == 3 else 1
                    rk = blk_keys("x", t0, n)
                    for k in range(lo, lo + 4):
                        rk += blk_keys("xf", t0, n, k)
                    t = S.op("sp", lambda e: e.dma_start(out=out_d[:, lo:lo + 4, o0:o0 + n], in_=xT[:, lo:lo + 4, t0:t0 + n]),
                             reads=rk, dma=osem[qi])
                    last_out[qi] = t
                    if kc == 7 and after is not None:
                        after()
            for kc in range(8):
                pend_low.append(lambda kc=kc: fmul(kc))

        mixer_end = []

        def pre_load(ti):
            g0 = ST_G0[1]
            t0, n = TILES_ALL[1][ti]
            if ti == 0:
                S.op("sp", lambda e: e.dma_start(out=tmp0, in_=xT_d[:, :, g0 + t0:g0 + t0 + n]),
                     writes=[("tmp", 0, 0), ("tmp", 0, 1), ("tmp", 0, 2)], dma=xsem[0], extra=mixer_end)
            else:
                S.op("sp", lambda e: e.dma_start(out=tmp1a, in_=xT_d[:, 0:4, g0 + t0:g0 + t0 + n]),
                     writes=[("tmp", 1, 0)], dma=xsem[1], extra=mixer_end)
                S.op("sp", lambda e: e.dma_start(out=tmp1b, in_=xT_d[:, 4:6, g0 + t0:g0 + t0 + n]),
                     writes=[("tmp", 1, 1)], dma=xsem[2], extra=mixer_end)
                S.op("sp", lambda e: e.dma_start(out=tmp1c, in_=xT_d[:, 6:8, g0 + t0:g0 + t0 + n]),
                     writes=[("tmp", 1, 2)], dma=xsem[3], extra=mixer_end)

        def pre_norm_A(ti):
            t0, n = TILES_ALL[1][ti]
            if ti == 0:
                S.op("act", lambda e: e.activation(out=hT[:, :, t0:t0 + n], in_=tmp0, func=AF.Square),
                     reads=[("tmp", 0, 0), ("tmp", 0, 1), ("tmp", 0, 2)], writes=hkeys(t0, n))
            else:
                S.op("act", lambda e: e.activation(out=hT[:, 0:4, t0:t0 + n], in_=tmp1a, func=AF.Square),
                     reads=[("tmp", 1, 0)], writes=hkeys(t0, n, range(4)))
                S.op("act", lambda e: e.activation(out=hT[:, 4:6, t0:t0 + n], in_=tmp1b, func=AF.Square),
                     reads=[("tmp", 1, 1)], writes=hkeys(t0, n, range(4, 6)))
                S.op("act", lambda e: e.activation(out=hT[:, 6:8, t0:t0 + n], in_=tmp1c, func=AF.Square),
                     reads=[("tmp", 1, 2)], writes=hkeys(t0, n, range(6, 8)))

        def pre_norm_B(ti):
            t0, n = TILES_ALL[1][ti]
            b = norm_B0((t0, n))

            def hmul(kc):
                S.op("dve", lambda e: e.scalar_tensor_tensor(out=hT[:, kc, t0:t0 + n], in0=tmp_kc(ti, kc),
                                                             scalar=gains[:, G_FFN1 + kc:G_FFN1 + kc + 1], in1=ps[b][:, 0:n],
                                                             op0=ALU.mult, op1=ALU.mult),
                     reads=[("tmp", ti, (0 if kc < 4 else (1 if kc < 6 else 2))), ("ps", b), "gains"], writes=hkeys(t0, n, [kc]))
            for kc in range(8):
                pend_dve.append(((t0, n), lambda kc=kc: hmul(kc)))

        LEAD = 4

        def ffn(tag, tiles, pendA, pendB, sqrow, doneB, done_last_inside, hook=None):
            def phase1(fl, sg, su, t0, n, mid=None, fine=False):
                flush_for(t0, n)
                hk = hkeys(t0, n)
                bg = bankS()
                bu = bankS()

                def mmw(e, s, b):
                    for kc in range(8):
                        inst = e.matmul(ps[b][:, 0:n], lhsT=ring[:, s, kc * 128:(kc + 1) * 128],
                                        rhs=hT[:, kc, t0:t0 + n], start=(kc == 0), stop=(kc == 7))
                    return inst
                if fine:
                    for kc in range(8):
                        S.op("pe", lambda e, kc=kc: e.matmul(ps[bg][:, 0:n], lhsT=ring[:, sg, kc * 128:(kc + 1) * 128],
                                                              rhs=hT[:, kc, t0:t0 + n], start=(kc == 0), stop=(kc == 7)),
                             reads=hkeys(t0, n, [kc]) + [("ring", sg)], writes=[("ps", bg)])
                else:
                    S.op("pe", lambda e: mmw(e, sg, bg), reads=hk + [("ring", sg)], writes=[("ps", bg)])
                if mid is not None:
                    mid()
                S.op("pe", lambda e: mmw(e, su, bu), reads=hk + [("ring", su)], writes=[("ps", bu)])
                sb_i = state["sS"]
                state["sS"] ^= 1
                S.op("act", lambda e: e.activation(out=sS[:, sb_i, 0:n], in_=ps[bg][:, 0:n], func=AF.Silu),
                     reads=[("ps", bg)], writes=[("sS", sb_i)])
                S.op("dve", lambda e: e.tensor_tensor(out=aT[:, fl, t0:t0 + n], in0=ps[bu][:, 0:n], in1=sS[:, sb_i, 0:n], op=ALU.mult),
                     reads=[("ps", bu), ("sS", sb_i)], writes=blk_keys("a", t0, n, fl))
                drain_dve(4)

            nt = len(tiles)
            for gi, grp in enumerate(GROUPS):
                fls = list(enumerate(grp))
                if hook is not None and gi == len(GROUPS) - 1:
                    hook("last_group_start", 0, 0)
                if gi == 0:
                    lead = fls[:LEAD]
                    slots = [(consume((tag + "g", f)), consume((tag + "u", f))) for (_, f) in lead]
                    for ti, (t0, n) in enumerate(tiles):
                        early = ti + 1 < nt and pendA[ti] is None
                        if ti + 1 < nt and pendA[ti] is not None:
                            pendA[ti]()
                        for li, ((fl, f), (sg, su)) in enumerate(zip(lead, slots)):
                            if li == 0 and early:
                                phase1(fl, sg, su, t0, n, lambda ti=ti: (pendB[ti](), drain_dve(4)), fine=True)
                            else:
                                phase1(fl, sg, su, t0, n, fine=(li == 0))
                            if li == 0 and ti + 1 < nt and not early:
                                pendB[ti]()
                                drain_dve(4)
                    release(2 * len(lead))
                    fls = fls[LEAD:]
                for fl, f in fls:
                    sg = consume((tag + "g", f))
                    su = consume((tag + "u", f))
                    for (t0, n) in tiles:
                        phase1(fl, sg, su, t0, n)
                    release(2)
                dslots = [consume((tag + "d", f)) for f in grp]
                last = gi == len(GROUPS) - 1
                if last:
                    S.op("act", lambda e: e.activation(out=rstq[:, 0:1], in_=epst[:], func=AF.Ln), reads=["eps"], writes=[("rstq", 0)])
                if last and hook is not None:
                    hook("last_p2_start", 0, 0)
                for ti, (t0, n) in enumerate(tiles):
                    for o in range(8):
                        b = (bankS() if hook is not None else bank6()) if last else bankM(True)

                        def mmd(e, b=b, o=o, t0=t0, n=n, dslots=dslots):
                            for fl, s in enumerate(dslots):
                                inst = e.matmul(ps[b][:, 0:n], lhsT=ring[:, s, o * 128:(o + 1) * 128],
                                                rhs=aT[:, fl, t0:t0 + n], start=(fl == 0), stop=(fl == len(dslots) - 1))
                            return inst
                        rk = [("ring", s) for s in dslots]
                        for fl in range(len(dslots)):
                            rk += blk_keys("a", t0, n, fl)
                        S.op("pe", mmd, reads=rk, writes=[("ps", b)])
                        S.op("dve", lambda e, b=b, o=o, t0=t0, n=n: e.scalar_tensor_tensor(
                            out=xT[:, o, t0:t0 + n], in0=ps[b][:, 0:n], scalar=0.5, in1=xT[:, o, t0:t0 + n],
                            op0=ALU.mult, op1=ALU.add),
                            reads=[("ps", b)] + blk_keys("x", t0, n), writes=blk_keys("x", t0, n) + blk_keys("xr", t0, n, o))
                        drain_dve(2 if (last and hook is None and ti == nt - 1 and 1 <= o <= 4) else 1)
                        if last and hook is not None:
                            hook("last_p2_group", ti, o)
                        if last and ti >= 1 and o == 0:
                            doneB(ti - 1)
                        if last:
                            sqrow(ti, o)
                if last and done_last_inside:
                    doneB(nt - 1)
                    drain_dve(1000)
                release(len(dslots))

        def proj(slot, t0, n, bank):
            flush_for(t0, n)

            def mm(e):
                for kc in range(8):
                    inst = e.matmul(ps[bank][:, 0:n], lhsT=ring[:, slot, kc * 128:(kc + 1) * 128],
                                    rhs=hT[:, kc, t0:t0 + n], start=(kc == 0), stop=(kc == 7))
                return inst
            S.op("pe", mm, reads=hkeys(t0, n) + [("ring", slot)], writes=[("ps", bank)])

        def mixer(st, tiles_all, tiles_own, pendB_last):
            g0 = ST_G0[st]
            nblk = ST_LEN[st] // 128
            gb0 = g0 // 128
            def own_part(tile):
                a, m = tile
                if st == 0 and a < HALO:
                    return (HALO, a + m - HALO)
                return tile

            while pend_low and not pend_dve:
                pend_low.pop(0)()
            s_k = consume(("k", 0))
            s_v = consume(("v", 0))
            s_q = [consume(("q", c)) for c in range(4)]
            for ti, (t0, n) in enumerate(tiles_all):
                b = bankS()
                proj(s_k, t0, n, b)
                if ti == 0 and len(tiles_all) == 2 and pendB_last is not None:
                    pendB_last()
                    pendB_last = None
                S.op("act", lambda e, b=b, t0=t0, n=n: e.activation(out=kT[:, g0 + t0:g0 + t0 + n], in_=ps[b][:, 0:n], func=AF.Copy),
                     reads=[("ps", b)], writes=blk_keys("k", g0 + t0, n))
                lb0 = t0 // 128
                nb = n // 128
                b = bankS()

                def mmv(e, b=b, lb0=lb0, nb=nb):
                    for j in range(nb):
                        c0 = (lb0 + j) * 128
                        for kc in range(8):
                            inst = e.matmul(ps[b][:, j * 128:(j + 1) * 128], lhsT=hT[:, kc, c0:c0 + 128],
                                            rhs=ring[:, s_v, kc * 128:(kc + 1) * 128], start=(kc == 0), stop=(kc == 7))
                    return inst
                S.op("pe", mmv, reads=hkeys(t0, n) + [("ring", s_v)], writes=[("ps", b)])
                S.op("dve", lambda e, b=b, lb0=lb0, nb=nb: e.tensor_copy(
                    out=vaug[:, gb0 + lb0:gb0 + lb0 + nb, :, 0:64],
                    in_=ps[b][:, 0:nb * 128].rearrange("p (b k d) -> p b k d", b=nb, k=2, d=64)),
                    reads=[("ps", b), "vaug_init", "vaug_halo"], writes=[("v", gb0 + lb0 + j) for j in range(nb)])
                tq, nq = own_part((t0, n))
                for c in range(4):
                    b = bankS()
                    proj(s_q[c], tq, nq, b)
                    S.op("act", lambda e, b=b, tq=tq, nq=nq, c=c: e.activation(out=qT[:, c, tq:tq + nq], in_=ps[b][:, 0:nq], func=AF.Copy),
                         reads=[("ps", b)], writes=blk_keys("q", tq, nq, c))
                    drain_dve(2)
                if ti == 0 and pendB_last is not None:
                    pendB_last()
            release(6)

            cwide = [False]

            def cbank(pool):
                if cwide[0]:
                    return bank6()
                return bankM() if pool == "M" else bankS()

            def conv_uc(i, t0, n, slots_i):
                s_u, s_c, s_b = slots_i
                ub = state["uS"]
                state["uS"] ^= 1
                b1 = cbank("M")
                proj(s_u, t0, n, b1)
                S.op("act", lambda e: e.activation(out=uS[:, ub, 0:n], in_=ps[b1][:, 0:n], func=AF.Copy),
                     reads=[("ps", b1)], writes=[("uS", ub)])
                b2 = cbank("S")
                proj(s_c, t0, n, b2)
                S.op("dve", lambda e: e.tensor_tensor(out=cuT[:, i, 2 + g0 + t0:2 + g0 + t0 + n], in0=ps[b2][:, 0:n],
                                                      in1=uS[:, ub, 0:n], op=ALU.mult),
                     reads=[("ps", b2), ("uS", ub), "cu_pad"], writes=blk_keys("cu", g0 + t0, n, i))

            def conv_by(i, t0, n, slots_i):
                s_u, s_c, s_b = slots_i
                bb = state["bS_"]
                state["bS_"] ^= 1
                b1 = cbank("M")
                proj(s_b, t0, n, b1)
                S.op("act", lambda e: e.activation(out=bS[:, bb, 0:n], in_=ps[b1][:, 0:n], func=AF.Copy),
                     reads=[("ps", b1)], writes=[("bS", bb)])
                by = cbank("S")

                def mmy(e):
                    for r in range(3):
                        c0 = g0 + t0 + r
                        inst = e.matmul(ps[by][:, 0:n], lhsT=diag[:, i * 3 + r, :], rhs=cuT[:, i, c0:c0 + n],
                                        start=(r == 0), stop=(r == 2))
                    return inst
                rk = ["diag", "cu_pad"] + blk_keys("cu", max(g0 + t0 - 2, 0), n + 2, i)
                S.op("pe", mmy, reads=rk, writes=[("ps", by)])
                S.op("dve", lambda e: e.tensor_tensor(out=convT[:, i, t0:t0 + n], in0=ps[by][:, 0:n], in1=bS[:, bb, 0:n], op=ALU.mult),
                     reads=[("ps", by), ("bS", bb)], writes=blk_keys("conv", t0, n, i))

            cn_bank = {}

            def conv_norm_A(t0, n):
                ck = []
                for i in range(4):
                    ck += blk_keys("conv", t0, n, i)
                flush_for(t0, n)
                hk = hkeys(t0, n, range(4))
                S.op("act", lambda e: e.activation(out=hT[:, 0:4, t0:t0 + n], in_=convT[:, :, t0:t0 + n], func=AF.Square),
                     reads=ck, writes=hk)

            def conv_norm_B(t0, n):
                ck = []
                for i in range(4):
                    ck += blk_keys("conv", t0, n, i)
                hk = hkeys(t0, n, range(4))
                b = bankN()

                def mmn(e):
                    for i in range(4):
                        inst = e.matmul(ps[b][:, 0:n], lhsT=ones[:], rhs=hT[:, i, t0:t0 + n], start=(i == 0), stop=(i == 3))
                    return inst
                S.op("pe", mmn, reads=hk + ["ones"], writes=[("ps", b)])
                S.op("act", lambda e: e.activation(out=ps[b][:, 0:n], in_=ps[b][:, 0:n], func=AF.Ln, scale=1.0 / 512, bias=epst[:]),
                     reads=[("ps", b), "eps"], writes=[("ps", b)])
                S.op("act", lambda e: e.activation(out=ps[b][:, 0:n], in_=ps[b][:, 0:n], func=AF.Exp, scale=-0.5),
                     reads=[("ps", b)], writes=[("ps", b)])

                def cmul(e):
                    for i in range(4):
                        col = G_CONVN + i
                        inst = e.scalar_tensor_tensor(out=aT[:, 4 + i, t0:t0 + n], in0=convT[:, i, t0:t0 + n],
                                                      scalar=gains[:, col:col + 1], in1=ps[b][:, 0:n],
                                                      op0=ALU.mult, op1=ALU.mult)
                    return inst
                mk = []
                for i in range(4):
                    mk += blk_keys("mix", t0, n, 4 + i)
                S.op("dve", cmul, reads=ck + [("ps", b), "gains"], writes=mk)

            own_lb0 = tiles_own[0][0] // 128
            blocks = list(range(own_lb0, nblk))
            pbs = {}

            def att_A(lbq):
                gb = gb0 + lbq
                pb = state["att"] % 2
                state["att"] += 1
                pbs[lbq] = pb
                q0 = lbq * 128
                for jc in range(2):
                    ba = state["bS"] & 2
                    state["bS"] = (ba + 2) % 4
                    kcol = (gb - 1 + jc) * 128

                    def mms(e, jc=jc, ba=ba, kcol=kcol):
                        for kv in range(2):
                            b = ba + kv
                            e.matmul(ps[b].rearrange("p (c q) -> p c q", c=4), lhsT=kT[kv * 64:(kv + 1) * 64, kcol:kcol + 128],
                                     rhs=qT[kv * 64:(kv + 1) * 64, 0:4, q0:q0 + 128], start=True, stop=False)
                        for kv in range(2):
                            b = ba + kv
                            e.matmul(ps[b], lhsT=identb[:], rhs=BH[:, jc, kv * 512:(kv + 1) * 512], start=False, stop=False)
                            inst = e.matmul(ps[b], lhsT=identb[:], rhs=BL[:, jc, kv * 512:(kv + 1) * 512], start=False, stop=True)
                        return inst
                    S.op("pe", mms, reads=blk_keys("k", kcol, 128) + [("q", c, lbq) for c in range(4)] + ["BH", "BL", "identb"],
                         writes=[("ps", ba), ("ps", ba + 1)])
                    S.op("act", lambda e, jc=jc, ba=ba: e.activation(out=pT[:, pb, 2 * jc:2 * jc + 2, :], in_=psall[:, ba:ba + 2, :],
                                                                     func=AF.Exp, scale=0.125),
                         reads=[("ps", ba), ("ps", ba + 1)], writes=[("pTs", pb, jc)])

            def att_B(lbq):
                gb = gb0 + lbq
                pb = pbs[lbq]
                for kv in range(2):
                    def mmpv(e, kv=kv):
                        for c in range(4):
                            for jc in range(2):
                                inst = e.matmul(ps[6 + kv][:, c * 65:(c + 1) * 65],
                                                lhsT=pT[:, pb, jc * 2 + kv, c * 128:(c + 1) * 128],
                                                rhs=vaug[:, gb - 1 + jc, kv, 0:65], start=(jc == 0), stop=(jc == 1))
                        return inst
                    S.op("pe", mmpv, reads=[("pTs", pb, 0), ("pTs", pb, 1), ("v", gb - 1), ("v", gb), "vaug_halo"],
                         writes=[("ps", 6 + kv)])

                def dens(e):
                    for kv in range(2):
                        den = ps[6 + kv][:, 0:260].rearrange("p (c e) -> p c e", e=65)[:, :, 64:65]
                        inst = e.tensor_tensor(out=rden[:, pb, kv * 4:(kv + 1) * 4, :], in0=den, in1=esink[:, kv * 4:(kv + 1) * 4, :], op=ALU.add)
                    return inst
                S.op("dve", dens, reads=[("ps", 6), ("ps", 7), "esink"], writes=[("rden", pb)])
                S.op("dve", lambda e: e.reciprocal(out=rden[:, pb], in_=rden[:, pb]), reads=[("rden", pb)], writes=[("rden", pb)])

                def normz(e):
                    for kv in range(2):
                        pvv = ps[6 + kv][:, 0:260].rearrange("p (c e) -> p c e", e=65)[:, :, 0:64]
                        inst = e.tensor_tensor(
                            out=atmp[:, pb, kv * 256:(kv + 1) * 256].rearrange("p (c d) -> p c d", d=64),
                            in0=pvv, in1=rden[:, pb, kv * 4:(kv + 1) * 4, :].broadcast_to([128, 4, 64]), op=ALU.mult)
                    return inst
                S.op("dve", normz, reads=[("ps", 6), ("ps", 7), ("rden", pb)], writes=[("atmp", pb)])
                S.op("act", lambda e: e.activation(out=anf[:, pb, :], in_=atmp[:, pb, :], func=AF.Square, accum_out=ssq[:, pb:pb + 1]),
                     reads=[("atmp", pb)], writes=[("anf", pb), ("ssq", pb)])
                S.op("act", lambda e: e.activation(out=rstq[:, pb:pb + 1], in_=ssq[:, pb:pb + 1], func=AF.Ln, scale=1.0 / 512, bias=epst[:]),
                     reads=[("ssq", pb), "eps"], writes=[("rstq", pb)])
                S.op("act", lambda e: e.activation(out=rstq[:, pb:pb + 1], in_=rstq[:, pb:pb + 1], func=AF.Exp, scale=-0.5),
                     reads=[("rstq", pb)], writes=[("rstq", pb)])
                S.op("dve", lambda e: e.scalar_tensor_tensor(
                    out=anb[:, pb, :], in0=atmp[:, pb, :], scalar=rstq[:, pb:pb + 1], in1=gattn[:], op0=ALU.mult, op1=ALU.mult),
                    reads=[("atmp", pb), ("rstq", pb), "gattn"], writes=[("anb", pb)])

            def att_C(lbq):
                pb = pbs[lbq]
                q0 = lbq * 128
                b = bankM()

                def mmt(e):
                    for c in range(4):
                        inst = e.matmul(ps[b][:, c * 128:(c + 1) * 128], lhsT=anb[:, pb, c * 128:(c + 1) * 128], rhs=identb[:],
                                        start=True, stop=True)
                    return inst
                S.op("pe", mmt, reads=[("anb", pb), "identb"], writes=[("ps", b)])
                S.op("dve", lambda e: e.tensor_copy(
                    out=aT[:, 0:4, q0:q0 + 128], in_=ps[b].rearrange("p (c q) -> p c q", c=4)),
                    reads=[("ps", b)], writes=[("mix", c, lbq) for c in range(4)])

            conv_units = []
            cslots = {}

            def cunit_uc(i, t0, n, first):
                if first:
                    cslots[i] = (consume(("cu", i)), consume(("cc", i)), consume(("cb", i)))
                conv_uc(i, t0, n, cslots[i])

            def cunit_by(i, t0, n, last):
                conv_by(i, t0, n, cslots[i])
                if last:
                    release(3)
            for i in range(4):
                for ti, (t0, n) in enumerate(tiles_all):
                    if st == 0 and t0 < HALO:
                        t0, n = HALO - 2, t0 + n - (HALO - 2)
                    conv_units.append(lambda i=i, t0=t0, n=n, f=(ti == 0): cunit_uc(i, t0, n, f))
                for ti, tile in enumerate(tiles_all):
                    t0, n = own_part(tile)
                    conv_units.append(lambda i=i, t0=t0, n=n, l=(ti == len(tiles_all) - 1): cunit_by(i, t0, n, l))

            oslots = []

            wbank = [0]
            wstate = {}
            wsq = []

            def wout_half(ti, o, half):
                if not oslots:
                    oslots.extend(consume(("o", oo)) for oo in range(8))
                t0, n = tiles_own[ti]
                if half == 0:
                    b = wbank[0]
                    wbank[0] = (b + 1) % 6
                    wstate[(ti, o)] = b
                b = wstate[(ti, o)]
                kcs = range(4) if half == 0 else range(4, 8)
                mk = []
                for c in kcs:
                    mk += blk_keys("mix", t0, n, c)

                def mmo(e):
                    for kc in kcs:
                        inst = e.matmul(ps[b][:, 0:n], lhsT=ring[:, oslots[o], kc * 128:(kc + 1) * 128],
                                        rhs=aT[:, kc, t0:t0 + n], start=(kc == 0), stop=(kc == 7))
                    return inst
                S.op("pe", mmo, reads=mk + [("ring", oslots[o])], writes=[("ps", b)])
                if half == 1:
                    S.op("dve", lambda e: e.tensor_tensor(out=xT[:, o, t0:t0 + n], in0=ps[b][:, 0:n], in1=xT[:, o, t0:t0 + n], op=ALU.add),
                         reads=[("ps", b)] + blk_keys("x", t0, n), writes=blk_keys("x", t0, n) + blk_keys("xr", t0, n, o))
                    drain_dve(1)
                    wsq.append((ti, o))

            def wout_group(ti, o):
                wout_half(ti, o, 0)
                wout_half(ti, o, 1)

            nb_ = len(blocks)
            nsteps = nb_ + 2
            ncu = len(conv_units)
            for step in range(nsteps):
                if step < nb_:
                    att_A(blocks[step])
                else:
                    cwide[0] = True
                for _ in range(((step + 1) * ncu) // nsteps - (step * ncu) // nsteps):
                    if conv_units:
                        conv_units.pop(0)()
                if 1 <= step < nb_ + 1:
                    att_B(blocks[step - 1])
                if 2 <= step:
                    att_C(blocks[step - 2])
            while conv_units:
                conv_units.pop(0)()
            conv_norm_A(*tiles_own[0])
            for (t0, n) in tiles_own[1:]:
                conv_norm_A(t0, n)
            for o in range(0, 3):
                wout_half(0, o, 0)
            conv_norm_B(*tiles_own[0])
            for o in range(3, 6):
                wout_half(0, o, 0)
            for (t0, n) in tiles_own[1:]:
                conv_norm_B(t0, n)
            def wsq_flush():
                while wsq:
                    ti_, o_ = wsq.pop(0)
                    norm_sq_row(tiles_own[ti_], o_)
            for o in range(0, 6):
                wout_half(0, o, 1)
            for o in range(6, 8):
                wout_group(0, o)
            wsq_flush()
            for ti in range(1, len(tiles_own)):
                for o in range(8):
                    wout_group(ti, o)
                    if o == 0:
                        norm_B(tiles_own[ti - 1], G_FFN2)
                    wsq_flush()
                    if o >= 1:
                        drain_dve(1)
            release(8)
            del mixer_end[:]
            mixer_end.extend((S.engs[nm]["sem"], S.engs[nm]["count"], nm) for nm in ("pe", "act", "dve") if S.engs[nm]["count"] > 0)

        load_x_first(TILES_ALL[0][0])
        setup()
        S.op("act", lambda e: e.activation(out=rstq[:, 0:1], in_=epst[:], func=AF.Ln), reads=["eps"], writes=[("rstq", 0)])
        pump()
        for i, tile in enumerate(TILES_ALL[0]):
            if i >= 1:
                load_x_tile(0, i, tile)
        for st in range(2):
            tiles_all = TILES_ALL[st]
            tiles_own = TILES_OWN[st]
            if st == 0:
                norm_h_tile(tiles_all[0], G_FFN1)
                setup2()
                pendA = [(lambda tile=tile: norm_A(tile)) for tile in tiles_all[1:]]
                pendB = [(lambda tile=tile: norm_B(tile, G_FFN1)) for tile in tiles_all[1:]]
            else:
                while pend_dve:
                    pend_dve.pop(0)[1]()
                pendA = [None]
                pendB = [deferred_final[0]]
            ffn("1", tiles_all, pendA, pendB,
                lambda ti, o: norm_sq_row(tiles_all[ti], o), lambda ti: norm_B(tiles_all[ti], G_MIX), False)
            mixer(st, tiles_all, tiles_own, lambda: norm_B(tiles_all[-1], G_MIX))

            def finB(ti, st=st, tiles_own=tiles_own):
                if st == 0:
                    final_norm_B(st, tiles_own[ti], (lambda: load_x_tile(1, ti, TILES_ALL[1][ti])), 0, 4 + ti)
                else:
                    final_norm_B(st, tiles_own[ti], None)

            def hook(ev, ti, o):
                if ev == "last_group_start":
                    pre_load(0)
                    pre_load(1)
                elif ev == "last_p2_start":
                    pre_norm_A(0)
                    pre_norm_A(1)
                elif ev == "last_p2_group" and ti == 0 and o == 3:
                    pre_norm_B(0)
                elif ev == "last_p2_group" and ti == 0 and o == 7:
                    pre_norm_B(1)
            if st == 0:
                ffn("2", tiles_own, [None], [lambda: norm_B(tiles_own[-1], G_FFN2)],
                    lambda ti, o, tiles_own=tiles_own: final_sq_row(tiles_own[ti], o), finB, False, hook)
                deferred_final = [lambda finB=finB, k=len(tiles_own) - 1: finB(k)]
            else:
                t0l, nl = tiles_own[-1]
                halves = [(t0l, nl // 2), (t0l + nl // 2, nl // 2)]

                def doneB2(ti, tiles_own=tiles_own):
                    if ti < len(tiles_own) - 1:
                        final_norm_B(1, tiles_own[ti])
                    else:
                        drain_dve(1000)
                        final_norm_B(1, halves[0], None, 0)
                        drain_dve(1000)
                        final_norm_B(1, halves[1], None, nl // 2)
                ffn("2", tiles_own, [None], [lambda: norm_B(tiles_own[-1], G_FFN2)],
                    lambda ti, o, tiles_own=tiles_own: final_sq_row(tiles_own[ti], o), doneB2, True, None)
                drain_dve(1000)
        assert state["consumed"] == total_chunks and state["issued"] == total_chunks

        with nc.Block() as block:
            S.emit(block, list(last_out.values()))
    return nc


def _chunk_cols(W, cols):
    sub = np.ascontiguousarray(W[:, cols])
    return sub.reshape(8, 128, 128).transpose(1, 0, 2).reshape(128, 1024)


def _t5_bucket(n):
    n = np.maximum(n, 0)
    max_exact = 16
    large = max_exact + (np.log(np.maximum(n, 1).astype(np.float32) / np.float32(max_exact))
                         / np.float32(math.log(128 / max_exact)) * np.float32(32 - max_exact)).astype(np.int32)
    large = np.minimum(large, 31)
    return np.where(n < max_exact, n, large)


def kernel(x, rel_bias_table, ffn1_norm, ffn1_w_gate, ffn1_w_up, ffn1_w_down, mix_norm, w_in, conv_w, attn_sinks,
           attn_out_norm, conv_out_norm, w_out, ffn2_norm, ffn2_w_gate, ffn2_w_up, ffn2_w_down, final_norm):
    f32 = np.float32
    x = np.asarray(x, f32)
    order = stream_order()
    ws = np.empty((NCHUNK, 128, 1024), f32)
    W = {"1g": np.asarray(ffn1_w_gate, f32)[0], "1u": np.asarray(ffn1_w_up, f32)[0], "1d": np.asarray(ffn1_w_down, f32)[0],
         "2g": np.asarray(ffn2_w_gate, f32)[0], "2u": np.asarray(ffn2_w_up, f32)[0], "2d": np.asarray(ffn2_w_down, f32)[0]}
    win = np.asarray(w_in, f32)[0]
    wo = np.asarray(w_out, f32)[0]
    ar = np.arange
    for n, (kind, idx) in enumerate(order):
        if kind in ("1g", "1u", "2g", "2u"):
            ws[n] = _chunk_cols(W[kind], ar(idx * 128, (idx + 1) * 128))
        elif kind in ("1d", "2d"):
            ws[n] = W[kind][idx * 128:(idx + 1) * 128, :]
        elif kind == "q":
            cols = np.concatenate([ar(idx * 64, (idx + 1) * 64), ar((4 + idx) * 64, (5 + idx) * 64)])
            ws[n] = _chunk_cols(win, cols)
        elif kind == "k":
            ws[n] = _chunk_cols(win, ar(512, 640))
        elif kind == "v":
            ws[n] = _chunk_cols(win, ar(640, 768))
        elif kind == "cu":
            ws[n] = _chunk_cols(win, ar(768 + idx * 128, 768 + (idx + 1) * 128))
        elif kind == "cb":
            ws[n] = _chunk_cols(win, ar(1280 + idx * 128, 1280 + (idx + 1) * 128))
        elif kind == "cc":
            ws[n] = _chunk_cols(win, ar(1792 + idx * 128, 1792 + (idx + 1) * 128))
        elif kind == "o":
            ws[n] = _chunk_cols(wo, ar(idx * 128, (idx + 1) * 128))
        else:
            raise AssertionError(kind)
    gains = np.zeros((128, NGCOL), f32)

    def cols(v, n):
        return np.asarray(v, f32).reshape(n, 128).T
    gains[:, G_FFN1:G_FFN1 + 8] = cols(np.asarray(ffn1_norm)[0], 8)
    gains[:, G_MIX:G_MIX + 8] = cols(np.asarray(mix_norm)[0], 8)
    gains[:, G_FFN2:G_FFN2 + 8] = cols(np.asarray(ffn2_norm)[0], 8)
    gains[:, G_FINAL:G_FINAL + 8] = cols(np.asarray(final_norm), 8)
    gains[:, G_CONVN:G_CONVN + 4] = cols(np.asarray(conv_out_norm)[0], 4)
    cw = np.asarray(conv_w, f32)[0]
    for r in range(3):
        gains[:, G_CONVW + r * 4:G_CONVW + r * 4 + 4] = cols(cw[r], 4)
    gattn = np.ascontiguousarray(np.broadcast_to(np.asarray(attn_out_norm, f32)[0][None, :], (128, 512)))
    sinks = np.ascontiguousarray(np.broadcast_to(np.asarray(attn_sinks, f32)[0][None, :], (128, 8)))
    j = np.arange(128)[:, None]
    q = np.arange(128)[None, :]
    tbl = np.asarray(rel_bias_table, f32)
    biasT = np.empty((128, 2, 8, 128), f32)
    maskT = np.empty((128, 2, 8, 128), f32)
    for jc in range(2):
        dist = q + 128 - (jc * 128 + j)
        valid = (dist >= 0) & (dist < 128)
        bk = _t5_bucket(dist)
        g = tbl[bk]
        biasT[:, jc] = g.transpose(0, 2, 1)
        maskT[:, jc] = np.where(valid[:, None, :], f32(0.0), f32(-240000.0))
    biasT = biasT.reshape(128, 2, 1024)
    maskT = maskT.reshape(128, 2, 1024)
    ident = np.eye(128, dtype=f32)
    in_maps = []
    for c in range(NCORE):
        b, s = divmod(c, 4)
        own = x[b, s * SEG:(s + 1) * SEG]
        if s == 0:
            hal = np.zeros((HALO, D), f32)
        else:
            hal = x[b, s * SEG - HALO:s * SEG]
        xc = np.concatenate([hal, own], axis=0)
        xTc = np.ascontiguousarray(xc.T.reshape(8, 128, NTOK).transpose(1, 0, 2))
        in_maps.append({
            "xT": xTc, "ws": ws, "gains": gains, "gattn": gattn, "sinks": sinks,
            "halo_ok": np.full((128, 1), 0.0 if s == 0 else 1.0, f32),
            "maskT": maskT, "biasT": biasT, "ident": ident,
        })
    nc = build_program()
    res = run_bass_kernel_spmd(nc, in_maps, core_ids=list(range(NCORE)))
    out = np.empty((2, SEQ, D), f32)
    for c in range(NCORE):
        b, s = divmod(c, 4)
        oT = res.results[c]["outT"]
        out[b, s * SEG:(s + 1) * SEG] = oT.transpose(1, 0, 2).reshape(D, SEG).T
    return out
```

```python
import math
import numpy as np
import concourse.bass as bass
import concourse.mybir as mybir
from concourse.bass_utils import run_bass_kernel_spmd
from contextlib import ExitStack

F32 = mybir.dt.float32
BF16 = mybir.dt.bfloat16
AF = mybir.ActivationFunctionType
ALU = mybir.AluOpType

D = 1024
DFF = 2816
NF = DFF // 128
SEQ = 8192
NCORE = 8
SEG = 2048
HALO = 128
NTOK = SEG + HALO
ST_LEN = (1152, 1024)
ST_G0 = (0, 1152)
GROUPS = [list(range(0, 7)), list(range(7, 14)), list(range(14, 22))]
NS = 12
EPS = 1e-6
G_FFN1, G_MIX, G_FFN2, G_FINAL, G_CONVN, G_CONVW = 0, 8, 16, 24, 32, 36
NGCOL = 48


def stream_order():
    order = []

    def ffn(tag):
        for grp in GROUPS:
            for f in grp:
                order.append((tag + "g", f))
                order.append((tag + "u", f))
            for f in grp:
                order.append((tag + "d", f))
    ffn("1")
    order.append(("k", 0))
    order.append(("v", 0))
    for c in range(4):
        order.append(("q", c))
    for i in range(4):
        order.append(("cu", i))
        order.append(("cc", i))
        order.append(("cb", i))
    for o in range(8):
        order.append(("o", o))
    ffn("2")
    return order


NCHUNK = len(stream_order())


class Res:
    __slots__ = ("w", "r")

    def __init__(self):
        self.w = None
        self.r = []


class Sched:
    ENG = ("pe", "act", "dve", "pool", "sp")

    def __init__(self, nc, es):
        self.nc = nc
        self.es = es
        self.engs = {}
        for name in self.ENG:
            sem = es.enter_context(nc.semaphore("s_" + name))
            self.engs[name] = dict(sem=sem, count=0, ops=[], waited={})
        self.res = {}

    def dma_sem(self, name):
        return dict(sem=self.es.enter_context(self.nc.semaphore(name)), count=0)

    def R(self, key):
        r = self.res.get(key)
        if r is None:
            r = self.res[key] = Res()
        return r

    def op(self, eng, fn, reads=(), writes=(), dma=None, extra=()):
        E = self.engs[eng]
        need = []
        for k in reads:
            r = self.R(k)
            if r.w is not None:
                need.append(r.w)
        for k in writes:
            r = self.R(k)
            if r.w is not None:
                need.append(r.w)
            need.extend(r.r)
        need.extend(extra)
        if dma is not None:
            dma["count"] += 16
            tok = (dma["sem"], dma["count"], None)
        else:
            E["count"] += 1
            tok = (E["sem"], E["count"], eng)
        mx = {}
        for (sem, val, src) in need:
            if src == "pe" and eng == "pe" and dma is None:
                continue
            k = id(sem)
            if k not in mx or mx[k][1] < val:
                mx[k] = (sem, val)
        waits = []
        for k, (sem, val) in mx.items():
            if E["waited"].get(k, 0) >= val:
                continue
            E["waited"][k] = val
            waits.append((sem, val))
        E["ops"].append((waits, fn, tok))
        for k in reads:
            self.R(k).r.append(tok)
        for k in writes:
            r = self.R(k)
            r.w = tok
            r.r = []
        return tok

    def emit(self, block, final_waits=()):
        def runner(name):
            def body(e):
                for (waits, fn, tok) in self.engs[name]["ops"]:
                    for (sem, val) in waits:
                        e.wait_ge(sem, val)
                    inst = fn(e)
                    inst.then_inc(tok[0], 16 if tok[2] is None else 1)
                if name == "sp":
                    for (sem, val, _) in final_waits:
                        e.wait_ge(sem, val)
            return body

        block.tensor(runner("pe"))
        block.scalar(runner("act"))
        block.vector(runner("dve"))
        block.gpsimd(runner("pool"))
        block.sync(runner("sp"))


def hkeys(t0, n, kcs=range(8)):
    return [("h", kc, b) for kc in kcs for b in range(t0 // 128, (t0 + n + 127) // 128)]


def blk_keys(kind, t0, n, *extra):
    return [(kind,) + tuple(extra) + (b,) for b in range(t0 // 128, (t0 + n + 127) // 128)]


def build_program():
    nc = bass.Bass("TRN2", target_bir_lowering=False)
    xT_d = nc.dram_tensor("xT", [128, 8, NTOK], F32, kind="ExternalInput").ap()
    ws_d = nc.dram_tensor("ws", [NCHUNK, 128, 1024], F32, kind="ExternalInput").ap()
    gains_d = nc.dram_tensor("gains", [128, NGCOL], F32, kind="ExternalInput").ap()
    gattn_d = nc.dram_tensor("gattn", [128, 512], F32, kind="ExternalInput").ap()
    sinks_d = nc.dram_tensor("sinks", [128, 8], F32, kind="ExternalInput").ap()
    halo_d = nc.dram_tensor("halo_ok", [128, 1], F32, kind="ExternalInput").ap()
    mask_d = nc.dram_tensor("maskT", [128, 2, 1024], F32, kind="ExternalInput").ap()
    bias_d = nc.dram_tensor("biasT", [128, 2, 1024], F32, kind="ExternalInput").ap()
    ident_d = nc.dram_tensor("ident", [128, 128], F32, kind="ExternalInput").ap()
    out_d = nc.dram_tensor("outT", [128, 8, SEG], F32, kind="ExternalOutput").ap()

    with ExitStack() as es:
        def sb(name, shape, dt):
            return es.enter_context(nc.sbuf_tensor(name, shape, dt))

        xT = sb("xT_sb", [128, 8, 1152], F32)
        hT = sb("hT_sb", [128, 8, 1152], BF16)
        aT = sb("aT_sb", [128, 8, 1152], BF16)
        ring = sb("ring_sb", [128, NS, 1024], BF16)
        qT = sb("qT_sb", [128, 4, 1152], BF16)
        kT = sb("kT_sb", [128, 18 * 128], BF16)
        vaug = sb("vaug_sb", [128, 18, 2, 66], BF16)
        cuT = sb("cuT_sb", [128, 4, 2 + 18 * 128], BF16)
        uS = sb("uS_sb", [128, 2, 512], F32)
        bS = sb("bS_sb", [128, 2, 512], F32)
        convT = sb("convT_sb", [128, 4, 1152], F32)
        pT = sb("pT_sb", [128, 2, 4, 512], BF16)
        atmp = sb("atmp_sb", [128, 2, 512], F32)
        anf = sb("anf_sb", [128, 2, 512], F32)
        anb = sb("anb_sb", [128, 2, 512], BF16)
        EB = sb("EB_sb", [128, 2, 1024], F32)
        sS = sb("sS_sb", [128, 2, 512], F32)
        identf = sb("identf_sb", [128, 128], F32)
        ones = sb("ones_sb", [128, 128], BF16)
        diag = sb("diag_sb", [128, 12, 128], BF16)
        gains = sb("gains_sb", [128, NGCOL], F32)
        gattn = sb("gattn_sb", [128, 512], F32)
        esink = sb("esink_sb", [128, 8, 1], F32)
        rden = sb("rden_sb", [128, 2, 8, 1], F32)
        halo = sb("halo_sb", [128, 1], F32)
        epst = sb("eps_sb", [128, 1], F32)
        ssq = sb("ssq_sb", [128, 2], F32)
        rstq = sb("rstq_sb", [128, 2], F32)
        psall = es.enter_context(nc.psum_tensor("psall", [128, 8, 512], F32))
        ps = [psall[:, i, :] for i in range(8)]
        BH = sb("BH_sb", [128, 2, 1024], BF16)
        BL = sb("BL_sb", [128, 2, 1024], BF16)
        identb = sb("identb_sb", [128, 128], BF16)

        S = Sched(nc, es)
        ringsem = [S.dma_sem("rg%d" % i) for i in range(NS)]
        xsem = [S.dma_sem("xl%d" % i) for i in range(4)]
        osem = [S.dma_sem("os%d" % i) for i in range(2)]
        last_out = {}
        setup_sems = {n: S.dma_sem("su_" + n) for n in ("gains", "gattn", "sinks", "halo", "mask", "mask2", "bias", "ident")}

        TILES_ALL = ([(0, 384), (384, 384), (768, 384)], [(0, 512), (512, 512)])
        TILES_OWN = ([(128, 512), (640, 512)], [(0, 512), (512, 512)])
        state = dict(xtok=[], bS=0, bM=0, bN=0, b6=0, issued=0, released=0, consumed=0, sS=0, uS=0, bS_=0, att=0)
        order = stream_order()
        total_chunks = 2 * NCHUNK

        def bankS():
            b = state["bS"]
            state["bS"] = (b + 1) % 4
            return b

        def bankM(wide=False):
            b = state["bM"] % 2
            state["bM"] = (b + 1) % 2
            return 4 + b

        def bank6():
            b = state["b6"]
            state["b6"] = (b + 1) % 6
            return b

        def bankN():
            b = state["bN"]
            state["bN"] ^= 1
            return 6 + b

        pend_dve = []
        pend_low = []

        def drain_dve(k=1):
            for _ in range(k):
                if pend_dve:
                    pend_dve.pop(0)[1]()
                elif pend_low:
                    pend_low.pop(0)()

        def flush_for(t0, n):
            lo, hi = t0, t0 + n
            while any((a < hi and lo < a + m_) for ((a, m_), _) in pend_dve):
                pend_dve.pop(0)[1]()

        def pump():
            while state["issued"] < total_chunks and state["issued"] < state["released"] + NS:
                n = state["issued"]
                slot = n % NS
                ci = n % NCHUNK
                S.op("pool", lambda e, slot=slot, ci=ci: e.dma_start(out=ring[:, slot, :], in_=ws_d[ci]),
                     writes=[("ring", slot)], dma=ringsem[slot], extra=state["xtok"])
                state["issued"] += 1

        def consume(expect):
            n = state["consumed"]
            assert order[n % NCHUNK] == expect, (order[n % NCHUNK], expect)
            state["consumed"] += 1
            return n % NS

        def release(k):
            state["released"] += k
            pump()

        def setup():
            S.op("act", lambda e: e.dma_start(out=gains[:], in_=gains_d), writes=["gains"], dma=setup_sems["gains"])
            S.op("act", lambda e: e.dma_start(out=identf[:], in_=ident_d), writes=["identf"], dma=setup_sems["ident"])
            S.op("act", lambda e: e.dma_start(out=gattn[:], in_=gattn_d), writes=["gattn"], dma=setup_sems["gattn"])
            S.op("act", lambda e: e.dma_start(out=esink[:, :, 0], in_=sinks_d), writes=["esink"], dma=setup_sems["sinks"])
            S.op("act", lambda e: e.dma_start(out=halo[:], in_=halo_d), writes=["halo"], dma=setup_sems["halo"])
            S.op("act", lambda e: e.dma_start(out=EB[:], in_=bias_d), writes=["EB"], dma=setup_sems["bias"])
            S.op("act", lambda e: e.dma_start(out=uS[:].rearrange("p a b -> p (a b)"), in_=mask_d[:, 0, :]),
                 writes=[("uS", 0), ("uS", 1)], dma=setup_sems["mask"])
            S.op("act", lambda e: e.dma_start(out=bS[:].rearrange("p a b -> p (a b)"), in_=mask_d[:, 1, :]),
                 writes=[("bS", 0), ("bS", 1)], dma=setup_sems["mask2"])
            S.op("pool", lambda e: e.memset(ones[:], 1.0), writes=["ones"])
            S.op("pool", lambda e: e.memset(epst[:], EPS), writes=["eps"])
            S.op("pool", lambda e: e.memset(vaug[:], 1.0), writes=["vaug_init"])
            S.op("pool", lambda e: e.memset(cuT[:, :, 0:2], 0.0), writes=["cu_pad"])

        def setup2():
            ops = []
            _Sop = S.op

            def defer(*a, **k):
                ops.append(lambda: _Sop(*a, **k))
            defer("act", lambda e: e.activation(out=esink[:], in_=esink[:], func=AF.Exp), reads=["esink"], writes=["esink"])
            defer("dve", lambda e: e.scalar_tensor_tensor(out=EB[:, 0, :], in0=EB[:, 0, :], scalar=8.0,
                                                         in1=uS[:].rearrange("p a b -> p (a b)"), op0=ALU.mult, op1=ALU.add),
                 reads=["EB", ("uS", 0), ("uS", 1)], writes=["EB"])
            defer("dve", lambda e: e.scalar_tensor_tensor(out=EB[:, 1, :], in0=EB[:, 1, :], scalar=8.0,
                                                         in1=bS[:].rearrange("p a b -> p (a b)"), op0=ALU.mult, op1=ALU.add),
                 reads=["EB", ("bS", 0), ("bS", 1)], writes=["EB"])
            defer("dve", lambda e: e.tensor_copy(out=BH[:], in_=EB[:]), reads=["EB"], writes=["BH"])
            defer("dve", lambda e: e.tensor_tensor(out=EB[:], in0=EB[:], in1=BH[:], op=ALU.subtract), reads=["EB", "BH"], writes=["EB"])
            defer("dve", lambda e: e.tensor_copy(out=BL[:], in_=EB[:]), reads=["EB"], writes=["BL"])
            defer("dve", lambda e: e.tensor_copy(out=identb[:], in_=identf[:]), reads=["identf"], writes=["identb"])

            def halo_cols(e):
                e.tensor_copy(out=vaug[:, 0, 0, 64:65], in_=halo[:, 0:1])
                return e.tensor_copy(out=vaug[:, 0, 1, 64:65], in_=halo[:, 0:1])
            defer("dve", halo_cols, reads=["halo", "vaug_init"], writes=["vaug_halo"])

            def mkdiag(e):
                inst = None
                for i in range(4):
                    for r in range(3):
                        col = G_CONVW + r * 4 + i
                        inst = e.tensor_scalar(out=diag[:, i * 3 + r, :], in0=identf[:], scalar1=gains[:, col:col + 1],
                                               scalar2=None, op0=ALU.mult)
                return inst
            defer("dve", mkdiag, reads=["identf", "gains"], writes=["diag"])
            pend_low.extend(ops)

        def load_x_first(tile):
            t0, n = tile
            tok = S.op("sp", lambda e: e.dma_start(out=xT[:, 0:4, t0:t0 + n], in_=xT_d[:, 0:4, t0:t0 + n]),
                       writes=blk_keys("x0a", t0, n), dma=xsem[0])
            S.op("act", lambda e: e.dma_start(out=xT[:, 4:8, t0:t0 + n], in_=xT_d[:, 4:8, t0:t0 + n]),
                 writes=blk_keys("x0b", t0, n), dma=xsem[3])
            state["xtok"] = [tok]

        def load_x_tile(st, i, tile, gate=False):
            g0 = ST_G0[st]
            t0, n = tile
            tok = S.op("sp", lambda e: e.dma_start(out=xT[:, :, t0:t0 + n], in_=xT_d[:, :, g0 + t0:g0 + t0 + n]),
                       writes=blk_keys("x", t0, n), dma=xsem[i])
            state["xtok"] = [tok] if gate else []

        def norm_A(tile):
            t0, n = tile
            S.op("act", lambda e: e.activation(out=hT[:, :, t0:t0 + n], in_=xT[:, :, t0:t0 + n], func=AF.Square),
                 reads=blk_keys("x", t0, n) + blk_keys("x0a", t0, n) + blk_keys("x0b", t0, n), writes=hkeys(t0, n))

        def norm_sq_row(tile, o):
            t0, n = tile
            S.op("act", lambda e: e.activation(out=hT[:, o, t0:t0 + n], in_=xT[:, o, t0:t0 + n], func=AF.Square),
                 reads=blk_keys("xr", t0, n, o), writes=hkeys(t0, n, [o]))

        def norm_B0(tile):
            t0, n = tile
            hk = hkeys(t0, n)
            b = bankN()

            def mm(e):
                for kc in range(8):
                    inst = e.matmul(ps[b][:, 0:n], lhsT=ones[:], rhs=hT[:, kc, t0:t0 + n], start=(kc == 0), stop=(kc == 7))
                return inst
            S.op("pe", mm, reads=hk + ["ones"], writes=[("ps", b)])
            S.op("act", lambda e: e.activation(out=ps[b][:, 0:n], in_=ps[b][:, 0:n], func=AF.Ln, scale=1.0 / D, bias=epst[:]),
                 reads=[("ps", b), "eps"], writes=[("ps", b)])
            S.op("act", lambda e: e.activation(out=ps[b][:, 0:n], in_=ps[b][:, 0:n], func=AF.Exp, scale=-0.5),
                 reads=[("ps", b)], writes=[("ps", b)])
            return b

        def norm_B(tile, gcol):
            t0, n = tile
            b = norm_B0(tile)

            def hmul(kc):
                S.op("dve", lambda e: e.scalar_tensor_tensor(out=hT[:, kc, t0:t0 + n], in0=xT[:, kc, t0:t0 + n],
                                                             scalar=gains[:, gcol + kc:gcol + kc + 1], in1=ps[b][:, 0:n],
                                                             op0=ALU.mult, op1=ALU.mult),
                     reads=blk_keys("x", t0, n) + [("ps", b), "gains"], writes=hkeys(t0, n, [kc]))
            for kc in range(8):
                pend_dve.append(((t0, n), lambda kc=kc: hmul(kc)))

        def norm_h_tile(tile, gcol):
            norm_A(tile)
            norm_B(tile, gcol)
            flush_for(*tile)

        finals = []
        tmp0 = convT[:].rearrange("p a b -> p (a b)")[:, 0:4096].rearrange("p (k n) -> p k n", k=8)
        tmp1a = EB[:].rearrange("p a b -> p (a b)").rearrange("p (k n) -> p k n", k=4)
        tmp1b = atmp[:]
        tmp1c = anf[:]
        sqs = qT[:].rearrange("p a b -> p (a b)")[:, 0:4096].rearrange("p (k n) -> p k n", k=8)

        def tmp_kc(ti, kc):
            if ti == 0:
                return tmp0[:, kc, :]
            if kc < 4:
                return tmp1a[:, kc, :]
            return tmp1b[:, kc - 4, :] if kc < 6 else tmp1c[:, kc - 6, :]

        def final_sq_row(tile, o):
            t0, n = tile
            S.op("act", lambda e: e.activation(out=sqs[:, o, 0:n], in_=xT[:, o, t0:t0 + n], func=AF.Square),
                 reads=blk_keys("xr", t0, n, o), writes=[("sqs", o)])

        def final_norm_B(st, tile, after=None, c0=0, bank=None):
            t0, n = tile
            if bank is None:
                while pend_dve:
                    pend_dve.pop(0)[1]()
                b = bankN()
            else:
                b = bank

            def mm(e):
                for kc in range(8):
                    inst = e.matmul(ps[b][:, 0:n], lhsT=ones[:], rhs=sqs[:, kc, c0:c0 + n], start=(kc == 0), stop=(kc == 7))
                return inst
            S.op("pe", mm, reads=[("sqs", k) for k in range(8)] + ["ones"], writes=[("ps", b)])
            S.op("act", lambda e: e.activation(out=ps[b][:, 0:n], in_=ps[b][:, 0:n], func=AF.Ln, scale=1.0 / D, bias=epst[:]),
                 reads=[("ps", b), "eps"], writes=[("ps", b)])
            S.op("act", lambda e: e.activation(out=ps[b][:, 0:n], in_=ps[b][:, 0:n], func=AF.Exp, scale=-0.5),
                 reads=[("ps", b)], writes=[("ps", b)])
            o0 = (t0 - HALO) if st == 0 else (1024 + t0)

            def fmul(kc):
                S.op("dve", lambda e: e.scalar_tensor_tensor(out=xT[:, kc, t0:t0 + n], in0=xT[:, kc, t0:t0 + n],
                                                             scalar=gains[:, G_FINAL + kc:G_FINAL + kc + 1], in1=ps[b][:, 0:n],
                                                             op0=ALU.mult, op1=ALU.mult),
                     reads=blk_keys("x", t0, n) + [("ps", b), "gains"], writes=blk_keys("xf", t0, n, kc))
                if kc == 3 or kc == 7:
                    lo = kc - 3
                    qi = 0 if kc == 3 else 1
                    rk = blk_keys("x", t0, n)
                    for k in range(lo, lo + 4):
                        rk += blk_keys("xf", t0, n, k)
                    t = S.op("sp", lambda e: e.dma_start(out=out_d[:, lo:lo + 4, o0:o0 + n], in_=xT[:, lo:lo + 4, t0:t0 + n]),
                             reads=rk, dma=osem[qi])
                    last_out[qi] = t
                    if kc == 7 and after is not None:
                        after()
            for kc in range(8):
                pend_low.append(lambda kc=kc: fmul(kc))

        mixer_end = []

        def pre_load(ti):
            g0 = ST_G0[1]
            t0, n = TILES_ALL[1][ti]
            if ti == 0:
                S.op("sp", lambda e: e.dma_start(out=tmp0, in_=xT_d[:, :, g0 + t0:g0 + t0 + n]),
                     writes=[("tmp", 0, 0), ("tmp", 0, 1), ("tmp", 0, 2)], dma=xsem[0], extra=mixer_end)
            else:
                S.op("sp", lambda e: e.dma_start(out=tmp1a, in_=xT_d[:, 0:4, g0 + t0:g0 + t0 + n]),
                     writes=[("tmp", 1, 0)], dma=xsem[1], extra=mixer_end)
                S.op("sp", lambda e: e.dma_start(out=tmp1b, in_=xT_d[:, 4:6, g0 + t0:g0 + t0 + n]),
                     writes=[("tmp", 1, 1)], dma=xsem[2], extra=mixer_end)
                S.op("sp", lambda e: e.dma_start(out=tmp1c, in_=xT_d[:, 6:8, g0 + t0:g0 + t0 + n]),
                     writes=[("tmp", 1, 2)], dma=xsem[3], extra=mixer_end)

        def pre_norm_A(ti):
            t0, n = TILES_ALL[1][ti]
            if ti == 0:
                S.op("act", lambda e: e.activation(out=hT[:, :, t0:t0 + n], in_=tmp0, func=AF.Square),
                     reads=[("tmp", 0, 0), ("tmp", 0, 1), ("tmp", 0, 2)], writes=hkeys(t0, n))
            else:
                S.op("act", lambda e: e.activation(out=hT[:, 0:4, t0:t0 + n], in_=tmp1a, func=AF.Square),
                     reads=[("tmp", 1, 0)], writes=hkeys(t0, n, range(4)))
                S.op("act", lambda e: e.activation(out=hT[:, 4:6, t0:t0 + n], in_=tmp1b, func=AF.Square),
                     reads=[("tmp", 1, 1)], writes=hkeys(t0, n, range(4, 6)))
                S.op("act", lambda e: e.activation(out=hT[:, 6:8, t0:t0 + n], in_=tmp1c, func=AF.Square),
                     reads=[("tmp", 1, 2)], writes=hkeys(t0, n, range(6, 8)))

        def pre_norm_B(ti):
            t0, n = TILES_ALL[1][ti]
            b = norm_B0((t0, n))

            def hmul(kc):
                S.op("dve", lambda e: e.scalar_tensor_tensor(out=hT[:, kc, t0:t0 + n], in0=tmp_kc(ti, kc),
                                                             scalar=gains[:, G_FFN1 + kc:G_FFN1 + kc + 1], in1=ps[b][:, 0:n],
                                                             op0=ALU.mult, op1=ALU.mult),
                     reads=[("tmp", ti, (0 if kc < 4 else (1 if kc < 6 else 2))), ("ps", b), "gains"], writes=hkeys(t0, n, [kc]))
            for kc in range(8):
                pend_dve.append(((t0, n), lambda kc=kc: hmul(kc)))

        LEAD = 5

        def ffn(tag, tiles, pendA, pendB, sqrow, doneB, done_last_inside, hook=None):
            def phase1(fl, sg, su, t0, n, mid=None, fine=False):
                flush_for(t0, n)
                hk = hkeys(t0, n)
                bg = bankS()
                bu = bankS()

                def mmw(e, s, b):
                    for kc in range(8):
                        inst = e.matmul(ps[b][:, 0:n], lhsT=ring[:, s, kc * 128:(kc + 1) * 128],
                                        rhs=hT[:, kc, t0:t0 + n], start=(kc == 0), stop=(kc == 7))
                    return inst
                if fine:
                    for kc in range(8):
                        S.op("pe", lambda e, kc=kc: e.matmul(ps[bg][:, 0:n], lhsT=ring[:, sg, kc * 128:(kc + 1) * 128],
                                                              rhs=hT[:, kc, t0:t0 + n], start=(kc == 0), stop=(kc == 7)),
                             reads=hkeys(t0, n, [kc]) + [("ring", sg)], writes=[("ps", bg)])
                else:
                    S.op("pe", lambda e: mmw(e, sg, bg), reads=hk + [("ring", sg)], writes=[("ps", bg)])
                if mid is not None:
                    mid()
                S.op("pe", lambda e: mmw(e, su, bu), reads=hk + [("ring", su)], writes=[("ps", bu)])
                sb_i = state["sS"]
                state["sS"] ^= 1
                S.op("act", lambda e: e.activation(out=sS[:, sb_i, 0:n], in_=ps[bg][:, 0:n], func=AF.Silu),
                     reads=[("ps", bg)], writes=[("sS", sb_i)])
                S.op("dve", lambda e: e.tensor_tensor(out=aT[:, fl, t0:t0 + n], in0=ps[bu][:, 0:n], in1=sS[:, sb_i, 0:n], op=ALU.mult),
                     reads=[("ps", bu), ("sS", sb_i)], writes=blk_keys("a", t0, n, fl))
                drain_dve(4)

            nt = len(tiles)
            for gi, grp in enumerate(GROUPS):
                fls = list(enumerate(grp))
                if hook is not None and gi == len(GROUPS) - 1:
                    hook("last_group_start", 0, 0)
                if gi == 0:
                    lead = fls[:LEAD]
                    slots = [(consume((tag + "g", f)), consume((tag + "u", f))) for (_, f) in lead]
                    for ti, (t0, n) in enumerate(tiles):
                        early = ti + 1 < nt and pendA[ti] is None
                        if ti + 1 < nt and pendA[ti] is not None:
                            pendA[ti]()
                        for li, ((fl, f), (sg, su)) in enumerate(zip(lead, slots)):
                            if li == 0 and early:
                                phase1(fl, sg, su, t0, n, lambda ti=ti: (pendB[ti](), drain_dve(4)), fine=True)
                            else:
                                phase1(fl, sg, su, t0, n, fine=(li == 0))
                            if li == 0 and ti + 1 < nt and not early:
                                pendB[ti]()
                                drain_dve(4)
                    release(2 * len(lead))
                    fls = fls[LEAD:]
                for fl, f in fls:
                    sg = consume((tag + "g", f))
                    su = consume((tag + "u", f))
                    for (t0, n) in tiles:
                        phase1(fl, sg, su, t0, n)
                    release(2)
                dslots = [consume((tag + "d", f)) for f in grp]
                last = gi == len(GROUPS) - 1
                if last:
                    S.op("act", lambda e: e.activation(out=rstq[:, 0:1], in_=epst[:], func=AF.Ln), reads=["eps"], writes=[("rstq", 0)])
                if last and hook is not None:
                    hook("last_p2_start", 0, 0)
                for ti, (t0, n) in enumerate(tiles):
                    for o in range(8):
                        b = (bankS() if hook is not None else bank6()) if last else bankM(True)

                        def mmd(e, b=b, o=o, t0=t0, n=n, dslots=dslots):
                            for fl, s in enumerate(dslots):
                                inst = e.matmul(ps[b][:, 0:n], lhsT=ring[:, s, o * 128:(o + 1) * 128],
                                                rhs=aT[:, fl, t0:t0 + n], start=(fl == 0), stop=(fl == len(dslots) - 1))
                            return inst
                        rk = [("ring", s) for s in dslots]
                        for fl in range(len(dslots)):
                            rk += blk_keys("a", t0, n, fl)
                        S.op("pe", mmd, reads=rk, writes=[("ps", b)])
                        S.op("dve", lambda e, b=b, o=o, t0=t0, n=n: e.scalar_tensor_tensor(
                            out=xT[:, o, t0:t0 + n], in0=ps[b][:, 0:n], scalar=0.5, in1=xT[:, o, t0:t0 + n],
                            op0=ALU.mult, op1=ALU.add),
                            reads=[("ps", b)] + blk_keys("x", t0, n), writes=blk_keys("x", t0, n) + blk_keys("xr", t0, n, o))
                        drain_dve(2 if (last and hook is None and ti == nt - 1 and 1 <= o <= 4) else 1)
                        if last and hook is not None:
                            hook("last_p2_group", ti, o)
                        if last and ti >= 1 and o == 0:
                            doneB(ti - 1)
                        if last:
                            sqrow(ti, o)
                if last and done_last_inside:
                    doneB(nt - 1)
                    drain_dve(1000)
                release(len(dslots))

        def proj(slot, t0, n, bank):
            flush_for(t0, n)

            def mm(e):
                for kc in range(8):
                    inst = e.matmul(ps[bank][:, 0:n], lhsT=ring[:, slot, kc * 128:(kc + 1) * 128],
                                    rhs=hT[:, kc, t0:t0 + n], start=(kc == 0), stop=(kc == 7))
                return inst
            S.op("pe", mm, reads=hkeys(t0, n) + [("ring", slot)], writes=[("ps", bank)])

        def mixer(st, tiles_all, tiles_own, pendB_last):
            g0 = ST_G0[st]
            nblk = ST_LEN[st] // 128
            gb0 = g0 // 128
            def own_part(tile):
                a, m = tile
                if st == 0 and a < HALO:
                    return (HALO, a + m - HALO)
                return tile

            while pend_low and not pend_dve:
                pend_low.pop(0)()
            s_k = consume(("k", 0))
            s_v = consume(("v", 0))
            s_q = [consume(("q", c)) for c in range(4)]
            for ti, (t0, n) in enumerate(tiles_all):
                b = bankS()
                proj(s_k, t0, n, b)
                if ti == 0 and len(tiles_all) == 2 and pendB_last is not None:
                    pendB_last()
                    pendB_last = None
                S.op("act", lambda e, b=b, t0=t0, n=n: e.activation(out=kT[:, g0 + t0:g0 + t0 + n], in_=ps[b][:, 0:n], func=AF.Copy),
                     reads=[("ps", b)], writes=blk_keys("k", g0 + t0, n))
                lb0 = t0 // 128
                nb = n // 128
                b = bankS()

                def mmv(e, b=b, lb0=lb0, nb=nb):
                    for j in range(nb):
                        c0 = (lb0 + j) * 128
                        for kc in range(8):
                            inst = e.matmul(ps[b][:, j * 128:(j + 1) * 128], lhsT=hT[:, kc, c0:c0 + 128],
                                            rhs=ring[:, s_v, kc * 128:(kc + 1) * 128], start=(kc == 0), stop=(kc == 7))
                    return inst
                S.op("pe", mmv, reads=hkeys(t0, n) + [("ring", s_v)], writes=[("ps", b)])
                S.op("dve", lambda e, b=b, lb0=lb0, nb=nb: e.tensor_copy(
                    out=vaug[:, gb0 + lb0:gb0 + lb0 + nb, :, 0:64],
                    in_=ps[b][:, 0:nb * 128].rearrange("p (b k d) -> p b k d", b=nb, k=2, d=64)),
                    reads=[("ps", b), "vaug_init", "vaug_halo"], writes=[("v", gb0 + lb0 + j) for j in range(nb)])
                tq, nq = own_part((t0, n))
                for c in range(4):
                    b = bankS()
                    proj(s_q[c], tq, nq, b)
                    S.op("act", lambda e, b=b, tq=tq, nq=nq, c=c: e.activation(out=qT[:, c, tq:tq + nq], in_=ps[b][:, 0:nq], func=AF.Copy),
                         reads=[("ps", b)], writes=blk_keys("q", tq, nq, c))
                    drain_dve(2)
                if ti == 0 and pendB_last is not None:
                    pendB_last()
            release(6)

            cwide = [False]

            def cbank(pool):
                if cwide[0]:
                    return bank6()
                return bankM() if pool == "M" else bankS()

            def conv_uc(i, t0, n, slots_i):
                s_u, s_c, s_b = slots_i
                ub = state["uS"]
                state["uS"] ^= 1
                b1 = cbank("M")
                proj(s_u, t0, n, b1)
                S.op("act", lambda e: e.activation(out=uS[:, ub, 0:n], in_=ps[b1][:, 0:n], func=AF.Copy),
                     reads=[("ps", b1)], writes=[("uS", ub)])
                b2 = cbank("S")
                proj(s_c, t0, n, b2)
                S.op("dve", lambda e: e.tensor_tensor(out=cuT[:, i, 2 + g0 + t0:2 + g0 + t0 + n], in0=ps[b2][:, 0:n],
                                                      in1=uS[:, ub, 0:n], op=ALU.mult),
                     reads=[("ps", b2), ("uS", ub), "cu_pad"], writes=blk_keys("cu", g0 + t0, n, i))

            def conv_by(i, t0, n, slots_i):
                s_u, s_c, s_b = slots_i
                bb = state["bS_"]
                state["bS_"] ^= 1
                b1 = cbank("M")
                proj(s_b, t0, n, b1)
                S.op("act", lambda e: e.activation(out=bS[:, bb, 0:n], in_=ps[b1][:, 0:n], func=AF.Copy),
                     reads=[("ps", b1)], writes=[("bS", bb)])
                by = cbank("S")

                def mmy(e):
                    for r in range(3):
                        c0 = g0 + t0 + r
                        inst = e.matmul(ps[by][:, 0:n], lhsT=diag[:, i * 3 + r, :], rhs=cuT[:, i, c0:c0 + n],
                                        start=(r == 0), stop=(r == 2))
                    return inst
                rk = ["diag", "cu_pad"] + blk_keys("cu", max(g0 + t0 - 2, 0), n + 2, i)
                S.op("pe", mmy, reads=rk, writes=[("ps", by)])
                S.op("dve", lambda e: e.tensor_tensor(out=convT[:, i, t0:t0 + n], in0=ps[by][:, 0:n], in1=bS[:, bb, 0:n], op=ALU.mult),
                     reads=[("ps", by), ("bS", bb)], writes=blk_keys("conv", t0, n, i))

            cn_bank = {}

            def conv_norm_A(t0, n):
                ck = []
                for i in range(4):
                    ck += blk_keys("conv", t0, n, i)
                flush_for(t0, n)
                hk = hkeys(t0, n, range(4))
                S.op("act", lambda e: e.activation(out=hT[:, 0:4, t0:t0 + n], in_=convT[:, :, t0:t0 + n], func=AF.Square),
                     reads=ck, writes=hk)

            def conv_norm_B(t0, n):
                ck = []
                for i in range(4):
                    ck += blk_keys("conv", t0, n, i)
                hk = hkeys(t0, n, range(4))
                b = bankN()

                def mmn(e):
                    for i in range(4):
                        inst = e.matmul(ps[b][:, 0:n], lhsT=ones[:], rhs=hT[:, i, t0:t0 + n], start=(i == 0), stop=(i == 3))
                    return inst
                S.op("pe", mmn, reads=hk + ["ones"], writes=[("ps", b)])
                S.op("act", lambda e: e.activation(out=ps[b][:, 0:n], in_=ps[b][:, 0:n], func=AF.Ln, scale=1.0 / 512, bias=epst[:]),
                     reads=[("ps", b), "eps"], writes=[("ps", b)])
                S.op("act", lambda e: e.activation(out=ps[b][:, 0:n], in_=ps[b][:, 0:n], func=AF.Exp, scale=-0.5),
                     reads=[("ps", b)], writes=[("ps", b)])

                def cmul(e):
                    for i in range(4):
                        col = G_CONVN + i
                        inst = e.scalar_tensor_tensor(out=aT[:, 4 + i, t0:t0 + n], in0=convT[:, i, t0:t0 + n],
                                                      scalar=gains[:, col:col + 1], in1=ps[b][:, 0:n],
                                                      op0=ALU.mult, op1=ALU.mult)
                    return inst
                mk = []
                for i in range(4):
                    mk += blk_keys("mix", t0, n, 4 + i)
                S.op("dve", cmul, reads=ck + [("ps", b), "gains"], writes=mk)

            own_lb0 = tiles_own[0][0] // 128
            blocks = list(range(own_lb0, nblk))
            pbs = {}

            def att_A(lbq):
                gb = gb0 + lbq
                pb = state["att"] % 2
                state["att"] += 1
                pbs[lbq] = pb
                q0 = lbq * 128
                for jc in range(2):
                    ba = state["bS"] & 2
                    state["bS"] = (ba + 2) % 4
                    kcol = (gb - 1 + jc) * 128

                    def mms(e, jc=jc, ba=ba, kcol=kcol):
                        for kv in range(2):
                            b = ba + kv
                            e.matmul(ps[b].rearrange("p (c q) -> p c q", c=4), lhsT=kT[kv * 64:(kv + 1) * 64, kcol:kcol + 128],
                                     rhs=qT[kv * 64:(kv + 1) * 64, 0:4, q0:q0 + 128], start=True, stop=False)
                        for kv in range(2):
                            b = ba + kv
                            e.matmul(ps[b], lhsT=identb[:], rhs=BH[:, jc, kv * 512:(kv + 1) * 512], start=False, stop=False)
                            inst = e.matmul(ps[b], lhsT=identb[:], rhs=BL[:, jc, kv * 512:(kv + 1) * 512], start=False, stop=True)
                        return inst
                    S.op("pe", mms, reads=blk_keys("k", kcol, 128) + [("q", c, lbq) for c in range(4)] + ["BH", "BL", "identb"],
                         writes=[("ps", ba), ("ps", ba + 1)])
                    S.op("act", lambda e, jc=jc, ba=ba: e.activation(out=pT[:, pb, 2 * jc:2 * jc + 2, :], in_=psall[:, ba:ba + 2, :],
                                                                     func=AF.Exp, scale=0.125),
                         reads=[("ps", ba), ("ps", ba + 1)], writes=[("pTs", pb, jc)])

            def att_B(lbq):
                gb = gb0 + lbq
                pb = pbs[lbq]
                for kv in range(2):
                    def mmpv(e, kv=kv):
                        for c in range(4):
                            for jc in range(2):
                                inst = e.matmul(ps[6 + kv][:, c * 65:(c + 1) * 65],
                                                lhsT=pT[:, pb, jc * 2 + kv, c * 128:(c + 1) * 128],
                                                rhs=vaug[:, gb - 1 + jc, kv, 0:65], start=(jc == 0), stop=(jc == 1))
                        return inst
                    S.op("pe", mmpv, reads=[("pTs", pb, 0), ("pTs", pb, 1), ("v", gb - 1), ("v", gb), "vaug_halo"],
                         writes=[("ps", 6 + kv)])

                def dens(e):
                    for kv in range(2):
                        den = ps[6 + kv][:, 0:260].rearrange("p (c e) -> p c e", e=65)[:, :, 64:65]
                        inst = e.tensor_tensor(out=rden[:, pb, kv * 4:(kv + 1) * 4, :], in0=den, in1=esink[:, kv * 4:(kv + 1) * 4, :], op=ALU.add)
                    return inst
                S.op("dve", dens, reads=[("ps", 6), ("ps", 7), "esink"], writes=[("rden", pb)])
                S.op("dve", lambda e: e.reciprocal(out=rden[:, pb], in_=rden[:, pb]), reads=[("rden", pb)], writes=[("rden", pb)])

                def normz(e):
                    for kv in range(2):
                        pvv = ps[6 + kv][:, 0:260].rearrange("p (c e) -> p c e", e=65)[:, :, 0:64]
                        inst = e.tensor_tensor(
                            out=atmp[:, pb, kv * 256:(kv + 1) * 256].rearrange("p (c d) -> p c d", d=64),
                            in0=pvv, in1=rden[:, pb, kv * 4:(kv + 1) * 4, :].broadcast_to([128, 4, 64]), op=ALU.mult)
                    return inst
                S.op("dve", normz, reads=[("ps", 6), ("ps", 7), ("rden", pb)], writes=[("atmp", pb)])
                S.op("act", lambda e: e.activation(out=anf[:, pb, :], in_=atmp[:, pb, :], func=AF.Square, accum_out=ssq[:, pb:pb + 1]),
                     reads=[("atmp", pb)], writes=[("anf", pb), ("ssq", pb)])
                S.op("act", lambda e: e.activation(out=rstq[:, pb:pb + 1], in_=ssq[:, pb:pb + 1], func=AF.Ln, scale=1.0 / 512, bias=epst[:]),
                     reads=[("ssq", pb), "eps"], writes=[("rstq", pb)])
                S.op("act", lambda e: e.activation(out=rstq[:, pb:pb + 1], in_=rstq[:, pb:pb + 1], func=AF.Exp, scale=-0.5),
                     reads=[("rstq", pb)], writes=[("rstq", pb)])
                S.op("dve", lambda e: e.scalar_tensor_tensor(
                    out=anb[:, pb, :], in0=atmp[:, pb, :], scalar=rstq[:, pb:pb + 1], in1=gattn[:], op0=ALU.mult, op1=ALU.mult),
                    reads=[("atmp", pb), ("rstq", pb), "gattn"], writes=[("anb", pb)])

            def att_C(lbq):
                pb = pbs[lbq]
                q0 = lbq * 128
                b = bankM()

                def mmt(e):
                    for c in range(4):
                        inst = e.matmul(ps[b][:, c * 128:(c + 1) * 128], lhsT=anb[:, pb, c * 128:(c + 1) * 128], rhs=identb[:],
                                        start=True, stop=True)
                    return inst
                S.op("pe", mmt, reads=[("anb", pb), "identb"], writes=[("ps", b)])
                S.op("dve", lambda e: e.tensor_copy(
                    out=aT[:, 0:4, q0:q0 + 128], in_=ps[b].rearrange("p (c q) -> p c q", c=4)),
                    reads=[("ps", b)], writes=[("mix", c, lbq) for c in range(4)])

            conv_units = []
            cslots = {}

            def cunit_uc(i, t0, n, first):
                if first:
                    cslots[i] = (consume(("cu", i)), consume(("cc", i)), consume(("cb", i)))
                conv_uc(i, t0, n, cslots[i])

            def cunit_by(i, t0, n, last):
                conv_by(i, t0, n, cslots[i])
                if last:
                    release(3)
            for i in range(4):
                for ti, (t0, n) in enumerate(tiles_all):
                    if st == 0 and t0 < HALO:
                        t0, n = HALO - 2, t0 + n - (HALO - 2)
                    conv_units.append(lambda i=i, t0=t0, n=n, f=(ti == 0): cunit_uc(i, t0, n, f))
                for ti, tile in enumerate(tiles_all):
                    t0, n = own_part(tile)
                    conv_units.append(lambda i=i, t0=t0, n=n, l=(ti == len(tiles_all) - 1): cunit_by(i, t0, n, l))

            oslots = []

            wbank = [0]
            wstate = {}
            wsq = []

            def wout_half(ti, o, half):
                if not oslots:
                    oslots.extend(consume(("o", oo)) for oo in range(8))
                t0, n = tiles_own[ti]
                if half == 0:
                    b = wbank[0]
                    wbank[0] = (b + 1) % 6
                    wstate[(ti, o)] = b
                b = wstate[(ti, o)]
                kcs = range(4) if half == 0 else range(4, 8)
                mk = []
                for c in kcs:
                    mk += blk_keys("mix", t0, n, c)

                def mmo(e):
                    for kc in kcs:
                        inst = e.matmul(ps[b][:, 0:n], lhsT=ring[:, oslots[o], kc * 128:(kc + 1) * 128],
                                        rhs=aT[:, kc, t0:t0 + n], start=(kc == 0), stop=(kc == 7))
                    return inst
                S.op("pe", mmo, reads=mk + [("ring", oslots[o])], writes=[("ps", b)])
                if half == 1:
                    S.op("dve", lambda e: e.tensor_tensor(out=xT[:, o, t0:t0 + n], in0=ps[b][:, 0:n], in1=xT[:, o, t0:t0 + n], op=ALU.add),
                         reads=[("ps", b)] + blk_keys("x", t0, n), writes=blk_keys("x", t0, n) + blk_keys("xr", t0, n, o))
                    drain_dve(1)
                    wsq.append((ti, o))

            def wout_group(ti, o):
                wout_half(ti, o, 0)
                wout_half(ti, o, 1)

            nb_ = len(blocks)
            nsteps = nb_ + 2
            ncu = len(conv_units)
            for step in range(nsteps):
                if step < nb_:
                    att_A(blocks[step])
                else:
                    cwide[0] = True
                for _ in range(((step + 1) * ncu) // nsteps - (step * ncu) // nsteps):
                    if conv_units:
                        conv_units.pop(0)()
                if 1 <= step < nb_ + 1:
                    att_B(blocks[step - 1])
                if 2 <= step:
                    att_C(blocks[step - 2])
            while conv_units:
                conv_units.pop(0)()
            conv_norm_A(*tiles_own[0])
            for (t0, n) in tiles_own[1:]:
                conv_norm_A(t0, n)
            for o in range(0, 3):
                wout_half(0, o, 0)
            conv_norm_B(*tiles_own[0])
            for o in range(3, 6):
                wout_half(0, o, 0)
            for (t0, n) in tiles_own[1:]:
                conv_norm_B(t0, n)
            def wsq_flush():
                while wsq:
                    ti_, o_ = wsq.pop(0)
                    norm_sq_row(tiles_own[ti_], o_)
            for o in range(0, 6):
                wout_half(0, o, 1)
            for o in range(6, 8):
                wout_group(0, o)
            wsq_flush()
            for ti in range(1, len(tiles_own)):
                for o in range(8):
                    wout_group(ti, o)
                    if o == 0:
                        norm_B(tiles_own[ti - 1], G_FFN2)
                    wsq_flush()
                    if o >= 1:
                        drain_dve(1)
            release(8)
            del mixer_end[:]
            mixer_end.extend((S.engs[nm]["sem"], S.engs[nm]["count"], nm) for nm in ("pe", "act", "dve") if S.engs[nm]["count"] > 0)

        load_x_first(TILES_ALL[0][0])
        setup()
        S.op("act", lambda e: e.activation(out=rstq[:, 0:1], in_=epst[:], func=AF.Ln), reads=["eps"], writes=[("rstq", 0)])
        pump()
        for i, tile in enumerate(TILES_ALL[0]):
            if i >= 1:
                load_x_tile(0, i, tile)
        for st in range(2):
            tiles_all = TILES_ALL[st]
            tiles_own = TILES_OWN[st]
            if st == 0:
                norm_h_tile(tiles_all[0], G_FFN1)
                setup2()
                pendA = [(lambda tile=tile: norm_A(tile)) for tile in tiles_all[1:]]
                pendB = [(lambda tile=tile: norm_B(tile, G_FFN1)) for tile in tiles_all[1:]]
            else:
                while pend_dve:
                    pend_dve.pop(0)[1]()
                pendA = [None]
                pendB = [deferred_final[0]]
            ffn("1", tiles_all, pendA, pendB,
                lambda ti, o: norm_sq_row(tiles_all[ti], o), lambda ti: norm_B(tiles_all[ti], G_MIX), False)
            mixer(st, tiles_all, tiles_own, lambda: norm_B(tiles_all[-1], G_MIX))

            def finB(ti, st=st, tiles_own=tiles_own):
                if st == 0:
                    final_norm_B(st, tiles_own[ti], (lambda: load_x_tile(1, ti, TILES_ALL[1][ti])), 0, 4 + ti)
                else:
                    final_norm_B(st, tiles_own[ti], None)

            def hook(ev, ti, o):
                if ev == "last_group_start":
                    pre_load(0)
                    pre_load(1)
                elif ev == "last_p2_start":
                    pre_norm_A(0)
                    pre_norm_A(1)
                elif ev == "last_p2_group" and ti == 0 and o == 3:
                    pre_norm_B(0)
                elif ev == "last_p2_group" and ti == 0 and o == 7:
                    pre_norm_B(1)
            if st == 0:
                ffn("2", tiles_own, [None], [lambda: norm_B(tiles_own[-1], G_FFN2)],
                    lambda ti, o, tiles_own=tiles_own: final_sq_row(tiles_own[ti], o), finB, False, hook)
                deferred_final = [lambda finB=finB, k=len(tiles_own) - 1: finB(k)]
            else:
                t0l, nl = tiles_own[-1]
                halves = [(t0l, nl // 2), (t0l + nl // 2, nl // 2)]

                def doneB2(ti, tiles_own=tiles_own):
                    if ti < len(tiles_own) - 1:
                        final_norm_B(1, tiles_own[ti])
                    else:
                        drain_dve(1000)
                        final_norm_B(1, halves[0], None, 0)
                        drain_dve(1000)
                        final_norm_B(1, halves[1], None, nl // 2)
                ffn("2", tiles_own, [None], [lambda: norm_B(tiles_own[-1], G_FFN2)],
                    lambda ti, o, tiles_own=tiles_own: final_sq_row(tiles_own[ti], o), doneB2, True, None)
                drain_dve(1000)
        assert state["consumed"] == total_chunks and state["issued"] == total_chunks

        with nc.Block() as block:
            S.emit(block, list(last_out.values()))
    return nc


def _chunk_cols(W, cols):
    sub = np.ascontiguousarray(W[:, cols])
    return sub.reshape(8, 128, 128).transpose(1, 0, 2).reshape(128, 1024)


def _t5_bucket(n):
    n = np.maximum(n, 0)
    max_exact = 16
    large = max_exact + (np.log(np.maximum(n, 1).astype(np.float32) / np.float32(max_exact))
                         / np.float32(math.log(128 / max_exact)) * np.float32(32 - max_exact)).astype(np.int32)
    large = np.minimum(large, 31)
    return np.where(n < max_exact, n, large)


def kernel(x, rel_bias_table, ffn1_norm, ffn1_w_gate, ffn1_w_up, ffn1_w_down, mix_norm, w_in, conv_w, attn_sinks,
           attn_out_norm, conv_out_norm, w_out, ffn2_norm, ffn2_w_gate, ffn2_w_up, ffn2_w_down, final_norm):
    f32 = np.float32
    x = np.asarray(x, f32)
    order = stream_order()
    ws = np.empty((NCHUNK, 128, 1024), f32)
    W = {"1g": np.asarray(ffn1_w_gate, f32)[0], "1u": np.asarray(ffn1_w_up, f32)[0], "1d": np.asarray(ffn1_w_down, f32)[0],
         "2g": np.asarray(ffn2_w_gate, f32)[0], "2u": np.asarray(ffn2_w_up, f32)[0], "2d": np.asarray(ffn2_w_down, f32)[0]}
    win = np.asarray(w_in, f32)[0]
    wo = np.asarray(w_out, f32)[0]
    ar = np.arange
    for n, (kind, idx) in enumerate(order):
        if kind in ("1g", "1u", "2g", "2u"):
            ws[n] = _chunk_cols(W[kind], ar(idx * 128, (idx + 1) * 128))
        elif kind in ("1d", "2d"):
            ws[n] = W[kind][idx * 128:(idx + 1) * 128, :]
        elif kind == "q":
            cols = np.concatenate([ar(idx * 64, (idx + 1) * 64), ar((4 + idx) * 64, (5 + idx) * 64)])
            ws[n] = _chunk_cols(win, cols)
        elif kind == "k":
            ws[n] = _chunk_cols(win, ar(512, 640))
        elif kind == "v":
            ws[n] = _chunk_cols(win, ar(640, 768))
        elif kind == "cu":
            ws[n] = _chunk_cols(win, ar(768 + idx * 128, 768 + (idx + 1) * 128))
        elif kind == "cb":
            ws[n] = _chunk_cols(win, ar(1280 + idx * 128, 1280 + (idx + 1) * 128))
        elif kind == "cc":
            ws[n] = _chunk_cols(win, ar(1792 + idx * 128, 1792 + (idx + 1) * 128))
        elif kind == "o":
            ws[n] = _chunk_cols(wo, ar(idx * 128, (idx + 1) * 128))
        else:
            raise AssertionError(kind)
    gains = np.zeros((128, NGCOL), f32)

    def cols(v, n):
        return np.asarray(v, f32).reshape(n, 128).T
    gains[:, G_FFN1:G_FFN1 + 8] = cols(np.asarray(ffn1_norm)[0], 8)
    gains[:, G_MIX:G_MIX + 8] = cols(np.asarray(mix_norm)[0], 8)
    gains[:, G_FFN2:G_FFN2 + 8] = cols(np.asarray(ffn2_norm)[0], 8)
    gains[:, G_FINAL:G_FINAL + 8] = cols(np.asarray(final_norm), 8)
    gains[:, G_CONVN:G_CONVN + 4] = cols(np.asarray(conv_out_norm)[0], 4)
    cw = np.asarray(conv_w, f32)[0]
    for r in range(3):
        gains[:, G_CONVW + r * 4:G_CONVW + r * 4 + 4] = cols(cw[r], 4)
    gattn = np.ascontiguousarray(np.broadcast_to(np.asarray(attn_out_norm, f32)[0][None, :], (128, 512)))
    sinks = np.ascontiguousarray(np.broadcast_to(np.asarray(attn_sinks, f32)[0][None, :], (128, 8)))
    j = np.arange(128)[:, None]
    q = np.arange(128)[None, :]
    tbl = np.asarray(rel_bias_table, f32)
    biasT = np.empty((128, 2, 8, 128), f32)
    maskT = np.empty((128, 2, 8, 128), f32)
    for jc in range(2):
        dist = q + 128 - (jc * 128 + j)
        valid = (dist >= 0) & (dist < 128)
        bk = _t5_bucket(dist)
        g = tbl[bk]
        biasT[:, jc] = g.transpose(0, 2, 1)
        maskT[:, jc] = np.where(valid[:, None, :], f32(0.0), f32(-240000.0))
    biasT = biasT.reshape(128, 2, 1024)
    maskT = maskT.reshape(128, 2, 1024)
    ident = np.eye(128, dtype=f32)
    in_maps = []
    for c in range(NCORE):
        b, s = divmod(c, 4)
        own = x[b, s * SEG:(s + 1) * SEG]
        if s == 0:
            hal = np.zeros((HALO, D), f32)
        else:
            hal = x[b, s * SEG - HALO:s * SEG]
        xc = np.concatenate([hal, own], axis=0)
        xTc = np.ascontiguousarray(xc.T.reshape(8, 128, NTOK).transpose(1, 0, 2))
        in_maps.append({
            "xT": xTc, "ws": ws, "gains": gains, "gattn": gattn, "sinks": sinks,
            "halo_ok": np.full((128, 1), 0.0 if s == 0 else 1.0, f32),
            "maskT": maskT, "biasT": biasT, "ident": ident,
        })
    nc = build_program()
    res = run_bass_kernel_spmd(nc, in_maps, core_ids=list(range(NCORE)))
    out = np.empty((2, SEQ, D), f32)
    for c in range(NCORE):
        b, s = divmod(c, 4)
        oT = res.results[c]["outT"]
        out[b, s * SEG:(s + 1) * SEG] = oT.transpose(1, 0, 2).reshape(D, SEG).T
    return out
```

```python
import math
import numpy as np
import concourse.bass as bass
import concourse.mybir as mybir
from concourse.bass_utils import run_bass_kernel_spmd
from contextlib import ExitStack

F32 = mybir.dt.float32
BF16 = mybir.dt.bfloat16
AF = mybir.ActivationFunctionType
ALU = mybir.AluOpType

D = 1024
DFF = 2816
NF = DFF // 128
SEQ = 8192
NCORE = 8
SEG = 2048
HALO = 128
NTOK = SEG + HALO
ST_LEN = (1152, 1024)
ST_G0 = (0, 1152)
GROUPS = [list(range(0, 7)), list(range(7, 14)), list(range(14, 22))]
NS = 13
EPS = 1e-6
G_FFN1, G_MIX, G_FFN2, G_FINAL, G_CONVN, G_CONVW = 0, 8, 16, 24, 32, 36
NGCOL = 48


def stream_order():
    order = []

    def ffn(tag):
        for grp in GROUPS:
            for f in grp:
                order.append((tag + "g", f))
                order.append((tag + "u", f))
            for f in grp:
                order.append((tag + "d", f))
    ffn("1")
    order.append(("k", 0))
    order.append(("v", 0))
    for c in range(4):
        order.append(("q", c))
    for i in range(4):
        order.append(("cu", i))
        order.append(("cc", i))
        order.append(("cb", i))
    for o in range(8):
        order.append(("o", o))
    ffn("2")
    return order


NCHUNK = len(stream_order())


class Res:
    __slots__ = ("w", "r")

    def __init__(self):
        self.w = None
        self.r = []


class Sched:
    ENG = ("pe", "act", "dve", "pool", "sp")

    def __init__(self, nc, es):
        self.nc = nc
        self.es = es
        self.engs = {}
        for name in self.ENG:
            sem = es.enter_context(nc.semaphore("s_" + name))
            self.engs[name] = dict(sem=sem, count=0, ops=[], waited={})
        self.res = {}

    def dma_sem(self, name):
        return dict(sem=self.es.enter_context(self.nc.semaphore(name)), count=0)

    def R(self, key):
        r = self.res.get(key)
        if r is None:
            r = self.res[key] = Res()
        return r

    def op(self, eng, fn, reads=(), writes=(), dma=None, extra=()):
        E = self.engs[eng]
        need = []
        for k in reads:
            r = self.R(k)
            if r.w is not None:
                need.append(r.w)
        for k in writes:
            r = self.R(k)
            if r.w is not None:
                need.append(r.w)
            need.extend(r.r)
        need.extend(extra)
        if dma is not None:
            dma["count"] += 16
            tok = (dma["sem"], dma["count"], None)
        else:
            E["count"] += 1
            tok = (E["sem"], E["count"], eng)
        mx = {}
        for (sem, val, src) in need:
            if src == "pe" and eng == "pe" and dma is None:
                continue
            k = id(sem)
            if k not in mx or mx[k][1] < val:
                mx[k] = (sem, val)
        waits = []
        for k, (sem, val) in mx.items():
            if E["waited"].get(k, 0) >= val:
                continue
            E["waited"][k] = val
            waits.append((sem, val))
        E["ops"].append((waits, fn, tok))
        for k in reads:
            self.R(k).r.append(tok)
        for k in writes:
            r = self.R(k)
            r.w = tok
            r.r = []
        return tok

    def emit(self, block, final_waits=()):
        def runner(name):
            def body(e):
                for (waits, fn, tok) in self.engs[name]["ops"]:
                    for (sem, val) in waits:
                        e.wait_ge(sem, val)
                    inst = fn(e)
                    inst.then_inc(tok[0], 16 if tok[2] is None else 1)
                if name == "sp":
                    for (sem, val, _) in final_waits:
                        e.wait_ge(sem, val)
            return body

        block.tensor(runner("pe"))
        block.scalar(runner("act"))
        block.vector(runner("dve"))
        block.gpsimd(runner("pool"))
        block.sync(runner("sp"))


def hkeys(t0, n, kcs=range(8)):
    return [("h", kc, b) for kc in kcs for b in range(t0 // 128, (t0 + n + 127) // 128)]


def blk_keys(kind, t0, n, *extra):
    return [(kind,) + tuple(extra) + (b,) for b in range(t0 // 128, (t0 + n + 127) // 128)]


def build_program():
    nc = bass.Bass("TRN2", target_bir_lowering=False)
    xT_d = nc.dram_tensor("xT", [128, 8, NTOK], F32, kind="ExternalInput").ap()
    ws_d = nc.dram_tensor("ws", [NCHUNK, 128, 1024], F32, kind="ExternalInput").ap()
    gains_d = nc.dram_tensor("gains", [128, NGCOL], F32, kind="ExternalInput").ap()
    gattn_d = nc.dram_tensor("gattn", [128, 512], F32, kind="ExternalInput").ap()
    sinks_d = nc.dram_tensor("sinks", [128, 8], F32, kind="ExternalInput").ap()
    halo_d = nc.dram_tensor("halo_ok", [128, 1], F32, kind="ExternalInput").ap()
    mask_d = nc.dram_tensor("maskT", [128, 2, 1024], F32, kind="ExternalInput").ap()
    bias_d = nc.dram_tensor("biasT", [128, 2, 1024], F32, kind="ExternalInput").ap()
    ident_d = nc.dram_tensor("ident", [128, 128], F32, kind="ExternalInput").ap()
    out_d = nc.dram_tensor("outT", [128, 8, SEG], F32, kind="ExternalOutput").ap()

    with ExitStack() as es:
        def sb(name, shape, dt):
            return es.enter_context(nc.sbuf_tensor(name, shape, dt))

        xT = sb("xT_sb", [128, 8, 1152], F32)
        hT = sb("hT_sb", [128, 8, 1152], BF16)
        aT = sb("aT_sb", [128, 8, 1152], BF16)
        ring = sb("ring_sb", [128, NS, 1024], BF16)
        qT = sb("qT_sb", [128, 4, 1152], BF16)
        kT = sb("kT_sb", [128, 18 * 128], BF16)
        vaug = sb("vaug_sb", [128, 18, 2, 66], BF16)
        cuT = sb("cuT_sb", [128, 4, 2 + 18 * 128], BF16)
        uS = sb("uS_sb", [128, 2, 512], F32)
        bS = sb("bS_sb", [128, 2, 512], F32)
        convT = sb("convT_sb", [128, 4, 1152], F32)
        pT = sb("pT_sb", [128, 2, 4, 512], BF16)
        atmp = sb("atmp_sb", [128, 2, 512], F32)
        anf = sb("anf_sb", [128, 2, 512], F32)
        anb = sb("anb_sb", [128, 2, 512], BF16)
        EB = sb("EB_sb", [128, 2, 1024], F32)
        sS = sb("sS_sb", [128, 2, 512], F32)
        identf = sb("identf_sb", [128, 128], F32)
        ones = sb("ones_sb", [128, 128], BF16)
        diag = sb("diag_sb", [128, 12, 128], BF16)
        gains = sb("gains_sb", [128, NGCOL], F32)
        gattn = sb("gattn_sb", [128, 512], F32)
        esink = sb("esink_sb", [128, 8, 1], F32)
        rden = sb("rden_sb", [128, 2, 8, 1], F32)
        halo = sb("halo_sb", [128, 1], F32)
        epst = sb("eps_sb", [128, 1], F32)
        ssq = sb("ssq_sb", [128, 2], F32)
        rstq = sb("rstq_sb", [128, 2], F32)
        psall = es.enter_context(nc.psum_tensor("psall", [128, 8, 512], F32))
        ps = [psall[:, i, :] for i in range(8)]
        BH = sb("BH_sb", [128, 2, 1024], BF16)
        BL = sb("BL_sb", [128, 2, 1024], BF16)
        identb = sb("identb_sb", [128, 128], BF16)

        S = Sched(nc, es)
        ringsem = [S.dma_sem("rg%d" % i) for i in range(NS)]
        xsem = [S.dma_sem("xl%d" % i) for i in range(4)]
        osem = [S.dma_sem("os%d" % i) for i in range(2)]
        last_out = {}
        setup_sems = {n: S.dma_sem("su_" + n) for n in ("gains", "gattn", "sinks", "halo", "mask", "mask2", "bias", "ident")}

        TILES_ALL = ([(0, 384), (384, 384), (768, 384)], [(0, 512), (512, 512)])
        TILES_OWN = ([(128, 512), (640, 512)], [(0, 512), (512, 512)])
        state = dict(xtok=[], bS=0, bM=0, bN=0, b6=0, issued=0, released=0, consumed=0, sS=0, uS=0, bS_=0, att=0)
        order = stream_order()
        total_chunks = 2 * NCHUNK

        def bankS():
            b = state["bS"]
            state["bS"] = (b + 1) % 4
            return b

        def bankM(wide=False):
            b = state["bM"] % 2
            state["bM"] = (b + 1) % 2
            return 4 + b

        def bank6():
            b = state["b6"]
            state["b6"] = (b + 1) % 6
            return b

        def bankN():
            b = state["bN"]
            state["bN"] ^= 1
            return 6 + b

        pend_dve = []
        pend_low = []

        def drain_dve(k=1):
            for _ in range(k):
                if pend_dve:
                    pend_dve.pop(0)[1]()
                elif pend_low:
                    pend_low.pop(0)()

        def flush_for(t0, n):
            lo, hi = t0, t0 + n
            while any((a < hi and lo < a + m_) for ((a, m_), _) in pend_dve):
                pend_dve.pop(0)[1]()

        def pump():
            while state["issued"] < total_chunks and state["issued"] < state["released"] + NS:
                n = state["issued"]
                slot = n % NS
                ci = n % NCHUNK
                S.op("pool", lambda e, slot=slot, ci=ci: e.dma_start(out=ring[:, slot, :], in_=ws_d[ci]),
                     writes=[("ring", slot)], dma=ringsem[slot], extra=state["xtok"])
                state["issued"] += 1

        def consume(expect):
            n = state["consumed"]
            assert order[n % NCHUNK] == expect, (order[n % NCHUNK], expect)
            state["consumed"] += 1
            return n % NS

        def release(k):
            state["released"] += k
            pump()

        def setup():
            S.op("act", lambda e: e.dma_start(out=gains[:], in_=gains_d), writes=["gains"], dma=setup_sems["gains"])
            S.op("act", lambda e: e.dma_start(out=identf[:], in_=ident_d), writes=["identf"], dma=setup_sems["ident"])
            S.op("act", lambda e: e.dma_start(out=gattn[:], in_=gattn_d), writes=["gattn"], dma=setup_sems["gattn"])
            S.op("act", lambda e: e.dma_start(out=esink[:, :, 0], in_=sinks_d), writes=["esink"], dma=setup_sems["sinks"])
            S.op("act", lambda e: e.dma_start(out=halo[:], in_=halo_d), writes=["halo"], dma=setup_sems["halo"])
            S.op("act", lambda e: e.dma_start(out=EB[:], in_=bias_d), writes=["EB"], dma=setup_sems["bias"])
            S.op("act", lambda e: e.dma_start(out=uS[:].rearrange("p a b -> p (a b)"), in_=mask_d[:, 0, :]),
                 writes=[("uS", 0), ("uS", 1)], dma=setup_sems["mask"])
            S.op("act", lambda e: e.dma_start(out=bS[:].rearrange("p a b -> p (a b)"), in_=mask_d[:, 1, :]),
                 writes=[("bS", 0), ("bS", 1)], dma=setup_sems["mask2"])
            S.op("pool", lambda e: e.memset(ones[:], 1.0), writes=["ones"])
            S.op("pool", lambda e: e.memset(epst[:], EPS), writes=["eps"])
            S.op("pool", lambda e: e.memset(vaug[:], 1.0), writes=["vaug_init"])
            S.op("pool", lambda e: e.memset(cuT[:, :, 0:2], 0.0), writes=["cu_pad"])

        def setup2():
            ops = []
            _Sop = S.op

            def defer(*a, **k):
                ops.append(lambda: _Sop(*a, **k))
            defer("act", lambda e: e.activation(out=esink[:], in_=esink[:], func=AF.Exp), reads=["esink"], writes=["esink"])
            defer("dve", lambda e: e.scalar_tensor_tensor(out=EB[:, 0, :], in0=EB[:, 0, :], scalar=8.0,
                                                         in1=uS[:].rearrange("p a b -> p (a b)"), op0=ALU.mult, op1=ALU.add),
                 reads=["EB", ("uS", 0), ("uS", 1)], writes=["EB"])
            defer("dve", lambda e: e.scalar_tensor_tensor(out=EB[:, 1, :], in0=EB[:, 1, :], scalar=8.0,
                                                         in1=bS[:].rearrange("p a b -> p (a b)"), op0=ALU.mult, op1=ALU.add),
                 reads=["EB", ("bS", 0), ("bS", 1)], writes=["EB"])
            defer("dve", lambda e: e.tensor_copy(out=BH[:], in_=EB[:]), reads=["EB"], writes=["BH"])
            defer("dve", lambda e: e.tensor_tensor(out=EB[:], in0=EB[:], in1=BH[:], op=ALU.subtract), reads=["EB", "BH"], writes=["EB"])
            defer("dve", lambda e: e.tensor_copy(out=BL[:], in_=EB[:]), reads=["EB"], writes=["BL"])
            defer("dve", lambda e: e.tensor_copy(out=identb[:], in_=identf[:]), reads=["identf"], writes=["identb"])

            def halo_cols(e):
                e.tensor_copy(out=vaug[:, 0, 0, 64:65], in_=halo[:, 0:1])
                return e.tensor_copy(out=vaug[:, 0, 1, 64:65], in_=halo[:, 0:1])
            defer("dve", halo_cols, reads=["halo", "vaug_init"], writes=["vaug_halo"])

            def mkdiag(e):
                inst = None
                for i in range(4):
                    for r in range(3):
                        col = G_CONVW + r * 4 + i
                        inst = e.tensor_scalar(out=diag[:, i * 3 + r, :], in0=identf[:], scalar1=gains[:, col:col + 1],
                                               scalar2=None, op0=ALU.mult)
                return inst
            defer("dve", mkdiag, reads=["identf", "gains"], writes=["diag"])
            pend_low.extend(ops)

        def load_x_first(tile):
            t0, n = tile
            tok = S.op("sp", lambda e: e.dma_start(out=xT[:, 0:4, t0:t0 + n], in_=xT_d[:, 0:4, t0:t0 + n]),
                       writes=blk_keys("x0a", t0, n), dma=xsem[0])
            S.op("act", lambda e: e.dma_start(out=xT[:, 4:8, t0:t0 + n], in_=xT_d[:, 4:8, t0:t0 + n]),
                 writes=blk_keys("x0b", t0, n), dma=xsem[3])
            state["xtok"] = [tok]

        def load_x_tile(st, i, tile, gate=False):
            g0 = ST_G0[st]
            t0, n = tile
            tok = S.op("sp", lambda e: e.dma_start(out=xT[:, :, t0:t0 + n], in_=xT_d[:, :, g0 + t0:g0 + t0 + n]),
                       writes=blk_keys("x", t0, n), dma=xsem[i])
            state["xtok"] = [tok] if gate else []

        def norm_A(tile):
            t0, n = tile
            S.op("act", lambda e: e.activation(out=hT[:, :, t0:t0 + n], in_=xT[:, :, t0:t0 + n], func=AF.Square),
                 reads=blk_keys("x", t0, n) + blk_keys("x0a", t0, n) + blk_keys("x0b", t0, n), writes=hkeys(t0, n))

        def norm_sq_row(tile, o):
            t0, n = tile
            S.op("act", lambda e: e.activation(out=hT[:, o, t0:t0 + n], in_=xT[:, o, t0:t0 + n], func=AF.Square),
                 reads=blk_keys("xr", t0, n, o), writes=hkeys(t0, n, [o]))

        def norm_B0(tile):
            t0, n = tile
            hk = hkeys(t0, n)
            b = bankN()

            def mm(e):
                for kc in range(8):
                    inst = e.matmul(ps[b][:, 0:n], lhsT=ones[:], rhs=hT[:, kc, t0:t0 + n], start=(kc == 0), stop=(kc == 7))
                return inst
            S.op("pe", mm, reads=hk + ["ones"], writes=[("ps", b)])
            S.op("act", lambda e: e.activation(out=ps[b][:, 0:n], in_=ps[b][:, 0:n], func=AF.Ln, scale=1.0 / D, bias=epst[:]),
                 reads=[("ps", b), "eps"], writes=[("ps", b)])
            S.op("act", lambda e: e.activation(out=ps[b][:, 0:n], in_=ps[b][:, 0:n], func=AF.Exp, scale=-0.5),
                 reads=[("ps", b)], writes=[("ps", b)])
            return b

        def norm_B(tile, gcol):
            t0, n = tile
            b = norm_B0(tile)

            def hmul(kc):
                S.op("dve", lambda e: e.scalar_tensor_tensor(out=hT[:, kc, t0:t0 + n], in0=xT[:, kc, t0:t0 + n],
                                                             scalar=gains[:, gcol + kc:gcol + kc + 1], in1=ps[b][:, 0:n],
                                                             op0=ALU.mult, op1=ALU.mult),
                     reads=blk_keys("x", t0, n) + [("ps", b), "gains"], writes=hkeys(t0, n, [kc]))
            for kc in range(8):
                pend_dve.append(((t0, n), lambda kc=kc: hmul(kc)))

        def norm_h_tile(tile, gcol):
            norm_A(tile)
            norm_B(tile, gcol)
            flush_for(*tile)

        finals = []
        tmp0 = convT[:].rearrange("p a b -> p (a b)")[:, 0:4096].rearrange("p (k n) -> p k n", k=8)
        tmp1a = EB[:].rearrange("p a b -> p (a b)").rearrange("p (k n) -> p k n", k=4)
        tmp1b = atmp[:]
        tmp1c = anf[:]
        sqs = qT[:].rearrange("p a b -> p (a b)")[:, 0:4096].rearrange("p (k n) -> p k n", k=8)

        def tmp_kc(ti, kc):
            if ti == 0:
                return tmp0[:, kc, :]
            if kc < 4:
                return tmp1a[:, kc, :]
            return tmp1b[:, kc - 4, :] if kc < 6 else tmp1c[:, kc - 6, :]

        def final_sq_row(tile, o):
            t0, n = tile
            S.op("act", lambda e: e.activation(out=sqs[:, o, 0:n], in_=xT[:, o, t0:t0 + n], func=AF.Square),
                 reads=blk_keys("xr", t0, n, o), writes=[("sqs", o)])

        def final_norm_B(st, tile, after=None, c0=0, bank=None):
            t0, n = tile
            if bank is None:
                while pend_dve:
                    pend_dve.pop(0)[1]()
                b = bankN()
            else:
                b = bank

            def mm(e):
                for kc in range(8):
                    inst = e.matmul(ps[b][:, 0:n], lhsT=ones[:], rhs=sqs[:, kc, c0:c0 + n], start=(kc == 0), stop=(kc == 7))
                return inst
            S.op("pe", mm, reads=[("sqs", k) for k in range(8)] + ["ones"], writes=[("ps", b)])
            S.op("act", lambda e: e.activation(out=ps[b][:, 0:n], in_=ps[b][:, 0:n], func=AF.Ln, scale=1.0 / D, bias=epst[:]),
                 reads=[("ps", b), "eps"], writes=[("ps", b)])
            S.op("act", lambda e: e.activation(out=ps[b][:, 0:n], in_=ps[b][:, 0:n], func=AF.Exp, scale=-0.5),
                 reads=[("ps", b)], writes=[("ps", b)])
            o0 = (t0 - HALO) if st == 0 else (1024 + t0)

            def fmul(kc):
                S.op("dve", lambda e: e.scalar_tensor_tensor(out=xT[:, kc, t0:t0 + n], in0=xT[:, kc, t0:t0 + n],
                                                             scalar=gains[:, G_FINAL + kc:G_FINAL + kc + 1], in1=ps[b][:, 0:n],
                                                             op0=ALU.mult, op1=ALU.mult),
                     reads=blk_keys("x", t0, n) + [("ps", b), "gains"], writes=blk_keys("xf", t0, n, kc))
                if kc == 3 or kc == 7:
                    lo = kc - 3
                    qi = 0 if kc == 3 else 1
                    rk = blk_keys("x", t0, n)
                    for k in range(lo, lo + 4):
                        rk += blk_keys("xf", t0, n, k)
                    t = S.op("sp", lambda e: e.dma_start(out=out_d[:, lo:lo + 4, o0:o0 + n], in_=xT[:, lo:lo + 4, t0:t0 + n]),
                             reads=rk, dma=osem[qi])
                    last_out[qi] = t
                    if kc == 7 and after is not None:
                        after()
            for kc in range(8):
                pend_low.append(lambda kc=kc: fmul(kc))

        mixer_end = []

        def pre_load(ti):
            g0 = ST_G0[1]
            t0, n = TILES_ALL[1][ti]
            if ti == 0:
                S.op("sp", lambda e: e.dma_start(out=tmp0, in_=xT_d[:, :, g0 + t0:g0 + t0 + n]),
                     writes=[("tmp", 0, 0), ("tmp", 0, 1), ("tmp", 0, 2)], dma=xsem[0], extra=mixer_end)
            else:
                S.op("sp", lambda e: e.dma_start(out=tmp1a, in_=xT_d[:, 0:4, g0 + t0:g0 + t0 + n]),
                     writes=[("tmp", 1, 0)], dma=xsem[1], extra=mixer_end)
                S.op("sp", lambda e: e.dma_start(out=tmp1b, in_=xT_d[:, 4:6, g0 + t0:g0 + t0 + n]),
                     writes=[("tmp", 1, 1)], dma=xsem[2], extra=mixer_end)
                S.op("sp", lambda e: e.dma_start(out=tmp1c, in_=xT_d[:, 6:8, g0 + t0:g0 + t0 + n]),
                     writes=[("tmp", 1, 2)], dma=xsem[3], extra=mixer_end)

        def pre_norm_A(ti):
            t0, n = TILES_ALL[1][ti]
            if ti == 0:
                S.op("act", lambda e: e.activation(out=hT[:, :, t0:t0 + n], in_=tmp0, func=AF.Square),
                     reads=[("tmp", 0, 0), ("tmp", 0, 1), ("tmp", 0, 2)], writes=hkeys(t0, n))
            else:
                S.op("act", lambda e: e.activation(out=hT[:, 0:4, t0:t0 + n], in_=tmp1a, func=AF.Square),
                     reads=[("tmp", 1, 0)], writes=hkeys(t0, n, range(4)))
                S.op("act", lambda e: e.activation(out=hT[:, 4:6, t0:t0 + n], in_=tmp1b, func=AF.Square),
                     reads=[("tmp", 1, 1)], writes=hkeys(t0, n, range(4, 6)))
                S.op("act", lambda e: e.activation(out=hT[:, 6:8, t0:t0 + n], in_=tmp1c, func=AF.Square),
                     reads=[("tmp", 1, 2)], writes=hkeys(t0, n, range(6, 8)))

        def pre_norm_B(ti):
            t0, n = TILES_ALL[1][ti]
            b = norm_B0((t0, n))

            def hmul(kc):
                S.op("dve", lambda e: e.scalar_tensor_tensor(out=hT[:, kc, t0:t0 + n], in0=tmp_kc(ti, kc),
                                                             scalar=gains[:, G_FFN1 + kc:G_FFN1 + kc + 1], in1=ps[b][:, 0:n],
                                                             op0=ALU.mult, op1=ALU.mult),
                     reads=[("tmp", ti, (0 if kc < 4 else (1 if kc < 6 else 2))), ("ps", b), "gains"], writes=hkeys(t0, n, [kc]))
            for kc in range(8):
                pend_dve.append(((t0, n), lambda kc=kc: hmul(kc)))

        LEAD = 4

        def ffn(tag, tiles, pendA, pendB, sqrow, doneB, done_last_inside, hook=None):
            def phase1(fl, sg, su, t0, n, mid=None, fine=False):
                flush_for(t0, n)
                hk = hkeys(t0, n)
                bg = bankS()
                bu = bankS()

                def mmw(e, s, b):
                    for kc in range(8):
                        inst = e.matmul(ps[b][:, 0:n], lhsT=ring[:, s, kc * 128:(kc + 1) * 128],
                                        rhs=hT[:, kc, t0:t0 + n], start=(kc == 0), stop=(kc == 7))
                    return inst
                if fine:
                    for kc in range(8):
                        S.op("pe", lambda e, kc=kc: e.matmul(ps[bg][:, 0:n], lhsT=ring[:, sg, kc * 128:(kc + 1) * 128],
                                                              rhs=hT[:, kc, t0:t0 + n], start=(kc == 0), stop=(kc == 7)),
                             reads=hkeys(t0, n, [kc]) + [("ring", sg)], writes=[("ps", bg)])
                else:
                    S.op("pe", lambda e: mmw(e, sg, bg), reads=hk + [("ring", sg)], writes=[("ps", bg)])
                if mid is not None:
                    mid()
                S.op("pe", lambda e: mmw(e, su, bu), reads=hk + [("ring", su)], writes=[("ps", bu)])
                sb_i = state["sS"]
                state["sS"] ^= 1
                S.op("act", lambda e: e.activation(out=sS[:, sb_i, 0:n], in_=ps[bg][:, 0:n], func=AF.Silu),
                     reads=[("ps", bg)], writes=[("sS", sb_i)])
                S.op("dve", lambda e: e.tensor_tensor(out=aT[:, fl, t0:t0 + n], in0=ps[bu][:, 0:n], in1=sS[:, sb_i, 0:n], op=ALU.mult),
                     reads=[("ps", bu), ("sS", sb_i)], writes=blk_keys("a", t0, n, fl))
                drain_dve(4)

            nt = len(tiles)
            for gi, grp in enumerate(GROUPS):
                fls = list(enumerate(grp))
                if hook is not None and gi == len(GROUPS) - 1:
                    hook("last_group_start", 0, 0)
                if gi == 0:
                    lead = fls[:LEAD]
                    slots = [(consume((tag + "g", f)), consume((tag + "u", f))) for (_, f) in lead]
                    for ti, (t0, n) in enumerate(tiles):
                        early = ti + 1 < nt and pendA[ti] is None
                        if ti + 1 < nt and pendA[ti] is not None:
                            pendA[ti]()
                        for li, ((fl, f), (sg, su)) in enumerate(zip(lead, slots)):
                            if li == 0 and early:
                                phase1(fl, sg, su, t0, n, lambda ti=ti: (pendB[ti](), drain_dve(4)), fine=True)
                            else:
                                phase1(fl, sg, su, t0, n, fine=(li == 0))
                            if li == 0 and ti + 1 < nt and not early:
                                pendB[ti]()
                                drain_dve(4)
                    release(2 * len(lead))
                    fls = fls[LEAD:]
                for fl, f in fls:
                    sg = consume((tag + "g", f))
                    su = consume((tag + "u", f))
                    for (t0, n) in tiles:
                        phase1(fl, sg, su, t0, n)
                    release(2)
                dslots = [consume((tag + "d", f)) for f in grp]
                last = gi == len(GROUPS) - 1
                if last:
                    S.op("act", lambda e: e.activation(out=rstq[:, 0:1], in_=epst[:], func=AF.Ln), reads=["eps"], writes=[("rstq", 0)])
                if last and hook is not None:
                    hook("last_p2_start", 0, 0)
                for ti, (t0, n) in enumerate(tiles):
                    for o in range(8):
                        b = (bankS() if hook is not None else bank6()) if last else bankM(True)

                        def mmd(e, b=b, o=o, t0=t0, n=n, dslots=dslots):
                            for fl, s in enumerate(dslots):
                                inst = e.matmul(ps[b][:, 0:n], lhsT=ring[:, s, o * 128:(o + 1) * 128],
                                                rhs=aT[:, fl, t0:t0 + n], start=(fl == 0), stop=(fl == len(dslots) - 1))
                            return inst
                        rk = [("ring", s) for s in dslots]
                        for fl in range(len(dslots)):
                            rk += blk_keys("a", t0, n, fl)
                        S.op("pe", mmd, reads=rk, writes=[("ps", b)])
                        S.op("dve", lambda e, b=b, o=o, t0=t0, n=n: e.scalar_tensor_tensor(
                            out=xT[:, o, t0:t0 + n], in0=ps[b][:, 0:n], scalar=0.5, in1=xT[:, o, t0:t0 + n],
                            op0=ALU.mult, op1=ALU.add),
                            reads=[("ps", b)] + blk_keys("x", t0, n), writes=blk_keys("x", t0, n) + blk_keys("xr", t0, n, o))
                        drain_dve(2 if (last and hook is None and ti == nt - 1 and 1 <= o <= 4) else 1)
                        if last and hook is not None:
                            hook("last_p2_group", ti, o)
                        if last and ti >= 1 and o == 0:
                            doneB(ti - 1)
                        if last:
                            sqrow(ti, o)
                if last and done_last_inside:
                    doneB(nt - 1)
                    drain_dve(1000)
                release(len(dslots))

        def proj(slot, t0, n, bank):
            flush_for(t0, n)

            def mm(e):
                for kc in range(8):
                    inst = e.matmul(ps[bank][:, 0:n], lhsT=ring[:, slot, kc * 128:(kc + 1) * 128],
                                    rhs=hT[:, kc, t0:t0 + n], start=(kc == 0), stop=(kc == 7))
                return inst
            S.op("pe", mm, reads=hkeys(t0, n) + [("ring", slot)], writes=[("ps", bank)])

        def mixer(st, tiles_all, tiles_own, pendB_last):
            g0 = ST_G0[st]
            nblk = ST_LEN[st] // 128
            gb0 = g0 // 128
            def own_part(tile):
                a, m = tile
                if st == 0 and a < HALO:
                    return (HALO, a + m - HALO)
                return tile

            while pend_low and not pend_dve:
                pend_low.pop(0)()
            s_k = consume(("k", 0))
            s_v = consume(("v", 0))
            s_q = [consume(("q", c)) for c in range(4)]
            for ti, (t0, n) in enumerate(tiles_all):
                b = bankS()
                proj(s_k, t0, n, b)
                if ti == 0 and len(tiles_all) == 2 and pendB_last is not None:
                    pendB_last()
                    pendB_last = None
                S.op("act", lambda e, b=b, t0=t0, n=n: e.activation(out=kT[:, g0 + t0:g0 + t0 + n], in_=ps[b][:, 0:n], func=AF.Copy),
                     reads=[("ps", b)], writes=blk_keys("k", g0 + t0, n))
                lb0 = t0 // 128
                nb = n // 128
                b = bankS()

                def mmv(e, b=b, lb0=lb0, nb=nb):
                    for j in range(nb):
                        c0 = (lb0 + j) * 128
                        for kc in range(8):
                            inst = e.matmul(ps[b][:, j * 128:(j + 1) * 128], lhsT=hT[:, kc, c0:c0 + 128],
                                            rhs=ring[:, s_v, kc * 128:(kc + 1) * 128], start=(kc == 0), stop=(kc == 7))
                    return inst
                S.op("pe", mmv, reads=hkeys(t0, n) + [("ring", s_v)], writes=[("ps", b)])
                S.op("dve", lambda e, b=b, lb0=lb0, nb=nb: e.tensor_copy(
                    out=vaug[:, gb0 + lb0:gb0 + lb0 + nb, :, 0:64],
                    in_=ps[b][:, 0:nb * 128].rearrange("p (b k d) -> p b k d", b=nb, k=2, d=64)),
                    reads=[("ps", b), "vaug_init", "vaug_halo"], writes=[("v", gb0 + lb0 + j) for j in range(nb)])
                tq, nq = own_part((t0, n))
                for c in range(4):
                    b = bankS()
                    proj(s_q[c], tq, nq, b)
                    S.op("act", lambda e, b=b, tq=tq, nq=nq, c=c: e.activation(out=qT[:, c, tq:tq + nq], in_=ps[b][:, 0:nq], func=AF.Copy),
                         reads=[("ps", b)], writes=blk_keys("q", tq, nq, c))
                    drain_dve(2)
                if ti == 0 and pendB_last is not None:
                    pendB_last()
            release(6)

            cwide = [False]

            def cbank(pool):
                if cwide[0]:
                    return bank6()
                return bankM() if pool == "M" else bankS()

            def conv_uc(i, t0, n, slots_i):
                s_u, s_c, s_b = slots_i
                ub = state["uS"]
                state["uS"] ^= 1
                b1 = cbank("M")
                proj(s_u, t0, n, b1)
                S.op("act", lambda e: e.activation(out=uS[:, ub, 0:n], in_=ps[b1][:, 0:n], func=AF.Copy),
                     reads=[("ps", b1)], writes=[("uS", ub)])
                b2 = cbank("S")
                proj(s_c, t0, n, b2)
                S.op("dve", lambda e: e.tensor_tensor(out=cuT[:, i, 2 + g0 + t0:2 + g0 + t0 + n], in0=ps[b2][:, 0:n],
                                                      in1=uS[:, ub, 0:n], op=ALU.mult),
                     reads=[("ps", b2), ("uS", ub), "cu_pad"], writes=blk_keys("cu", g0 + t0, n, i))

            def conv_by(i, t0, n, slots_i):
                s_u, s_c, s_b = slots_i
                bb = state["bS_"]
                state["bS_"] ^= 1
                b1 = cbank("M")
                proj(s_b, t0, n, b1)
                S.op("act", lambda e: e.activation(out=bS[:, bb, 0:n], in_=ps[b1][:, 0:n], func=AF.Copy),
                     reads=[("ps", b1)], writes=[("bS", bb)])
                by = cbank("S")

                def mmy(e):
                    for r in range(3):
                        c0 = g0 + t0 + r
                        inst = e.matmul(ps[by][:, 0:n], lhsT=diag[:, i * 3 + r, :], rhs=cuT[:, i, c0:c0 + n],
                                        start=(r == 0), stop=(r == 2))
                    return inst
                rk = ["diag", "cu_pad"] + blk_keys("cu", max(g0 + t0 - 2, 0), n + 2, i)
                S.op("pe", mmy, reads=rk, writes=[("ps", by)])
                S.op("dve", lambda e: e.tensor_tensor(out=convT[:, i, t0:t0 + n], in0=ps[by][:, 0:n], in1=bS[:, bb, 0:n], op=ALU.mult),
                     reads=[("ps", by), ("bS", bb)], writes=blk_keys("conv", t0, n, i))

            cn_bank = {}

            def conv_norm_A(t0, n):
                ck = []
                for i in range(4):
                    ck += blk_keys("conv", t0, n, i)
                flush_for(t0, n)
                hk = hkeys(t0, n, range(4))
                S.op("act", lambda e: e.activation(out=hT[:, 0:4, t0:t0 + n], in_=convT[:, :, t0:t0 + n], func=AF.Square),
                     reads=ck, writes=hk)

            def conv_norm_B(t0, n):
                ck = []
                for i in range(4):
                    ck += blk_keys("conv", t0, n, i)
                hk = hkeys(t0, n, range(4))
                b = bankN()

                def mmn(e):
                    for i in range(4):
                        inst = e.matmul(ps[b][:, 0:n], lhsT=ones[:], rhs=hT[:, i, t0:t0 + n], start=(i == 0), stop=(i == 3))
                    return inst
                S.op("pe", mmn, reads=hk + ["ones"], writes=[("ps", b)])
                S.op("act", lambda e: e.activation(out=ps[b][:, 0:n], in_=ps[b][:, 0:n], func=AF.Ln, scale=1.0 / 512, bias=epst[:]),
                     reads=[("ps", b), "eps"], writes=[("ps", b)])
                S.op("act", lambda e: e.activation(out=ps[b][:, 0:n], in_=ps[b][:, 0:n], func=AF.Exp, scale=-0.5),
                     reads=[("ps", b)], writes=[("ps", b)])

                def cmul(e):
                    for i in range(4):
                        col = G_CONVN + i
                        inst = e.scalar_tensor_tensor(out=aT[:, 4 + i, t0:t0 + n], in0=convT[:, i, t0:t0 + n],
                                                      scalar=gains[:, col:col + 1], in1=ps[b][:, 0:n],
                                                      op0=ALU.mult, op1=ALU.mult)
                    return inst
                mk = []
                for i in range(4):
                    mk += blk_keys("mix", t0, n, 4 + i)
                S.op("dve", cmul, reads=ck + [("ps", b), "gains"], writes=mk)

            own_lb0 = tiles_own[0][0] // 128
            blocks = list(range(own_lb0, nblk))
            pbs = {}

            def att_A(lbq):
                gb = gb0 + lbq
                pb = state["att"] % 2
                state["att"] += 1
                pbs[lbq] = pb
                q0 = lbq * 128
                for jc in range(2):
                    ba = state["bS"] & 2
                    state["bS"] = (ba + 2) % 4
                    kcol = (gb - 1 + jc) * 128

                    def mms(e, jc=jc, ba=ba, kcol=kcol):
                        for kv in range(2):
                            b = ba + kv
                            e.matmul(ps[b].rearrange("p (c q) -> p c q", c=4), lhsT=kT[kv * 64:(kv + 1) * 64, kcol:kcol + 128],
                                     rhs=qT[kv * 64:(kv + 1) * 64, 0:4, q0:q0 + 128], start=True, stop=False)
                        for kv in range(2):
                            b = ba + kv
                            e.matmul(ps[b], lhsT=identb[:], rhs=BH[:, jc, kv * 512:(kv + 1) * 512], start=False, stop=False)
                            inst = e.matmul(ps[b], lhsT=identb[:], rhs=BL[:, jc, kv * 512:(kv + 1) * 512], start=False, stop=True)
                        return inst
                    S.op("pe", mms, reads=blk_keys("k", kcol, 128) + [("q", c, lbq) for c in range(4)] + ["BH", "BL", "identb"],
                         writes=[("ps", ba), ("ps", ba + 1)])
                    S.op("act", lambda e, jc=jc, ba=ba: e.activation(out=pT[:, pb, 2 * jc:2 * jc + 2, :], in_=psall[:, ba:ba + 2, :],
                                                                     func=AF.Exp, scale=0.125),
                         reads=[("ps", ba), ("ps", ba + 1)], writes=[("pTs", pb, jc)])

            def att_B(lbq):
                gb = gb0 + lbq
                pb = pbs[lbq]
                for kv in range(2):
                    def mmpv(e, kv=kv):
                        for c in range(4):
                            for jc in range(2):
                                inst = e.matmul(ps[6 + kv][:, c * 65:(c + 1) * 65],
                                                lhsT=pT[:, pb, jc * 2 + kv, c * 128:(c + 1) * 128],
                                                rhs=vaug[:, gb - 1 + jc, kv, 0:65], start=(jc == 0), stop=(jc == 1))
                        return inst
                    S.op("pe", mmpv, reads=[("pTs", pb, 0), ("pTs", pb, 1), ("v", gb - 1), ("v", gb), "vaug_halo"],
                         writes=[("ps", 6 + kv)])

                def dens(e):
                    for kv in range(2):
                        den = ps[6 + kv][:, 0:260].rearrange("p (c e) -> p c e", e=65)[:, :, 64:65]
                        inst = e.tensor_tensor(out=rden[:, pb, kv * 4:(kv + 1) * 4, :], in0=den, in1=esink[:, kv * 4:(kv + 1) * 4, :], op=ALU.add)
                    return inst
                S.op("dve", dens, reads=[("ps", 6), ("ps", 7), "esink"], writes=[("rden", pb)])
                S.op("dve", lambda e: e.reciprocal(out=rden[:, pb], in_=rden[:, pb]), reads=[("rden", pb)], writes=[("rden", pb)])

                def normz(e):
                    for kv in range(2):
                        pvv = ps[6 + kv][:, 0:260].rearrange("p (c e) -> p c e", e=65)[:, :, 0:64]
                        inst = e.tensor_tensor(
                            out=atmp[:, pb, kv * 256:(kv + 1) * 256].rearrange("p (c d) -> p c d", d=64),
                            in0=pvv, in1=rden[:, pb, kv * 4:(kv + 1) * 4, :].broadcast_to([128, 4, 64]), op=ALU.mult)
                    return inst
                S.op("dve", normz, reads=[("ps", 6), ("ps", 7), ("rden", pb)], writes=[("atmp", pb)])
                S.op("act", lambda e: e.activation(out=anf[:, pb, :], in_=atmp[:, pb, :], func=AF.Square, accum_out=ssq[:, pb:pb + 1]),
                     reads=[("atmp", pb)], writes=[("anf", pb), ("ssq", pb)])
                S.op("act", lambda e: e.activation(out=rstq[:, pb:pb + 1], in_=ssq[:, pb:pb + 1], func=AF.Ln, scale=1.0 / 512, bias=epst[:]),
                     reads=[("ssq", pb), "eps"], writes=[("rstq", pb)])
                S.op("act", lambda e: e.activation(out=rstq[:, pb:pb + 1], in_=rstq[:, pb:pb + 1], func=AF.Exp, scale=-0.5),
                     reads=[("rstq", pb)], writes=[("rstq", pb)])
                S.op("dve", lambda e: e.scalar_tensor_tensor(
                    out=anb[:, pb, :], in0=atmp[:, pb, :], scalar=rstq[:, pb:pb + 1], in1=gattn[:], op0=ALU.mult, op1=ALU.mult),
                    reads=[("atmp", pb), ("rstq", pb), "gattn"], writes=[("anb", pb)])

            def att_C(lbq):
                pb = pbs[lbq]
                q0 = lbq * 128
                b = bankM()

                def mmt(e):
                    for c in range(4):
                        inst = e.matmul(ps[b][:, c * 128:(c + 1) * 128], lhsT=anb[:, pb, c * 128:(c + 1) * 128], rhs=identb[:],
                                        start=True, stop=True)
                    return inst
                S.op("pe", mmt, reads=[("anb", pb), "identb"], writes=[("ps", b)])
                S.op("dve", lambda e: e.tensor_copy(
                    out=aT[:, 0:4, q0:q0 + 128], in_=ps[b].rearrange("p (c q) -> p c q", c=4)),
                    reads=[("ps", b)], writes=[("mix", c, lbq) for c in range(4)])

            conv_units = []
            cslots = {}

            def cunit_uc(i, t0, n, first):
                if first:
                    cslots[i] = (consume(("cu", i)), consume(("cc", i)), consume(("cb", i)))
                conv_uc(i, t0, n, cslots[i])

            def cunit_by(i, t0, n, last):
                conv_by(i, t0, n, cslots[i])
                if last:
                    release(3)
            for i in range(4):
                for ti, (t0, n) in enumerate(tiles_all):
                    if st == 0 and t0 < HALO:
                        t0, n = HALO - 2, t0 + n - (HALO - 2)
                    conv_units.append(lambda i=i, t0=t0, n=n, f=(ti == 0): cunit_uc(i, t0, n, f))
                for ti, tile in enumerate(tiles_all):
                    t0, n = own_part(tile)
                    conv_units.append(lambda i=i, t0=t0, n=n, l=(ti == len(tiles_all) - 1): cunit_by(i, t0, n, l))

            oslots = []

            wbank = [0]
            wstate = {}
            wsq = []

            def wout_half(ti, o, half):
                if not oslots:
                    oslots.extend(consume(("o", oo)) for oo in range(8))
                t0, n = tiles_own[ti]
                if half == 0:
                    b = wbank[0]
                    wbank[0] = (b + 1) % 6
                    wstate[(ti, o)] = b
                b = wstate[(ti, o)]
                kcs = range(4) if half == 0 else range(4, 8)
                mk = []
                for c in kcs:
                    mk += blk_keys("mix", t0, n, c)

                def mmo(e):
                    for kc in kcs:
                        inst = e.matmul(ps[b][:, 0:n], lhsT=ring[:, oslots[o], kc * 128:(kc + 1) * 128],
                                        rhs=aT[:, kc, t0:t0 + n], start=(kc == 0), stop=(kc == 7))
                    return inst
                S.op("pe", mmo, reads=mk + [("ring", oslots[o])], writes=[("ps", b)])
                if half == 1:
                    S.op("dve", lambda e: e.tensor_tensor(out=xT[:, o, t0:t0 + n], in0=ps[b][:, 0:n], in1=xT[:, o, t0:t0 + n], op=ALU.add),
                         reads=[("ps", b)] + blk_keys("x", t0, n), writes=blk_keys("x", t0, n) + blk_keys("xr", t0, n, o))
                    drain_dve(1)
                    wsq.append((ti, o))

            def wout_group(ti, o):
                wout_half(ti, o, 0)
                wout_half(ti, o, 1)

            nb_ = len(blocks)
            nsteps = nb_ + 2
            ncu = len(conv_units)
            for step in range(nsteps):
                if step < nb_:
                    att_A(blocks[step])
                else:
                    cwide[0] = True
                for _ in range(((step + 1) * ncu) // nsteps - (step * ncu) // nsteps):
                    if conv_units:
                        conv_units.pop(0)()
                if 1 <= step < nb_ + 1:
                    att_B(blocks[step - 1])
                if 2 <= step:
                    att_C(blocks[step - 2])
            while conv_units:
                conv_units.pop(0)()
            conv_norm_A(*tiles_own[0])
            for (t0, n) in tiles_own[1:]:
                conv_norm_A(t0, n)
            for o in range(0, 3):
                wout_half(0, o, 0)
            conv_norm_B(*tiles_own[0])
            for o in range(3, 6):
                wout_half(0, o, 0)
            for (t0, n) in tiles_own[1:]:
                conv_norm_B(t0, n)
            def wsq_flush():
                while wsq:
                    ti_, o_ = wsq.pop(0)
                    norm_sq_row(tiles_own[ti_], o_)
            for o in range(0, 6):
                wout_half(0, o, 1)
            for o in range(6, 8):
                wout_group(0, o)
            wsq_flush()
            for ti in range(1, len(tiles_own)):
                for o in range(8):
                    wout_group(ti, o)
                    if o == 0:
                        norm_B(tiles_own[ti - 1], G_FFN2)
                    wsq_flush()
                    if o >= 1:
                        drain_dve(1)
            release(8)
            del mixer_end[:]
            mixer_end.extend((S.engs[nm]["sem"], S.engs[nm]["count"], nm) for nm in ("pe", "act", "dve") if S.engs[nm]["count"] > 0)

        load_x_first(TILES_ALL[0][0])
        setup()
        S.op("act", lambda e: e.activation(out=rstq[:, 0:1], in_=epst[:], func=AF.Ln), reads=["eps"], writes=[("rstq", 0)])
        pump()
        for i, tile in enumerate(TILES_ALL[0]):
            if i >= 1:
                load_x_tile(0, i, tile)
        for st in range(2):
            tiles_all = TILES_ALL[st]
            tiles_own = TILES_OWN[st]
            if st == 0:
                norm_h_tile(tiles_all[0], G_FFN1)
                setup2()
                pendA = [(lambda tile=tile: norm_A(tile)) for tile in tiles_all[1:]]
                pendB = [(lambda tile=tile: norm_B(tile, G_FFN1)) for tile in tiles_all[1:]]
            else:
                while pend_dve:
                    pend_dve.pop(0)[1]()
                pendA = [None]
                pendB = [deferred_final[0]]
            ffn("1", tiles_all, pendA, pendB,
                lambda ti, o: norm_sq_row(tiles_all[ti], o), lambda ti: norm_B(tiles_all[ti], G_MIX), False)
            mixer(st, tiles_all, tiles_own, lambda: norm_B(tiles_all[-1], G_MIX))

            def finB(ti, st=st, tiles_own=tiles_own):
                if st == 0:
                    final_norm_B(st, tiles_own[ti], (lambda: load_x_tile(1, ti, TILES_ALL[1][ti])), 0, 4 + ti)
                else:
                    final_norm_B(st, tiles_own[ti], None)

            def hook(ev, ti, o):
                if ev == "last_group_start":
                    pre_load(0)
                    pre_load(1)
                elif ev == "last_p2_start":
                    pre_norm_A(0)
                    pre_norm_A(1)
                elif ev == "last_p2_group" and ti == 0 and o == 3:
                    pre_norm_B(0)
                elif ev == "last_p2_group" and ti == 0 and o == 7:
                    pre_norm_B(1)
            if st == 0:
                ffn("2", tiles_own, [None], [lambda: norm_B(tiles_own[-1], G_FFN2)],
                    lambda ti, o, tiles_own=tiles_own: final_sq_row(tiles_own[ti], o), finB, False, hook)
                deferred_final = [lambda finB=finB, k=len(tiles_own) - 1: finB(k)]
            else:
                t0l, nl = tiles_own[-1]
                halves = [(t0l, nl // 2), (t0l + nl // 2, nl // 2)]

                def doneB2(ti, tiles_own=tiles_own):
                    if ti < len(tiles_own) - 1:
                        final_norm_B(1, tiles_own[ti])
                    else:
                        drain_dve(1000)
                        final_norm_B(1, halves[0], None, 0)
                        drain_dve(1000)
                        final_norm_B(1, halves[1], None, nl // 2)
                ffn("2", tiles_own, [None], [lambda: norm_B(tiles_own[-1], G_FFN2)],
                    lambda ti, o, tiles_own=tiles_own: final_sq_row(tiles_own[ti], o), doneB2, True, None)
                drain_dve(1000)
        assert state["consumed"] == total_chunks and state["issued"] == total_chunks

        with nc.Block() as block:
            S.emit(block, list(last_out.values()))
    return nc


def _chunk_cols(W, cols):
    sub = np.ascontiguousarray(W[:, cols])
    return sub.reshape(8, 128, 128).transpose(1, 0, 2).reshape(128, 1024)


def _t5_bucket(n):
    n = np.maximum(n, 0)
    max_exact = 16
    large = max_exact + (np.log(np.maximum(n, 1).astype(np.float32) / np.float32(max_exact))
                         / np.float32(math.log(128 / max_exact)) * np.float32(32 - max_exact)).astype(np.int32)
    large = np.minimum(large, 31)
    return np.where(n < max_exact, n, large)


def kernel(x, rel_bias_table, ffn1_norm, ffn1_w_gate, ffn1_w_up, ffn1_w_down, mix_norm, w_in, conv_w, attn_sinks,
           attn_out_norm, conv_out_norm, w_out, ffn2_norm, ffn2_w_gate, ffn2_w_up, ffn2_w_down, final_norm):
    f32 = np.float32
    x = np.asarray(x, f32)
    order = stream_order()
    ws = np.empty((NCHUNK, 128, 1024), f32)
    W = {"1g": np.asarray(ffn1_w_gate, f32)[0], "1u": np.asarray(ffn1_w_up, f32)[0], "1d": np.asarray(ffn1_w_down, f32)[0],
         "2g": np.asarray(ffn2_w_gate, f32)[0], "2u": np.asarray(ffn2_w_up, f32)[0], "2d": np.asarray(ffn2_w_down, f32)[0]}
    win = np.asarray(w_in, f32)[0]
    wo = np.asarray(w_out, f32)[0]
    ar = np.arange
    for n, (kind, idx) in enumerate(order):
        if kind in ("1g", "1u", "2g", "2u"):
            ws[n] = _chunk_cols(W[kind], ar(idx * 128, (idx + 1) * 128))
        elif kind in ("1d", "2d"):
            ws[n] = W[kind][idx * 128:(idx + 1) * 128, :]
        elif kind == "q":
            cols = np.concatenate([ar(idx * 64, (idx + 1) * 64), ar((4 + idx) * 64, (5 + idx) * 64)])
            ws[n] = _chunk_cols(win, cols)
        elif kind == "k":
            ws[n] = _chunk_cols(win, ar(512, 640))
        elif kind == "v":
            ws[n] = _chunk_cols(win, ar(640, 768))
        elif kind == "cu":
            ws[n] = _chunk_cols(win, ar(768 + idx * 128, 768 + (idx + 1) * 128))
        elif kind == "cb":
            ws[n] = _chunk_cols(win, ar(1280 + idx * 128, 1280 + (idx + 1) * 128))
        elif kind == "cc":
            ws[n] = _chunk_cols(win, ar(1792 + idx * 128, 1792 + (idx + 1) * 128))
        elif kind == "o":
            ws[n] = _chunk_cols(wo, ar(idx * 128, (idx + 1) * 128))
        else:
            raise AssertionError(kind)
    gains = np.zeros((128, NGCOL), f32)

    def cols(v, n):
        return np.asarray(v, f32).reshape(n, 128).T
    gains[:, G_FFN1:G_FFN1 + 8] = cols(np.asarray(ffn1_norm)[0], 8)
    gains[:, G_MIX:G_MIX + 8] = cols(np.asarray(mix_norm)[0], 8)
    gains[:, G_FFN2:G_FFN2 + 8] = cols(np.asarray(ffn2_norm)[0], 8)
    gains[:, G_FINAL:G_FINAL + 8] = cols(np.asarray(final_norm), 8)
    gains[:, G_CONVN:G_CONVN + 4] = cols(np.asarray(conv_out_norm)[0], 4)
    cw = np.asarray(conv_w, f32)[0]
    for r in range(3):
        gains[:, G_CONVW + r * 4:G_CONVW + r * 4 + 4] = cols(cw[r], 4)
    gattn = np.ascontiguousarray(np.broadcast_to(np.asarray(attn_out_norm, f32)[0][None, :], (128, 512)))
    sinks = np.ascontiguousarray(np.broadcast_to(np.asarray(attn_sinks, f32)[0][None, :], (128, 8)))
    j = np.arange(128)[:, None]
    q = np.arange(128)[None, :]
    tbl = np.asarray(rel_bias_table, f32)
    biasT = np.empty((128, 2, 8, 128), f32)
    maskT = np.empty((128, 2, 8, 128), f32)
    for jc in range(2):
        dist = q + 128 - (jc * 128 + j)
        valid = (dist >= 0) & (dist < 128)
        bk = _t5_bucket(dist)
        g = tbl[bk]
        biasT[:, jc] = g.transpose(0, 2, 1)
        maskT[:, jc] = np.where(valid[:, None, :], f32(0.0), f32(-240000.0))
    biasT = biasT.reshape(128, 2, 1024)
    maskT = maskT.reshape(128, 2, 1024)
    ident = np.eye(128, dtype=f32)
    in_maps = []
    for c in range(NCORE):
        b, s = divmod(c, 4)
        own = x[b, s * SEG:(s + 1) * SEG]
        if s == 0:
            hal = np.zeros((HALO, D), f32)
        else:
            hal = x[b, s * SEG - HALO:s * SEG]
        xc = np.concatenate([hal, own], axis=0)
        xTc = np.ascontiguousarray(xc.T.reshape(8, 128, NTOK).transpose(1, 0, 2))
        in_maps.append({
            "xT": xTc, "ws": ws, "gains": gains, "gattn": gattn, "sinks": sinks,
            "halo_ok": np.full((128, 1), 0.0 if s == 0 else 1.0, f32),
            "maskT": maskT, "biasT": biasT, "ident": ident,
        })
    nc = build_program()
    res = run_bass_kernel_spmd(nc, in_maps, core_ids=list(range(NCORE)))
    out = np.empty((2, SEQ, D), f32)
    for c in range(NCORE):
        b, s = divmod(c, 4)
        oT = res.results[c]["outT"]
        out[b, s * SEG:(s + 1) * SEG] = oT.transpose(1, 0, 2).reshape(D, SEG).T
    return out
```
